# Optimizing a Trainium2 kernel written in Bass

```python
import math
import jax, jax.numpy as jnp
from jax import lax
import numpy as np

D_MODEL = 2048
BATCH = 2
SEQ = 8192
DEPTH = 4

N_A_LAYERS = DEPTH // 2
N_B_LAYERS = DEPTH - N_A_LAYERS
HEAD_DIM = 128
A_HEADS = D_MODEL // (2 * HEAD_DIM)
A_SUBHEADS = 2 * A_HEADS
A_V_DIM = 2 * HEAD_DIM
B_HEADS = D_MODEL // HEAD_DIM
D_FF = 5504
NUM_BUCKETS = 32
MAX_DISTANCE = 128
Q_BLOCK = 128
EPS = 1e-6
BIAS_INIT = 0.5

kernel_name = "yoco_diffattn_stickbreaking_macaron"


def _rms_norm(x, g):
    xf = x.astype(jnp.float32)
    y = xf * lax.rsqrt(jnp.mean(xf * xf, axis=-1, keepdims=True) + EPS)
    return (y * g.astype(jnp.float32)).astype(x.dtype)


def _swiglu(x, wi, wo):
    gate, up = jnp.split(x @ wi, 2, axis=-1)
    return (jax.nn.silu(gate) * up) @ wo


def _t5_bucket(qpos, kpos):
    n = jnp.maximum(qpos[:, None] - kpos[None, :], 0)
    max_exact = NUM_BUCKETS // 2
    nf = jnp.maximum(n, max_exact).astype(jnp.float32)
    large = max_exact + (jnp.log(nf / max_exact) / math.log(MAX_DISTANCE / max_exact)
                         * (NUM_BUCKETS - max_exact)).astype(jnp.int32)
    large = jnp.minimum(large, NUM_BUCKETS - 1)
    return jnp.where(n < max_exact, n, large)


def _to_blocks(t):
    return jnp.swapaxes(t.reshape(t.shape[0], t.shape[1] // Q_BLOCK, Q_BLOCK, *t.shape[2:]), 0, 1)


def _from_blocks(t):
    t = jnp.swapaxes(t, 0, 1)
    return t.reshape(t.shape[0], t.shape[1] * t.shape[2], *t.shape[3:])


def _diff_attention(h, wqkv, q_g, k_g, lam_vecs, subln_g, wo, rel_bias, lambda_init):
    B, S, _ = h.shape
    q, k, v = jnp.split(h @ wqkv, 3, axis=-1)
    q = _rms_norm(q.reshape(B, S, A_SUBHEADS, HEAD_DIM), q_g)
    k = _rms_norm(k.reshape(B, S, A_SUBHEADS, HEAD_DIM), k_g)
    v = v.reshape(B, S, A_HEADS, A_V_DIM)
    lf = lam_vecs.astype(jnp.float32)
    lam = jnp.exp(jnp.sum(lf[0] * lf[1])) - jnp.exp(jnp.sum(lf[2] * lf[3])) + lambda_init
    kpos = jnp.arange(S)
    scale = HEAD_DIM ** -0.5

    def block(args):
        qb, start = args
        qpos = start + jnp.arange(Q_BLOCK)
        logits = jnp.einsum('bqhd,bkhd->bhqk', qb, k).astype(jnp.float32) * scale
        bias = jnp.moveaxis(rel_bias[_t5_bucket(qpos, kpos)], -1, 0).astype(jnp.float32)
        causal = kpos[None, :] <= qpos[:, None]
        logits = jnp.where(causal, logits + bias[None], -jnp.inf)
        p = jax.nn.softmax(logits, axis=-1).reshape(B, A_HEADS, 2, Q_BLOCK, S)
        attn = p[:, :, 0] - lam * p[:, :, 1]
        return jnp.einsum('bhqk,bkhe->bqhe', attn.astype(v.dtype), v)

    starts = jnp.arange(S // Q_BLOCK) * Q_BLOCK
    o = _from_blocks(lax.map(block, (_to_blocks(q), starts)))
    o = _rms_norm(o, subln_g) * (1.0 - lambda_init)
    return o.reshape(B, S, D_MODEL) @ wo


def _stick_breaking(h, wq, k, v, wo):
    B, S, _ = h.shape
    q = (h @ wq).reshape(B, S, B_HEADS, HEAD_DIM)
    kpos = jnp.arange(S)
    scale = HEAD_DIM ** -0.5

    def block(args):
        qb, start = args
        qpos = start + jnp.arange(Q_BLOCK)
        z = jnp.einsum('bqhd,bkhd->bhqk', qb, k).astype(jnp.float32) * scale
        strict = kpos[None, :] < qpos[:, None]
        log_keep = jnp.where(strict, jax.nn.log_sigmoid(-z), 0.0)
        key_axis = log_keep.ndim - 1
        log_w = jax.nn.log_sigmoid(z) + lax.cumsum(log_keep, axis=key_axis, reverse=True) - log_keep
        w = jnp.where(strict, jnp.exp(log_w), 0.0)
        return jnp.einsum('bhqk,bkhd->bqhd', w.astype(v.dtype), v)

    starts = jnp.arange(S // Q_BLOCK) * Q_BLOCK
    o = _from_blocks(lax.map(block, (_to_blocks(q), starts)))
    return o.reshape(B, S, D_MODEL) @ wo


def setup_inputs(seed: int = 0) -> dict:
    key = jax.random.key(seed)
    ks = jax.random.split(key, 20)
    f32 = jnp.float32

    def w(k, shape, fan_in):
        return jax.random.normal(k, shape, f32) * fan_in ** -0.5

    def gain(k, shape):
        return 1.0 + 0.02 * jax.random.normal(k, shape, f32)

    return {
        "x": jax.random.normal(ks[0], (BATCH, SEQ, D_MODEL), f32),
        "ffn_pre_norm": gain(ks[1], (DEPTH, D_MODEL)),
        "ffn_pre_wi": w(ks[2], (DEPTH, D_MODEL, 2 * D_FF), D_MODEL),
        "ffn_pre_wo": w(ks[3], (DEPTH, D_FF, D_MODEL), D_FF),
        "mix_norm": gain(ks[4], (DEPTH, D_MODEL)),
        "ffn_post_norm": gain(ks[5], (DEPTH, D_MODEL)),
        "ffn_post_wi": w(ks[6], (DEPTH, D_MODEL, 2 * D_FF), D_MODEL),
        "ffn_post_wo": w(ks[7], (DEPTH, D_FF, D_MODEL), D_FF),
        "rel_bias": BIAS_INIT * jax.random.normal(ks[8], (NUM_BUCKETS, A_SUBHEADS), f32),
        "a_wqkv": w(ks[9], (N_A_LAYERS, D_MODEL, 3 * D_MODEL), D_MODEL),
        "a_q_norm": gain(ks[10], (N_A_LAYERS, HEAD_DIM)),
        "a_k_norm": gain(ks[11], (N_A_LAYERS, HEAD_DIM)),
        "a_lambda": 0.1 * jax.random.normal(ks[12], (N_A_LAYERS, 4, HEAD_DIM), f32),
        "a_subln": gain(ks[13], (N_A_LAYERS, A_V_DIM)),
        "a_wo": w(ks[14], (N_A_LAYERS, D_MODEL, D_MODEL), D_MODEL),
        "kv_norm": gain(ks[15], (D_MODEL,)),
        "b_wkv": w(ks[16], (D_MODEL, 2 * D_MODEL), D_MODEL),
        "b_wq": w(ks[17], (N_B_LAYERS, D_MODEL, D_MODEL), D_MODEL),
        "b_wo": w(ks[18], (N_B_LAYERS, D_MODEL, D_MODEL), D_MODEL),
    }


def reference(x, ffn_pre_norm, ffn_pre_wi, ffn_pre_wo, mix_norm, ffn_post_norm, ffn_post_wi,
              ffn_post_wo, rel_bias, a_wqkv, a_q_norm, a_k_norm, a_lambda, a_subln, a_wo,
              kv_norm, b_wkv, b_wq, b_wo):
    B, S, _ = x.shape
    k_sh = None
    v_sh = None
    for l in range(DEPTH):
        if l == N_A_LAYERS:
            k_sh, v_sh = jnp.split(_rms_norm(x, kv_norm) @ b_wkv, 2, axis=-1)
            k_sh = k_sh.reshape(B, S, B_HEADS, HEAD_DIM)
            v_sh = v_sh.reshape(B, S, B_HEADS, HEAD_DIM)
        x = x + 0.5 * _swiglu(_rms_norm(x, ffn_pre_norm[l]), ffn_pre_wi[l], ffn_pre_wo[l])
        h = _rms_norm(x, mix_norm[l])
        if l < N_A_LAYERS:
            lambda_init = 0.8 - 0.6 * math.exp(-0.3 * l)
            x = x + _diff_attention(h, a_wqkv[l], a_q_norm[l], a_k_norm[l], a_lambda[l],
                                    a_subln[l], a_wo[l], rel_bias, lambda_init)
        else:
            i = l - N_A_LAYERS
            x = x + _stick_breaking(h, b_wq[i], k_sh, v_sh, b_wo[i])
        x = x + 0.5 * _swiglu(_rms_norm(x, ffn_post_norm[l]), ffn_post_wi[l], ffn_post_wo[l])
    return x
```

```python
import contextlib
import math
import numpy as np
import ml_dtypes
import concourse.bass as bass
import concourse.mybir as mybir
from concourse.bass_utils import run_bass_kernel_spmd

F32 = mybir.dt.float32
BF16 = mybir.dt.bfloat16
U8 = mybir.dt.uint8
AF = mybir.ActivationFunctionType
ALU = mybir.AluOpType
AX = mybir.AxisListType

D = 2048
DC = D // 128
SEQ = 8192
NB = 2
DEPTH = 4
NA = 2
DFF = 5504
FC = DFF // 128
HD = 128
NH = 16
TT = 512
NT = 4
TL = TT * NT
EPS = 1e-6
NEG = -30000.0
NCORES = 8

PE, ACT, DVE, POOL, SP = "pe", "act", "dve", "pool", "sp"


class Res:
    __slots__ = ("name", "w", "rd", "rdma")

    def __init__(self, name):
        self.name = name
        self.w = None
        self.rd = {}
        self.rdma = []


class Op:
    __slots__ = ("eng", "fn", "deps", "dkey", "signal", "ticket", "idx", "inc")


class Prog:
    def __init__(self, nc):
        self.nc = nc
        self.ops = []
        self.res = {}
        self.last = {}
        self.pending_dma = []
        self.bar = {}
        self.dma_keys = []

    def R(self, key):
        r = self.res.get(key)
        if r is None:
            r = Res(key)
            self.res[key] = r
        return r

    def op(self, eng, fn, reads=(), writes=(), dma=None, strict=False, inc=16, bg=False):
        o = Op()
        o.eng = eng
        o.fn = fn
        o.dkey = dma
        o.signal = dma is not None
        o.ticket = 0
        o.inc = inc
        o.idx = len(self.ops)
        deps = set()
        b = self.bar.pop(eng, None)
        if b:
            deps |= b
        for k in reads:
            r = self.R(k)
            if r.w is not None:
                deps.add(r.w)
        for k in writes:
            r = self.R(k)
            if r.w is not None:
                deps.add(r.w)
            deps.update(r.rd.values())
            deps.update(r.rdma)
        for k in reads:
            r = self.R(k)
            if dma is not None:
                r.rdma.append(o.idx)
            else:
                r.rd[eng] = o.idx
        for k in writes:
            r = self.R(k)
            r.w = o.idx
            r.rd = {}
            r.rdma = []
        o.deps = [d for d in deps if strict or self.ops[d].dkey is not None or self.ops[d].eng != eng]
        self.ops.append(o)
        if dma is None:
            self.last[eng] = o.idx
        if dma is not None:
            if not bg:
                self.pending_dma.append(o.idx)
            if dma not in self.dma_keys:
                self.dma_keys.append(dma)
        return o

    def barrier(self):
        deps = set(self.last.values()) | set(self.pending_dma)
        self.pending_dma = []
        for e in (PE, ACT, DVE, POOL, SP):
            self.bar[e] = set(deps) | self.bar.get(e, set())

    def emit(self):
        nc = self.nc
        ops = self.ops
        for o in ops:
            for d in o.deps:
                ops[d].signal = True
        cnt = {}
        for o in ops:
            if o.dkey is not None:
                k = ("dma", o.dkey)
                cnt[k] = cnt.get(k, 0) + o.inc
                o.ticket = cnt[k]
            elif o.signal:
                k = ("eng", o.eng)
                cnt[k] = cnt.get(k, 0) + 1
                o.ticket = cnt[k]
        final_dma = {k[1]: v for k, v in cnt.items() if k[0] == "dma"}
        with contextlib.ExitStack() as es:
            sems = {}
            for e in (PE, ACT, DVE, POOL, SP):
                sems[("eng", e)] = es.enter_context(nc.semaphore("s_" + e))
            for i, k in enumerate(self.dma_keys):
                sems[("dma", k)] = es.enter_context(nc.semaphore("d%d" % i))
            block = es.enter_context(nc.Block())

            def run(engname):
                def body(eng):
                    waited = {}
                    for o in ops:
                        if o.eng != engname:
                            continue
                        need = {}
                        for d in o.deps:
                            p = ops[d]
                            k = ("dma", p.dkey) if p.dkey is not None else ("eng", p.eng)
                            if p.ticket > need.get(k, 0):
                                need[k] = p.ticket
                        for k, v in need.items():
                            if waited.get(k, 0) < v:
                                eng.wait_ge(sems[k], v)
                                waited[k] = v
                        ins = o.fn(eng)
                        if o.dkey is not None:
                            if o.inc == 16:
                                ins.then_inc(sems[("dma", o.dkey)], 16)
                            else:
                                ins.then_inc(sems[("dma", o.dkey)])
                        elif o.signal:
                            ins.then_inc(sems[("eng", o.eng)], 1)
                    if engname == SP:
                        for k, v in final_dma.items():
                            if waited.get(("dma", k), 0) < v:
                                eng.wait_ge(sems[("dma", k)], v)
                return body

            block.tensor(run(PE))
            block.scalar(run(ACT))
            block.vector(run(DVE))
            block.gpsimd(run(POOL))
            block.sync(run(SP))


class Sbuf:
    def __init__(self, big, cap):
        self.big = big
        self.cap = cap
        self.off = 0

    def reset(self, off=0):
        self.off = off

    def alloc(self, free_elems, dt):
        sz = {F32: 4, BF16: 2}[dt]
        nbytes = (free_elems * sz + 63) // 64 * 64
        assert self.off + nbytes <= self.cap, ("SBUF overflow", self.off, nbytes, self.cap)
        v = self.big[:, self.off:self.off + free_elems * sz].bitcast(dt)
        self.off += nbytes
        return v


class Ctx:
    pass


def mk_weight_scratch(cx, name, K, ncols, nblocks):
    return cx.nc.dram_tensor(name, [nblocks, 128, K // 128, ncols], BF16).ap()


def emit_wcvt(cx, W, scr, blocks, ncols, tag, par=0, bg=False):
    p = cx.p
    Wv = W.rearrange("(kc p) n -> p kc n", p=128)
    for bi, c0 in enumerate(blocks):
        src = Wv[:, :, c0:c0 + ncols]
        dst = scr[bi]
        p.op(POOL, (lambda e, s=src, d=dst: e.dma_start(out=d, in_=s)),
             reads=(), writes=((tag, bi),), dma=("cv", par, bi % 4), bg=bg)


def stage_norm(cx, x_tile, g_tile, hT, ones_f, ssq_ps, sq_tiles, rstd, keys, n_dc=DC, dim=D):
    p = cx.p
    kx, kh, kssq, krstd, ksq, kg = keys
    for dc in range(n_dc):
        sq = sq_tiles[dc % 2]
        p.op(ACT, (lambda e, sq=sq, dc=dc: e.activation(out=sq, in_=x_tile[:, dc, :], func=AF.Square)),
             reads=(kx,), writes=((ksq, dc % 2),))
        p.op(PE, (lambda e, sq=sq, dc=dc: e.matmul(ssq_ps, lhsT=ones_f, rhs=sq, start=(dc == 0), stop=(dc == n_dc - 1))),
             reads=((ksq, dc % 2),), writes=(kssq,))
    p.op(ACT, lambda e: e.activation(out=rstd, in_=ssq_ps, func=AF.Sqrt, scale=1.0 / dim, bias=cx.eps_t),
         reads=(kssq, "eps_t"), writes=(krstd,))
    p.op(DVE, lambda e: e.reciprocal(out=rstd, in_=rstd), reads=(krstd,), writes=(krstd,))
    for dc in range(n_dc):
        eng = DVE
        p.op(eng, (lambda e, dc=dc: e.scalar_tensor_tensor(out=hT[:, dc, :], in0=x_tile[:, dc, :], scalar=g_tile[:, dc:dc + 1],
                                                           in1=rstd, op0=ALU.mult, op1=ALU.mult)),
             reads=(kx, krstd, kg), writes=((kh, dc),))


def stage_ffn(cx, x_in, x_out, g_dram, wi_scr, wo_scr, tag, wi_tag=None, wo_tag=None):
    p, nc, sb = cx.p, cx.nc, cx.sb
    p.barrier()
    sb.reset(cx.sb_base)
    xin_v = x_in.rearrange("(dc p) t -> p dc t", p=128)
    xout_v = x_out.rearrange("(dc p) t -> p dc t", p=128)
    g_tile = sb.alloc(DC, F32)
    xt = [sb.alloc(DC * TT, F32).rearrange("p (a b) -> p a b", a=DC) for _ in range(2)]
    hT = [sb.alloc(DC * TT, BF16).rearrange("p (a b) -> p a b", a=DC) for _ in range(2)]
    aT = sb.alloc(FC * TT, BF16).rearrange("p (a b) -> p a b", a=FC)
    NWI = 3
    wi_t = [sb.alloc(2 * DC * 128, BF16).rearrange("p (g k c) -> p g k c", g=2, k=DC) for _ in range(NWI)]
    wo_t = [sb.alloc(FC * 128, BF16).rearrange("p (k c) -> p k c", k=FC) for _ in range(2)]
    sq_t = [sb.alloc(TT, F32) for _ in range(2)]
    rstd = sb.alloc(TT, F32)
    sg_t = [sb.alloc(TT, F32) for _ in range(2)]
    yo_t = [sb.alloc(TT, F32) for _ in range(2)]
    ps = cx.psum
    gate_ps = [ps[0], ps[1]]
    up_ps = [ps[2], ps[3]]
    y_ps = [ps[4], ps[5]]
    ssq_ps = ps[6]
    T = tag
    wi_tag = wi_tag or (T + "wi")
    wo_tag = wo_tag or (T + "wo")
    allw = tuple((wi_tag, i) for i in range(2 * FC)) + tuple((wo_tag, i) for i in range(DC))
    p.op(SP, lambda e: e.dma_start(out=g_tile, in_=g_dram),
         reads=allw, writes=((T, "g"),), dma=("g",))
    wi_cnt = 0
    wo_cnt = 0
    for ti in range(NT):
        xb = ti % 2
        cs = slice(ti * TT, (ti + 1) * TT)
        p.op(SP, (lambda e, xb=xb, cs=cs: e.dma_start(out=xt[xb], in_=xin_v[:, :, cs])),
             writes=((T, "x", xb),), dma=("x", xb))
        stage_norm(cx, xt[xb], g_tile, hT[xb], cx.ones_f, ssq_ps, sq_t, rstd,
                   ((T, "x", xb), (T, "h", xb), (T, "ssq"), (T, "rstd"), (T, "sq"), (T, "g")))
        hkeys = tuple(((T, "h", xb), dc) for dc in range(DC))
        for fc in range(FC):
            wb = wi_cnt % NWI
            wi_cnt += 1
            p.op(SP, (lambda e, wb=wb, fc=fc: e.dma_start(out=wi_t[wb], in_=wi_scr[2 * fc:2 * fc + 2].rearrange("g p k c -> p g k c"))),
                 reads=((wi_tag, 2 * fc), (wi_tag, 2 * fc + 1)), writes=((T, "wi", wb),), dma=("wi", wb))
            pb = fc % 2
            for dc in range(DC):
                p.op(PE, (lambda e, wb=wb, dc=dc, pb=pb, xb=xb: e.matmul(gate_ps[pb], lhsT=wi_t[wb][:, 0, dc, :], rhs=hT[xb][:, dc, :],
                                                                  start=(dc == 0), stop=(dc == DC - 1))),
                     reads=((T, "wi", wb), ((T, "h", xb), dc)), writes=((T, "gate", pb),))
            for dc in range(DC):
                p.op(PE, (lambda e, wb=wb, dc=dc, pb=pb, xb=xb: e.matmul(up_ps[pb], lhsT=wi_t[wb][:, 1, dc, :], rhs=hT[xb][:, dc, :],
                                                                  start=(dc == 0), stop=(dc == DC - 1))),
                     reads=((T, "wi", wb), ((T, "h", xb), dc)), writes=((T, "up", pb),))
            p.op(ACT, (lambda e, pb=pb: e.activation(out=sg_t[pb], in_=gate_ps[pb], func=AF.Silu)),
                 reads=((T, "gate", pb),), writes=((T, "sg", pb),))
            p.op(DVE, (lambda e, pb=pb, fc=fc: e.tensor_tensor(out=aT[:, fc, :], in0=sg_t[pb], in1=up_ps[pb], op=ALU.mult)),
                 reads=((T, "sg", pb), (T, "up", pb)), writes=((T, "a", fc),))
        for dco in range(DC):
            wb = wo_cnt % 2
            wo_cnt += 1
            p.op(SP, (lambda e, wb=wb, dco=dco: e.dma_start(out=wo_t[wb], in_=wo_scr[dco])),
                 reads=((wo_tag, dco),), writes=((T, "wo", wb),), dma=("wo", wb))
            pb = dco % 2
            for fc in range(FC):
                p.op(PE, (lambda e, wb=wb, fc=fc, pb=pb: e.matmul(y_ps[pb], lhsT=wo_t[wb][:, fc, :], rhs=aT[:, fc, :],
                                                                  start=(fc == 0), stop=(fc == FC - 1))),
                     reads=((T, "wo", wb), (T, "a", fc)), writes=((T, "y", pb),))
            p.op(DVE, (lambda e, pb=pb, dco=dco, xb=xb: e.scalar_tensor_tensor(out=yo_t[pb], in0=y_ps[pb], scalar=0.5, in1=xt[xb][:, dco, :],
                                                                                op0=ALU.mult, op1=ALU.add)),
                 reads=((T, "y", pb), (T, "x", xb)), writes=((T, "yo", pb),))
            p.op(SP, (lambda e, pb=pb, dco=dco, cs=cs: e.dma_start(out=xout_v[:, dco, cs], in_=yo_t[pb])),
                 reads=((T, "yo", pb),), writes=((T, "xout", ti, dco),), dma=("yo", pb))


def load_norm_tile(cx, T, x_v, ti, xt, hT, g_tile, sq_t, rstd, ssq_ps, xb):
    p = cx.p
    cs = slice(ti * TT, (ti + 1) * TT)
    p.op(SP, (lambda e, xb=xb, cs=cs: e.dma_start(out=xt[xb], in_=x_v[:, :, cs])),
         writes=((T, "x", xb),), dma=("x", xb))
    stage_norm(cx, xt[xb], g_tile, hT[xb], cx.ones_f, ssq_ps, sq_t, rstd,
               ((T, "x", xb), (T, "h", xb), (T, "ssq"), (T, "rstd"), (T, "sq"), (T, "g")))


def stage_qkv(cx, x_in, g_dram, tag, fm_scr, fm_specs, v_scr, v_out, fm_tag=None, v_tag=None, v_blocked=False, post_tile=None, nfm_blocks=0):
    p, sb = cx.p, cx.sb
    p.barrier()
    sb.reset(cx.sb_base)
    T = tag
    x_v = x_in.rearrange("(dc p) t -> p dc t", p=128)
    g_tile = sb.alloc(DC, F32)
    xt = [sb.alloc(DC * TT, F32).rearrange("p (a b) -> p a b", a=DC) for _ in range(2)]
    hT = [sb.alloc(DC * TT, BF16).rearrange("p (a b) -> p a b", a=DC) for _ in range(2)]
    NW = 3
    w_t = [sb.alloc(DC * 128, BF16).rearrange("p (k c) -> p k c", k=DC) for _ in range(NW)]
    wv_t = [sb.alloc(DC * 512, BF16).rearrange("p (k c) -> p k c", k=DC) for _ in range(2)]
    sq_t = [sb.alloc(TT, F32) for _ in range(2)]
    rstd = sb.alloc(TT, F32)
    sq2 = [sb.alloc(TT, F32) for _ in range(2)]
    rs2 = [sb.alloc(TT, F32) for _ in range(2)]
    st_t = [sb.alloc(TT, BF16) for _ in range(3)]
    ps = cx.psum
    acc_ps = [ps[0], ps[1], ps[2]]
    ssq2_ps = [ps[3], ps[4]]
    ssq_ps = ps[6]
    fm_tag = fm_tag or (T + "wfm")
    v_tag = v_tag or (T + "wv")
    allw = tuple((fm_tag, i) for i in range(nfm_blocks)) + (tuple((v_tag, i) for i in range(4)) if v_scr is not None else ())
    p.op(SP, lambda e: e.dma_start(out=g_tile, in_=g_dram), reads=allw, writes=((T, "g"),), dma=("g",))
    wcnt = 0
    scnt = 0
    vcnt = 0
    for ti in range(NT):
        xb = ti % 2
        load_norm_tile(cx, T, x_v, ti, xt, hT, g_tile, sq_t, rstd, ssq_ps, xb)
        for si, (bi, dst_fn, gcol, scale) in enumerate(fm_specs):
            wb = wcnt % NW
            ab = wcnt % 3
            wcnt += 1
            p.op(SP, (lambda e, wb=wb, bi=bi: e.dma_start(out=w_t[wb], in_=fm_scr[bi])),
                 reads=((fm_tag, bi),), writes=((T, "w", wb),), dma=("wi", wb))
            for dc in range(DC):
                p.op(PE, (lambda e, wb=wb, dc=dc, ab=ab, xb=xb: e.matmul(acc_ps[ab], lhsT=w_t[wb][:, dc, :], rhs=hT[xb][:, dc, :],
                                                                        start=(dc == 0), stop=(dc == DC - 1))),
                     reads=((T, "w", wb), ((T, "h", xb), dc)), writes=((T, "acc", ab),))
            sb_i = scnt % 3
            scnt += 1
            if gcol is not None:
                qb = si % 2
                p.op(ACT, (lambda e, ab=ab, qb=qb: e.activation(out=sq2[qb], in_=acc_ps[ab], func=AF.Square)),
                     reads=((T, "acc", ab),), writes=((T, "sq2", qb),))
                p.op(PE, (lambda e, qb=qb: e.matmul(ssq2_ps[qb], lhsT=cx.ones_f, rhs=sq2[qb], start=True, stop=True)),
                     reads=((T, "sq2", qb), "ones_f"), writes=((T, "ssq2", qb),))
                p.op(ACT, (lambda e, qb=qb: e.activation(out=rs2[qb], in_=ssq2_ps[qb], func=AF.Sqrt, scale=1.0 / HD, bias=cx.eps_t)),
                     reads=((T, "ssq2", qb), "eps_t"), writes=((T, "rs2", qb),))
                p.op(DVE, (lambda e, qb=qb: e.reciprocal(out=rs2[qb], in_=rs2[qb])), reads=((T, "rs2", qb),), writes=((T, "rs2", qb),))
                p.op(DVE, (lambda e, ab=ab, qb=qb, sb_i=sb_i, gcol=gcol: e.scalar_tensor_tensor(
                    out=st_t[sb_i], in0=acc_ps[ab], scalar=gcol, in1=rs2[qb], op0=ALU.mult, op1=ALU.mult)),
                     reads=((T, "acc", ab), (T, "rs2", qb), (T, "gains")), writes=((T, "st", sb_i),))
            else:
                p.op(ACT, (lambda e, ab=ab, sb_i=sb_i, scale=scale: e.activation(out=st_t[sb_i], in_=acc_ps[ab], func=AF.Copy, scale=float(scale))),
                     reads=((T, "acc", ab),), writes=((T, "st", sb_i),))
            p.op(SP, (lambda e, sb_i=sb_i, dst_fn=dst_fn, ti=ti: e.dma_start(out=dst_fn(ti), in_=st_t[sb_i])),
                 reads=((T, "st", sb_i),), writes=((T, "fmout", si, ti),), dma=("st", sb_i))
        if v_scr is not None:
            for eb in range(4):
                vb = vcnt % 2
                vcnt += 1
                p.op(SP, (lambda e, vb=vb, eb=eb: e.dma_start(out=wv_t[vb], in_=v_scr[eb])),
                     reads=((v_tag, eb),), writes=((T, "wv", vb),), dma=("wo", vb))
                for tb in range(4):
                    ab = wcnt % 3
                    wcnt += 1
                    for dc in range(DC):
                        p.op(PE, (lambda e, vb=vb, dc=dc, ab=ab, xb=xb, tb=tb: e.matmul(
                            acc_ps[ab], lhsT=hT[xb][:, dc, tb * 128:(tb + 1) * 128], rhs=wv_t[vb][:, dc, :],
                            start=(dc == 0), stop=(dc == DC - 1))),
                             reads=((T, "wv", vb), ((T, "h", xb), dc)), writes=((T, "acc", ab),))
                    sb_i = scnt % 3
                    scnt += 1
                    p.op(ACT, (lambda e, ab=ab, sb_i=sb_i: e.activation(out=st_t[sb_i], in_=acc_ps[ab], func=AF.Copy)),
                         reads=((T, "acc", ab),), writes=((T, "st", sb_i),))
                    p.op(SP, (lambda e, sb_i=sb_i, ti=ti, tb=tb, eb=eb: e.dma_start(
                        out=(v_out[ti][eb][tb * 128:(tb + 1) * 128, :] if v_blocked else v_out[ti][tb * 128:(tb + 1) * 128, eb * 512:(eb + 1) * 512]), in_=st_t[sb_i])),
                         reads=((T, "st", sb_i),), writes=((T, "vout", ti, tb, eb),), dma=("st", sb_i))
        if post_tile is not None:
            post_tile(ti)


def stage_wo(cx, x_in, x_out, oT, wo_scr, tag, w_tag=None):
    p, sb = cx.p, cx.sb
    p.barrier()
    sb.reset(cx.sb_base)
    T = tag
    xin_v = x_in.rearrange("(dc p) t -> p dc t", p=128)
    xout_v = x_out.rearrange("(dc p) t -> p dc t", p=128)
    o_v = oT.rearrange("(dc p) t -> p dc t", p=128)
    xt = [sb.alloc(DC * TT, F32).rearrange("p (a b) -> p a b", a=DC) for _ in range(2)]
    ot = [sb.alloc(DC * TT, BF16).rearrange("p (a b) -> p a b", a=DC) for _ in range(2)]
    w_t = [sb.alloc(DC * 128, BF16).rearrange("p (k c) -> p k c", k=DC) for _ in range(3)]
    yo_t = [sb.alloc(TT, F32) for _ in range(2)]
    ps = cx.psum
    y_ps = [ps[0], ps[1]]
    wcnt = 0
    w_tag = w_tag or (T + "w")
    allw = tuple((w_tag, i) for i in range(DC))
    for ti in range(NT):
        xb = ti % 2
        cs = slice(ti * TT, (ti + 1) * TT)
        p.op(SP, (lambda e, xb=xb, cs=cs: e.dma_start(out=xt[xb], in_=xin_v[:, :, cs])),
             reads=(allw if ti == 0 else ()), writes=((T, "x", xb),), dma=("x", xb))
        p.op(SP, (lambda e, xb=xb, cs=cs: e.dma_start(out=ot[xb], in_=o_v[:, :, cs])),
             writes=((T, "o", xb),), dma=("o", xb))
        for dco in range(DC):
            wb = wcnt % 3
            pb = wcnt % 2
            wcnt += 1
            p.op(SP, (lambda e, wb=wb, dco=dco: e.dma_start(out=w_t[wb], in_=wo_scr[dco])),
                 reads=((w_tag, dco),), writes=((T, "w", wb),), dma=("wi", wb))
            for ec in range(DC):
                p.op(PE, (lambda e, wb=wb, ec=ec, pb=pb, xb=xb: e.matmul(y_ps[pb], lhsT=w_t[wb][:, ec, :], rhs=ot[xb][:, ec, :],
                                                                        start=(ec == 0), stop=(ec == DC - 1))),
                     reads=((T, "w", wb), (T, "o", xb)), writes=((T, "y", pb),))
            p.op(DVE, (lambda e, pb=pb, dco=dco, xb=xb: e.tensor_tensor(out=yo_t[pb], in0=y_ps[pb], in1=xt[xb][:, dco, :], op=ALU.add)),
                 reads=((T, "y", pb), (T, "x", xb)), writes=((T, "yo", pb),))
            p.op(SP, (lambda e, pb=pb, dco=dco, cs=cs: e.dma_start(out=xout_v[:, dco, cs], in_=yo_t[pb])),
                 reads=((T, "yo", pb),), writes=((T, "xout", ti, dco),), dma=("yo", pb))


def stage_attn_a(cx, qT, Kg, Vg, biasM, sel_d, b31_d, lam_d, gsub_d, oT, lambda_init, tag, dbg=None, cc=False):
    p, sb = cx.p, cx.sb
    p.barrier()
    sb.reset(cx.sb_base)
    T = tag
    ps = cx.psum
    s_ps = [ps[0], ps[1], ps[2]]
    o_ps = [ps[3], ps[4]]
    den_ps = ps[5]
    ssq_ps = ps[6]
    sel_t = sb.alloc(17 * 4, F32).rearrange("p (b c) -> p b c", c=4)
    b31_t = sb.alloc(NH, F32)
    lam_t = sb.alloc(4 * 128, F32).rearrange("p (a b) -> p a b", a=4)
    lprod = sb.alloc(128, F32)
    lsum = sb.alloc(2, F32)
    nlam = sb.alloc(1, F32)
    gsub_t = sb.alloc(2, F32)
    ccol = [sb.alloc(17, F32) for _ in range(2)]
    Kt = [sb.alloc(16 * TT, BF16).rearrange("p (g t) -> p g t", g=16) for _ in range(2)]
    Vt = sb.alloc(64 * 256, BF16).rearrange("p (b e) -> p b e", b=64)
    Qt = sb.alloc(2 * TL, BF16).rearrange("p (m t) -> p m t", m=2)
    Mt = [sb.alloc(1024, F32) for _ in range(2)]
    tmp_t = [sb.alloc(TT, F32) for _ in range(3)]
    pT_t = [sb.alloc(TT, BF16) for _ in range(4)]
    Oev = [sb.alloc(2 * TT, F32).rearrange("p (a b) -> p a b", a=2) for _ in range(2)]
    dev = [sb.alloc(TT, F32) for _ in range(2)]
    o_t = sb.alloc(2 * TT, F32).rearrange("p (a b) -> p a b", a=2)
    u_t = sb.alloc(TT, F32)
    sq_t = [sb.alloc(TT, F32) for _ in range(2)]
    rstd = sb.alloc(TT, F32)
    on_t = [sb.alloc(TT, BF16) for _ in range(2)]
    p.op(SP, lambda e: e.dma_start(out=sel_t, in_=sel_d.rearrange("p (b c) -> p b c", c=4)), writes=((T, "sel"),), dma=("c", 0))
    p.op(SP, lambda e: e.dma_start(out=b31_t, in_=b31_d), writes=((T, "b31"),), dma=("c", 1))
    p.op(SP, lambda e: e.dma_start(out=lam_t, in_=lam_d.rearrange("p (a b) -> p a b", a=4)), writes=((T, "lamv"),), dma=("c", 2))
    p.op(SP, lambda e: e.dma_start(out=gsub_t, in_=gsub_d), writes=((T, "gsub"),), dma=("c", 3))
    lprod2 = sb.alloc(128, F32)
    nlam0 = sb.alloc(1, F32)
    p.op(DVE, lambda e: e.tensor_tensor(out=lprod, in0=lam_t[:, 0, :], in1=lam_t[:, 1, :], op=ALU.mult),
         reads=((T, "lamv"),), writes=((T, "lp0"),))
    p.op(DVE, lambda e: e.reduce_sum(out=lsum[:, 0:1], in_=lprod, axis=AX.X), reads=((T, "lp0"),), writes=((T, "ls0"),), strict=True)
    p.op(DVE, lambda e: e.tensor_tensor(out=lprod2, in0=lam_t[:, 2, :], in1=lam_t[:, 3, :], op=ALU.mult),
         reads=((T, "lamv"),), writes=((T, "lp1"),))
    p.op(DVE, lambda e: e.reduce_sum(out=lsum[:, 1:2], in_=lprod2, axis=AX.X), reads=((T, "lp1"),), writes=((T, "ls1"),), strict=True)
    p.op(ACT, lambda e: e.activation(out=lsum, in_=lsum, func=AF.Exp), reads=((T, "ls0"), (T, "ls1")), writes=((T, "lsum"),), strict=True)
    p.op(DVE, lambda e: e.tensor_tensor(out=nlam0, in0=lsum[:, 1:2], in1=lsum[:, 0:1], op=ALU.subtract),
         reads=((T, "lsum"),), writes=((T, "nlam0"),))
    p.op(DVE, lambda e: e.tensor_scalar(out=nlam, in0=nlam0, scalar1=-float(lambda_init), scalar2=None, op0=ALU.add),
         reads=((T, "nlam0"),), writes=((T, "nlam"),), strict=True)
    p.op(DVE, lambda e: e.tensor_scalar(out=gsub_t, in0=gsub_t, scalar1=float(1.0 - lambda_init), scalar2=None, op0=ALU.mult),
         reads=((T, "gsub"),), writes=((T, "gsub"),), strict=True)
    scnt = 0
    pcnt = 0
    tcnt = 0
    for hd in (range(NH // 2) if dbg is None else [0]):
        for m in range(2):
            h = 2 * hd + m
            if cc:
                for Jl in range(NT):
                    p.op(SP, (lambda e, m=m, h=h, Jl=Jl: e.dma_start(out=Kt[m][:, 4 * Jl:4 * Jl + 4, :],
                                                                    in_=Kg[Jl, h // 4, :, (h % 4) * 128:(h % 4 + 1) * 128, :].rearrange("r d t -> d r t"))),
                         reads=(), writes=((T, "K", m, Jl),), dma=("k", m))
            else:
                p.op(SP, (lambda e, m=m, h=h: e.dma_start(out=Kt[m], in_=Kg[:, h].rearrange("g d t -> d g t"))),
                     reads=((T, "Kg"),), writes=((T, "K", m),), dma=("k", m))
            p.op(SP, (lambda e, m=m, h=h: e.dma_start(out=Qt[:, m, :], in_=qT[h])),
                 reads=((T, "qT"),), writes=((T, "Q", m),), dma=("q", m))
            p.op(SP, (lambda e, m=m, h=h: e.dma_start(out=Mt[m], in_=biasM[h])),
                 reads=(), writes=((T, "M", m),), dma=("m", m))
            p.op(DVE, (lambda e, m=m, h=h: e.scalar_tensor_tensor(out=ccol[m], in0=sel_t[:, :, 2], scalar=b31_t[:, h:h + 1],
                                                                 in1=sel_t[:, :, 3], op0=ALU.mult, op1=ALU.add)),
                 reads=((T, "sel"), (T, "b31")), writes=((T, "ccol", m),))
        if cc:
            for Jl in range(NT):
                p.op(SP, (lambda e, hd=hd, Jl=Jl: e.dma_start(out=Vt[:, 16 * Jl:16 * Jl + 16, :],
                                                            in_=Vg[Jl, hd // 2, :, :, (hd % 2) * 256:(hd % 2 + 1) * 256].rearrange("r (tb p) e -> p (r tb) e", p=128))),
                     reads=(), writes=((T, "V", Jl),), dma=("v",))
        else:
            p.op(SP, (lambda e, hd=hd: e.dma_start(out=Vt, in_=Vg[:, :, hd * 256:(hd + 1) * 256].rearrange("g (tb p) e -> p (g tb) e", p=128))),
                 reads=((T, "Vg"),), writes=((T, "V"),), dma=("v",))
        for J in (range(NT) if dbg is None else [0]):
            qs = slice(J * TT, (J + 1) * TT)
            for m in range(2):
                h = 2 * hd + m
                blocks = []
                for Jp in range(J + 1):
                    for rp in range(4):
                        for kbi in range(4):
                            if Jp == J:
                                si = 1 + rp * 4 + kbi
                            elif Jp == J - 1 and rp == 3 and kbi == 3:
                                si = 0
                            else:
                                si = None
                            blocks.append((4 * Jp + rp, kbi, si))
                nb = len(blocks)
                for bi, (g, kbi, si) in enumerate(blocks):
                    sbk = scnt % 3
                    scnt += 1
                    p.op(PE, (lambda e, sbk=sbk, m=m, g=g, kbi=kbi, qs=qs: e.matmul(
                        s_ps[sbk], lhsT=Kt[m][:, g, kbi * 128:(kbi + 1) * 128], rhs=Qt[:, m, qs], start=True, stop=True)),
                         reads=(((T, "K", m, g // 4) if cc else (T, "K", m)), (T, "Q", m)), writes=((T, "s", sbk),))
                    pb = pcnt % 4
                    pcnt += 1
                    if si is None:
                        p.op(ACT, (lambda e, sbk=sbk, pb=pb, h=h: e.activation(out=pT_t[pb], in_=s_ps[sbk], func=AF.Exp, bias=b31_t[:, h:h + 1])),
                             reads=((T, "s", sbk), (T, "b31")), writes=((T, "pT", pb),))
                    else:
                        tb = tcnt % 3
                        tcnt += 1
                        ta = Mt[m][:, 384 - kbi * 128:384 - kbi * 128 + TT]
                        tbb = Mt[m][:, 512:1024]
                        def f_sel(e, tb=tb, sbk=sbk, si=si, ta=ta, tbb=tbb):
                            e.scalar_tensor_tensor(out=tmp_t[tb], in0=ta, scalar=sel_t[:, si, 0:1], in1=s_ps[sbk], op0=ALU.mult, op1=ALU.add)
                            return e.scalar_tensor_tensor(out=tmp_t[tb], in0=tbb, scalar=sel_t[:, si, 1:2], in1=tmp_t[tb], op0=ALU.mult, op1=ALU.add)
                        p.op(DVE, f_sel, reads=((T, "s", sbk), (T, "M", m), (T, "sel")), writes=((T, "tmp", tb),))
                        p.op(ACT, (lambda e, tb=tb, pb=pb, m=m, si=si: e.activation(out=pT_t[pb], in_=tmp_t[tb], func=AF.Exp, bias=ccol[m][:, si:si + 1])),
                             reads=((T, "tmp", tb), (T, "ccol", m)), writes=((T, "pT", pb),))
                    vblk = g * 4 + kbi
                    for ec in range(2):
                        p.op(PE, (lambda e, pb=pb, vblk=vblk, ec=ec, bi=bi, nb=nb: e.matmul(
                            o_ps[ec], lhsT=Vt[:, vblk, ec * 128:(ec + 1) * 128], rhs=pT_t[pb], start=(bi == 0), stop=(bi == nb - 1))),
                             reads=((T, "pT", pb), ((T, "V", g // 4) if cc else (T, "V"))), writes=((T, "ops", ec),))
                    p.op(PE, (lambda e, pb=pb, bi=bi, nb=nb: e.matmul(den_ps, lhsT=cx.ones_b, rhs=pT_t[pb], start=(bi == 0), stop=(bi == nb - 1))),
                         reads=((T, "pT", pb), "ones_b"), writes=((T, "den"),))
                for ec in range(2):
                    p.op(ACT, (lambda e, m=m, ec=ec: e.activation(out=Oev[m][:, ec, :], in_=o_ps[ec], func=AF.Copy)),
                         reads=((T, "ops", ec),), writes=((T, "Oev", m, ec),))
                p.op(DVE, (lambda e, m=m: e.reciprocal(out=dev[m], in_=den_ps)), reads=((T, "den"),), writes=((T, "dev", m),))
            if dbg is not None:
                p.op(SP, lambda e: e.dma_start(out=dbg[5][:, 0:2], in_=lsum), reads=((T, "lsum"), (T, "nlam")), writes=(("dbg", "l"),), dma=("c", 0))
                p.op(SP, lambda e: e.dma_start(out=dbg[5][:, 2:3], in_=nlam, allow_slow_non_contiguous=True), reads=((T, "nlam"),), writes=(("dbg", "n"),), dma=("c", 0))
                p.op(SP, lambda e: e.dma_start(out=dbg[5][:, 4:6], in_=gsub_t), reads=((T, "gsub"),), writes=(("dbg", "g"),), dma=("c", 0))
                p.barrier()
                for m in range(1):
                    for ec in range(2):
                        p.op(SP, (lambda e, m=m, ec=ec: e.dma_start(out=dbg[m * 3 + ec], in_=Oev[m][:, ec, :])),
                             reads=((T, "Oev", m, ec),), writes=(("dbg", m, ec),), dma=("c", 0))
                    p.op(SP, (lambda e, m=m: e.dma_start(out=dbg[m * 3 + 2], in_=dev[m])),
                         reads=((T, "dev", m),), writes=(("dbg", m, 2),), dma=("c", 0))
                p.barrier()
            p.op(DVE, lambda e: e.tensor_scalar(out=dev[1], in0=dev[1], scalar1=nlam[:, 0:1], scalar2=None, op0=ALU.mult),
                 reads=((T, "dev", 1), (T, "nlam")), writes=((T, "dev", 1),))
            for ec in range(2):
                def f_comb(e, ec=ec):
                    e.tensor_tensor(out=o_t[:, ec, :], in0=Oev[0][:, ec, :], in1=dev[0], op=ALU.mult)
                    e.tensor_tensor(out=u_t, in0=Oev[1][:, ec, :], in1=dev[1], op=ALU.mult)
                    return e.tensor_tensor(out=o_t[:, ec, :], in0=o_t[:, ec, :], in1=u_t, op=ALU.add)
                p.op(DVE, f_comb, reads=((T, "Oev", 0, ec), (T, "Oev", 1, ec), (T, "dev", 0), (T, "dev", 1)), writes=((T, "o", ec),))
                p.op(ACT, (lambda e, ec=ec: e.activation(out=sq_t[ec], in_=o_t[:, ec, :], func=AF.Square)),
                     reads=((T, "o", ec),), writes=((T, "sq", ec),))
                p.op(PE, (lambda e, ec=ec: e.matmul(ssq_ps, lhsT=cx.ones_f, rhs=sq_t[ec], start=(ec == 0), stop=(ec == 1))),
                     reads=((T, "sq", ec), "ones_f"), writes=((T, "ssq"),))
            p.op(ACT, lambda e: e.activation(out=rstd, in_=ssq_ps, func=AF.Sqrt, scale=1.0 / 256.0, bias=cx.eps_t),
                 reads=((T, "ssq"), "eps_t"), writes=((T, "rstd"),))
            p.op(DVE, lambda e: e.reciprocal(out=rstd, in_=rstd), reads=((T, "rstd"),), writes=((T, "rstd"),))
            for ec in range(2):
                p.op(DVE, (lambda e, ec=ec: e.scalar_tensor_tensor(out=on_t[ec], in0=o_t[:, ec, :], scalar=gsub_t[:, ec:ec + 1], in1=rstd,
                                                                   op0=ALU.mult, op1=ALU.mult)),
                     reads=((T, "o", ec), (T, "rstd"), (T, "gsub")), writes=((T, "on", ec),))
                r0 = hd * 256 + ec * 128
                p.op(SP, (lambda e, ec=ec, r0=r0, qs=qs: e.dma_start(out=oT[r0:r0 + 128, qs], in_=on_t[ec])),
                     reads=((T, "on", ec),), writes=((T, "oT", hd, J, ec),), dma=("on", ec))


def stage_attn_b(cx, qT, Kg, Vg, m01_d, negm_d, cmat_d, oT, tag, cc=False):
    p, sb = cx.p, cx.sb
    p.barrier()
    sb.reset(cx.sb_base)
    T = tag
    ps = cx.psum
    z_ps = [ps[0], ps[1], ps[2], ps[3]]
    ob_ps = [ps[4], ps[5]]
    r_ps = ps[6]
    tr_ps = ps[7]
    m01_t = sb.alloc(16 * TT, BF16).rearrange("p (b t) -> p b t", b=16)
    negm_t = sb.alloc(16 * TT, BF16).rearrange("p (b t) -> p b t", b=16)
    cm_t = sb.alloc(3 * 128, BF16).rearrange("p (a b) -> p a b", a=3)
    identf = sb.alloc(128, F32)
    Kt = [sb.alloc(16 * TT, BF16).rearrange("p (g t) -> p g t", g=16) for _ in range(2)]
    Vt = [sb.alloc(64 * 128, BF16).rearrange("p (b e) -> p b e", b=64) for _ in range(2)]
    Qt = [sb.alloc(TL, BF16) for _ in range(2)]
    e_t = [sb.alloc(TT, F32) for _ in range(3)]
    sp_t = [sb.alloc(TT, BF16) for _ in range(4)]
    w_t = [sb.alloc(TT, BF16) for _ in range(3)]
    E_t = [sb.alloc(4, F32) for _ in range(3)]
    O_t = [sb.alloc(4 * 128, F32).rearrange("p (a b) -> p a b", a=4) for _ in range(2)]
    oo_t = [sb.alloc(TT, BF16) for _ in range(2)]
    p.op(SP, lambda e: e.dma_start(out=m01_t, in_=m01_d.rearrange("p (b t) -> p b t", b=16)), writes=((T, "m01"),), dma=("c", 0))
    p.op(SP, lambda e: e.dma_start(out=negm_t, in_=negm_d.rearrange("p (b t) -> p b t", b=16)), writes=((T, "negm"),), dma=("c", 1))
    p.op(SP, lambda e: e.dma_start(out=cm_t, in_=cmat_d.rearrange("p (a b) -> p a b", a=3)), writes=((T, "cm"),), dma=("c", 2))
    p.op(DVE, lambda e: e.tensor_copy(out=identf, in_=cm_t[:, 1, :]), reads=((T, "cm"),), writes=((T, "identf"),))
    negU = cm_t[:, 0, :]
    ident = cm_t[:, 1, :]
    negones = cm_t[:, 2, 0:1]
    zc = 0
    ec_ = 0
    spc = 0
    wc = 0
    Ec = 0
    obc = 0
    hj = 0
    for h in range(NH):
        kb_ = h % 2
        if cc:
            for Jl in range(NT):
                p.op(SP, (lambda e, kb_=kb_, h=h, Jl=Jl: e.dma_start(out=Kt[kb_][:, 4 * Jl:4 * Jl + 4, :],
                                                                    in_=Kg[Jl, h // 4, :, (h % 4) * 128:(h % 4 + 1) * 128, :].rearrange("r d t -> d r t"))),
                     reads=(), writes=((T, "K", kb_, Jl),), dma=("k", kb_))
        else:
            p.op(SP, (lambda e, kb_=kb_, h=h: e.dma_start(out=Kt[kb_], in_=Kg[:, h].rearrange("g d t -> d g t"))),
                 reads=((T, "Kg"),), writes=((T, "K", kb_),), dma=("k", kb_))
        p.op(SP, (lambda e, kb_=kb_, h=h: e.dma_start(out=Qt[kb_], in_=qT[h])),
             reads=((T, "qT"),), writes=((T, "Q", kb_),), dma=("q", kb_))
        if cc:
            for Jl in range(NT):
                p.op(SP, (lambda e, kb_=kb_, h=h, Jl=Jl: e.dma_start(out=Vt[kb_][:, 16 * Jl:16 * Jl + 16, :],
                                                                    in_=Vg[Jl, h // 4, :, :, (h % 4) * 128:(h % 4 + 1) * 128].rearrange("r (tb p) e -> p (r tb) e", p=128))),
                     reads=(), writes=((T, "V", kb_, Jl),), dma=("v", kb_))
        else:
            p.op(SP, (lambda e, kb_=kb_, h=h: e.dma_start(out=Vt[kb_], in_=Vg[:, :, h * 128:(h + 1) * 128].rearrange("g (tb p) e -> p (g tb) e", p=128))),
                 reads=((T, "Vg"),), writes=((T, "V", kb_),), dma=("v", kb_))
        for J in range(NT):
            qs = slice(J * TT, (J + 1) * TT)
            ob_ = hj % 2
            hj += 1
            blocks = []
            for Jp in range(J + 1):
                for rp in range(4):
                    for kbi in range(4):
                        blocks.append((4 * Jp + rp, kbi, (rp * 4 + kbi) if Jp == J else None))
            blocks = blocks[::-1]
            nb = len(blocks)
            for bi, (g, kbi, mi) in enumerate(blocks):
                zb = zc % 4
                zc += 1
                p.op(PE, (lambda e, zb=zb, kb_=kb_, g=g, kbi=kbi, qs=qs: e.matmul(
                    z_ps[zb], lhsT=Kt[kb_][:, g, kbi * 128:(kbi + 1) * 128], rhs=Qt[kb_][:, qs], start=True, stop=True)),
                     reads=(((T, "K", kb_, g // 4) if cc else (T, "K", kb_)), (T, "Q", kb_)), writes=((T, "z", zb),))
                eb = ec_ % 3
                ec_ += 1
                p.op(ACT, (lambda e, zb=zb, eb=eb: e.activation(out=e_t[eb], in_=z_ps[zb], func=AF.Exp)),
                     reads=((T, "z", zb),), writes=((T, "e", eb),))
                sb_ = spc % 4
                spc += 1
                p.op(ACT, (lambda e, eb=eb, sb_=sb_: e.activation(out=sp_t[sb_], in_=e_t[eb], func=AF.Ln, bias=cx.ones_f[:, 0:1])),
                     reads=((T, "e", eb),), writes=((T, "sp", sb_),))
                if mi is not None:
                    p.op(DVE, (lambda e, sb_=sb_, mi=mi: e.tensor_tensor(out=sp_t[sb_], in0=sp_t[sb_], in1=m01_t[:, mi, :], op=ALU.mult)),
                         reads=((T, "sp", sb_), (T, "m01")), writes=((T, "sp", sb_),))
                last_is_mask = mi is not None
                p.op(PE, (lambda e, zb=zb, sb_=sb_, lm=last_is_mask: e.matmul(z_ps[zb], lhsT=negU, rhs=sp_t[sb_], start=False, stop=(not lm))),
                     reads=((T, "sp", sb_), (T, "cm"), (T, "e", eb)), writes=((T, "z", zb),))
                if mi is not None:
                    p.op(PE, (lambda e, zb=zb, mi=mi: e.matmul(z_ps[zb], lhsT=ident, rhs=negm_t[:, mi, :], start=False, stop=True)),
                         reads=((T, "negm"), (T, "cm")), writes=((T, "z", zb),))
                wb = wc % 3
                wc += 1
                p.op(ACT, (lambda e, zb=zb, wb=wb: e.activation(out=w_t[wb], in_=z_ps[zb], func=AF.Exp)),
                     reads=((T, "z", zb),), writes=((T, "w", wb),))
                Eb = Ec % 3
                if bi > 0:
                    Ec += 1
                    p.op(ACT, (lambda e, Eb=Eb: e.activation(out=E_t[Eb], in_=r_ps[:, 0:4], func=AF.Exp)),
                         reads=((T, "R"),), writes=((T, "E", Eb),))
                def f_r(e, sb_=sb_, bi=bi, nb=nb):
                    ins = None
                    for ts_ in range(4):
                        ins = e.matmul(r_ps[:, ts_:ts_ + 1], lhsT=sp_t[sb_][:, ts_ * 128:(ts_ + 1) * 128], rhs=negones,
                                       start=(bi == 0 and ts_ == 0), stop=(bi == nb - 1), skip_group_check=True)
                    return ins
                p.op(PE, f_r, reads=((T, "sp", sb_), (T, "cm")), writes=((T, "R"),))
                ob2 = obc % 2
                obc += 1
                vblk = g * 4 + kbi
                def f_pv(e, wb=wb, ob2=ob2, vblk=vblk, kb_=kb_):
                    ins = None
                    for ts_ in range(4):
                        ins = e.matmul(ob_ps[ob2][:, ts_ * 128:(ts_ + 1) * 128], lhsT=w_t[wb][:, ts_ * 128:(ts_ + 1) * 128],
                                       rhs=Vt[kb_][:, vblk, :], start=True, stop=True)
                    return ins
                p.op(PE, f_pv, reads=((T, "w", wb), ((T, "V", kb_, g // 4) if cc else (T, "V", kb_))), writes=((T, "ob", ob2),))
                if bi == 0:
                    p.op(DVE, (lambda e, ob2=ob2, ob_=ob_: e.tensor_copy(out=O_t[ob_], in_=ob_ps[ob2].rearrange("p (a b) -> p a b", a=4))),
                         reads=((T, "ob", ob2),), writes=((T, "O", ob_),))
                else:
                    def f_acc(e, ob2=ob2, ob_=ob_, Eb=Eb):
                        ins = None
                        for ts_ in range(4):
                            ins = e.scalar_tensor_tensor(out=O_t[ob_][:, ts_, :], in0=ob_ps[ob2][:, ts_ * 128:(ts_ + 1) * 128],
                                                         scalar=E_t[Eb][:, ts_:ts_ + 1], in1=O_t[ob_][:, ts_, :], op0=ALU.mult, op1=ALU.add)
                        return ins
                    p.op(DVE, f_acc, reads=((T, "ob", ob2), (T, "E", Eb)), writes=((T, "O", ob_),))
            oo = hj % 2
            for ts_ in range(4):
                p.op(PE, (lambda e, ts_=ts_, ob_=ob_: e.transpose(tr_ps[:, ts_ * 128:(ts_ + 1) * 128], O_t[ob_][:, ts_, :], identf)),
                     reads=((T, "O", ob_), (T, "identf")), writes=((T, "tr", ts_),))
            p.op(ACT, (lambda e, oo=oo: e.activation(out=oo_t[oo], in_=tr_ps, func=AF.Copy)),
                 reads=tuple((T, "tr", i) for i in range(4)), writes=((T, "oo", oo),))
            p.op(SP, (lambda e, oo=oo, h=h, qs=qs: e.dma_start(out=oT[h * 128:(h + 1) * 128, qs], in_=oo_t[oo])),
                 reads=((T, "oo", oo),), writes=((T, "oT", h, J),), dma=("on", oo))


def alloc_common(cx, es):
    nc = cx.nc
    cap = 200 * 1024
    big = es.enter_context(nc.sbuf_tensor("big", [128, cap], U8))
    cx.sb = Sbuf(big, cap)
    cx.psum = [es.enter_context(nc.psum_tensor("ps%d" % i, [128, 512], F32)) for i in range(8)]
    cx.psum = [t[:] for t in cx.psum]
    cx.ones_f = cx.sb.alloc(128, F32)
    cx.ones_b = cx.sb.alloc(128, BF16)
    cx.ident_b = cx.sb.alloc(128, BF16)
    cx.eps_t = cx.sb.alloc(1, F32)
    cx.sb_base = cx.sb.off
    p = cx.p
    p.op(DVE, lambda e: e.memset(cx.ones_f, 1.0), writes=("ones_f",))
    p.op(DVE, lambda e: e.memset(cx.ones_b, 1.0), writes=("ones_b",))
    p.op(DVE, lambda e: e.memset(cx.eps_t, EPS), writes=("eps_t",))


def _new_cx():
    nc = bass.Bass("TRN2", target_bir_lowering=False)
    cx = Ctx()
    cx.nc = nc
    cx.p = Prog(nc)
    return nc, cx


def _ein(nc, name, shape, dt=F32):
    return nc.dram_tensor(name, list(shape), dt, kind="ExternalInput").ap()


def _eout(nc, name, shape, dt=F32):
    return nc.dram_tensor(name, list(shape), dt, kind="ExternalOutput").ap()


def ffn_blocks():
    blocks = []
    for fc in range(FC):
        blocks += [fc * 128, DFF + fc * 128]
    return blocks


def build_ffn_prog():
    nc, cx = _new_cx()
    x = _ein(nc, "x", [D, TL])
    g = _ein(nc, "g", [128, DC])
    wi = _ein(nc, "wi", [D, 2 * DFF])
    wo = _ein(nc, "wo", [DFF, D])
    y = _eout(nc, "y", [D, TL])
    wi_scr = mk_weight_scratch(cx, "wi_b", D, 128, 2 * FC)
    wo_scr = mk_weight_scratch(cx, "wo_b", DFF, 128, DC)
    with contextlib.ExitStack() as es:
        alloc_common(cx, es)
        emit_wcvt(cx, wi, wi_scr, ffn_blocks(), 128, "fwi")
        emit_wcvt(cx, wo, wo_scr, [i * 128 for i in range(DC)], 128, "fwo")
        cx.p.barrier()
        stage_ffn(cx, x, y, g, wi_scr, wo_scr, "f")
        cx.p.emit()
    return nc


def qkv_specs(cx, tag, qT, kT, gains, nq, nk, q_block0, k_block0, scale_q):
    specs = []
    for oc in range(nq):
        specs.append((q_block0 + oc, (lambda ti, oc=oc: qT[oc][:, ti * TT:(ti + 1) * TT]),
                      gains[:, 0:1] if gains is not None else None, scale_q))
    for oc in range(nk):
        specs.append((k_block0 + oc, (lambda ti, oc=oc: kT[ti][oc]),
                      gains[:, 1:2] if gains is not None else None, 1.0))
    return specs


def build_qkv_prog(kind):
    nc, cx = _new_cx()
    x = _ein(nc, "x", [D, TL])
    g = _ein(nc, "g", [128, DC])
    ncol = {"A": 3 * D, "KV": 2 * D, "Q": D}[kind]
    w = _ein(nc, "w", [D, ncol])
    scale = HD ** -0.5
    qT = kT = v = None
    if kind in ("A", "Q"):
        qT = _eout(nc, "qT", [NH, 128, TL], BF16)
    if kind in ("A", "KV"):
        kT = _eout(nc, "kT", [NT, NH, 128, TT], BF16)
        v = _eout(nc, "v", [NT, TT, D], BF16)
    nfm = {"A": 32, "KV": 16, "Q": 16}[kind]
    fm_scr = mk_weight_scratch(cx, "wfm_b", D, 128, nfm)
    v_scr = mk_weight_scratch(cx, "wv_b", D, 512, 4) if kind != "Q" else None
    with contextlib.ExitStack() as es:
        alloc_common(cx, es)
        T = "p"
        emit_wcvt(cx, w, fm_scr, [i * 128 for i in range(nfm)], 128, T + "wfm")
        if v_scr is not None:
            emit_wcvt(cx, w, v_scr, [nfm * 128 + i * 512 for i in range(4)], 512, T + "wv")
        gains = None
        if kind == "A":
            gqk = _ein(nc, "gqk", [128, 2])
            gains = cx.sb.alloc(2, F32)
            cx.sb_base = cx.sb.off
            cx.p.op(SP, lambda e: e.dma_start(out=gains, in_=gqk), writes=((T, "gains0"),), dma=("c", 0))
            cx.p.op(DVE, lambda e: e.tensor_scalar(out=gains[:, 0:1], in0=gains[:, 0:1], scalar1=float(scale), scalar2=None, op0=ALU.mult),
                    reads=((T, "gains0"),), writes=((T, "gains"),))
        if kind == "A":
            specs = qkv_specs(cx, T, qT, kT, gains, 16, 16, 0, 16, 1.0)
        elif kind == "KV":
            specs = qkv_specs(cx, T, None, kT, None, 0, 16, 0, 0, 1.0)
        else:
            specs = qkv_specs(cx, T, qT, None, None, 16, 0, 0, 0, scale)
        cx.p.barrier()
        stage_qkv(cx, x, g, T, fm_scr, specs, v_scr, v, nfm_blocks=nfm)
        cx.p.emit()
    return nc


def build_wo_prog():
    nc, cx = _new_cx()
    x = _ein(nc, "x", [D, TL])
    oT = _ein(nc, "oT", [D, TL], BF16)
    w = _ein(nc, "w", [D, D])
    y = _eout(nc, "y", [D, TL])
    scr = mk_weight_scratch(cx, "wo_b", D, 128, DC)
    with contextlib.ExitStack() as es:
        alloc_common(cx, es)
        emit_wcvt(cx, w, scr, [i * 128 for i in range(DC)], 128, "ow")
        cx.p.barrier()
        stage_wo(cx, x, y, oT, scr, "o", w_tag="ow")
        cx.p.emit()
    return nc


def build_attn_a_prog(lambda_init, debug=False):
    nc, cx = _new_cx()
    dbg = _eout(nc, "dbg", [6, 128, TT]) if debug else None
    qT = _ein(nc, "qT", [NH, 128, TL], BF16)
    Kg = _ein(nc, "Kg", [16, NH, 128, TT], BF16)
    Vg = _ein(nc, "Vg", [16, TT, D], BF16)
    biasM = _ein(nc, "biasM", [NH, 128, 1024])
    sel = _ein(nc, "sel", [128, 17 * 4])
    b31 = _ein(nc, "b31", [128, NH])
    lam = _ein(nc, "lam", [128, 4 * 128])
    gsub = _ein(nc, "gsub", [128, 2])
    oT = _eout(nc, "oT", [D, TL], BF16)
    with contextlib.ExitStack() as es:
        alloc_common(cx, es)
        stage_attn_a(cx, qT, Kg, Vg, biasM, sel, b31, lam, gsub, oT, lambda_init, "a", dbg=dbg)
        cx.p.emit()
    return nc


def build_attn_b_prog():
    nc, cx = _new_cx()
    qT = _ein(nc, "qT", [NH, 128, TL], BF16)
    Kg = _ein(nc, "Kg", [16, NH, 128, TT], BF16)
    Vg = _ein(nc, "Vg", [16, TT, D], BF16)
    m01 = _ein(nc, "m01", [128, 16 * TT], BF16)
    negm = _ein(nc, "negm", [128, 16 * TT], BF16)
    cmat = _ein(nc, "cmat", [128, 3 * 128], BF16)
    oT = _eout(nc, "oT", [D, TL], BF16)
    with contextlib.ExitStack() as es:
        alloc_common(cx, es)
        stage_attn_b(cx, qT, Kg, Vg, m01, negm, cmat, oT, "b")
        cx.p.emit()
    return nc


def build_fused_prog():
    nc, cx = _new_cx()
    p = cx.p
    x = _ein(nc, "x", [D, TL])
    y = _eout(nc, "y", [D, TL])
    W = {}
    for nm, shp in (("ffn_pre_wi", [DEPTH, D, 2 * DFF]), ("ffn_pre_wo", [DEPTH, DFF, D]),
                    ("ffn_post_wi", [DEPTH, D, 2 * DFF]), ("ffn_post_wo", [DEPTH, DFF, D]),
                    ("a_wqkv", [NA, D, 3 * D]), ("a_wo", [NA, D, D]), ("b_wkv", [D, 2 * D]),
                    ("b_wq", [DEPTH - NA, D, D]), ("b_wo", [DEPTH - NA, D, D]),
                    ("ffn_pre_norm", [DEPTH, 128, DC]), ("mix_norm", [DEPTH, 128, DC]), ("ffn_post_norm", [DEPTH, 128, DC]),
                    ("kv_norm", [128, DC]), ("gqk", [NA, 128, 2]), ("lam", [NA, 128, 512]), ("gsub", [NA, 128, 2]),
                    ("biasM", [NH, 128, 1024]), ("sel", [128, 68]), ("b31", [128, NH])):
        W[nm] = _ein(nc, nm, shp)
    for nm, shp in (("m01", [128, 16 * TT]), ("negm", [128, 16 * TT]), ("cmat", [128, 3 * 128])):
        W[nm] = _ein(nc, nm, shp, BF16)
    xs = [nc.dram_tensor("xa", [D, TL], F32).ap(), nc.dram_tensor("xb", [D, TL], F32).ap()]
    qT = nc.dram_tensor("qT", [NH, 128, TL], BF16).ap()
    oT = nc.dram_tensor("oT", [D, TL], BF16).ap()
    kT_loc = [nc.dram_tensor("kT%d" % i, [NT, NH, 128, TT], BF16).ap() for i in range(3)]
    v_loc = [nc.dram_tensor("vl%d" % i, [NT, 4, TT, 512], BF16).ap() for i in range(3)]
    Kg = [nc.dram_tensor("Kg%d" % i, [NT, 4, 4, 512, TT], BF16).ap() for i in range(3)]
    Vg = [nc.dram_tensor("Vg%d" % i, [NT, 4, 4, TT, 512], BF16).ap() for i in range(3)]
    wi_scr = [mk_weight_scratch(cx, "wi_b%d" % i, D, 128, 2 * FC) for i in range(2)]
    wo_scr = [mk_weight_scratch(cx, "wo_b%d" % i, DFF, 128, DC) for i in range(2)]
    fm_scr = mk_weight_scratch(cx, "fm_b", D, 128, 32)
    vw_scr = mk_weight_scratch(cx, "vw_b", D, 512, 4)
    ow_scr = mk_weight_scratch(cx, "ow_b", D, 128, DC)
    groups = [[0, 1, 2, 3], [4, 5, 6, 7]]
    scale = HD ** -0.5

    stages = []
    cur = {"x": x, "i": 0, "nffn": 0, "kv": 0}

    def nxt():
        o = xs[cur["i"] % 2]
        cur["i"] += 1
        return o

    def add_ffn(wi, wo, g, name):
        k = cur["nffn"] % 2
        cur["nffn"] += 1

        def cv(par, bg, wi=wi, wo=wo, k=k):
            emit_wcvt(cx, wi, wi_scr[k], ffn_blocks(), 128, "fwi%d" % k, par, bg)
            emit_wcvt(cx, wo, wo_scr[k], [i * 128 for i in range(DC)], 128, "fwo%d" % k, par, bg)

        def st(k=k, g=g, name=name):
            xin = cur["x"]
            xo = y if name == "last" else nxt()
            stage_ffn(cx, xin, xo, g, wi_scr[k], wo_scr[k], name, wi_tag="fwi%d" % k, wo_tag="fwo%d" % k)
            cur["x"] = xo
        stages.append((cv, st))

    def gather_hook(e_idx, T):
        def hook(ti):
            for hg in range(4):
                rk = tuple((T, "fmout", si, ti) for si in hook.kspecs[hg * 4:hg * 4 + 4])
                p.op(POOL, (lambda e, ti=ti, hg=hg: e.collective_compute(
                    "AllGather", ALU.bypass, groups, [kT_loc[e_idx][ti, hg * 4:hg * 4 + 4].rearrange("h d t -> (h d) t")],
                    [Kg[e_idx][ti, hg].rearrange("r d t -> (r d) t")])),
                     reads=rk, writes=((T, "Kgath", ti, hg),), dma=("cc",), inc=1)
            for eg in range(4):
                rv = tuple((T, "vout", ti, tb, eg) for tb in range(4))
                p.op(POOL, (lambda e, ti=ti, eg=eg: e.collective_compute(
                    "AllGather", ALU.bypass, groups, [v_loc[e_idx][ti, eg]],
                    [Vg[e_idx][ti, eg].rearrange("r t e -> (r t) e")])),
                     reads=rv, writes=((T, "Vgath", ti, eg),), dma=("cc",), inc=1)
        return hook

    def add_qkv(kind, w, g, name, l=0, e_idx=0):
        nfm = {"A": 32, "KV": 16, "Q": 16}[kind]

        def cv(par, bg, w=w, nfm=nfm, kind=kind):
            emit_wcvt(cx, w, fm_scr, [i * 128 for i in range(nfm)], 128, "fm", par, bg)
            if kind != "Q":
                emit_wcvt(cx, w, vw_scr, [nfm * 128 + i * 512 for i in range(4)], 512, "vw", par, bg)

        def st(kind=kind, g=g, name=name, l=l, e_idx=e_idx, nfm=nfm):
            T = name
            gains = None
            if kind == "A":
                cx.sb.reset(cx.sb_base0)
                gains = cx.sb.alloc(2, F32)
                cx.sb_base = cx.sb.off
                p.barrier()
                p.op(SP, lambda e: e.dma_start(out=gains, in_=W["gqk"][l]), writes=((T, "gains0"),), dma=("c", 0))
                p.op(DVE, lambda e: e.tensor_scalar(out=gains[:, 0:1], in0=gains[:, 0:1], scalar1=float(scale), scalar2=None, op0=ALU.mult),
                     reads=((T, "gains0"),), writes=((T, "gains"),))
                specs = qkv_specs(cx, T, qT, kT_loc[e_idx], gains, 16, 16, 0, 16, 1.0)
                kspecs = list(range(16, 32))
            elif kind == "KV":
                specs = qkv_specs(cx, T, None, kT_loc[e_idx], None, 0, 16, 0, 0, 1.0)
                kspecs = list(range(0, 16))
            else:
                specs = qkv_specs(cx, T, qT, None, None, 16, 0, 0, 0, scale)
                kspecs = None
            hook = None
            if kind != "Q":
                hook = gather_hook(e_idx, T)
                hook.kspecs = kspecs
            stage_qkv(cx, cur["x"], g, T, fm_scr, specs, vw_scr if kind != "Q" else None,
                      v_loc[e_idx] if kind != "Q" else None, fm_tag="fm", v_tag="vw", v_blocked=True,
                      post_tile=hook, nfm_blocks=nfm)
            cx.sb_base = cx.sb_base0
        stages.append((cv, st))

    def add_wo(w, name):
        def cv(par, bg, w=w):
            emit_wcvt(cx, w, ow_scr, [i * 128 for i in range(DC)], 128, "ow", par, bg)

        def st(name=name):
            xin = cur["x"]
            xo = nxt()
            stage_wo(cx, xin, xo, oT, ow_scr, name, w_tag="ow")
            cur["x"] = xo
        stages.append((cv, st))

    def add_attn_a(l, e_idx):
        lambda_init = 0.8 - 0.6 * math.exp(-0.3 * l)

        def st(l=l, e_idx=e_idx, lambda_init=lambda_init):
            stage_attn_a(cx, qT, Kg[e_idx], Vg[e_idx], W["biasM"], W["sel"], W["b31"], W["lam"][l], W["gsub"][l], oT,
                         lambda_init, "aa%d" % l, cc=True)
        stages.append((None, st))

    def add_attn_b(i):
        def st(i=i):
            stage_attn_b(cx, qT, Kg[2], Vg[2], W["m01"], W["negm"], W["cmat"], oT, "ab%d" % i, cc=True)
        stages.append((None, st))

    for l in range(DEPTH):
        if l == NA:
            add_qkv("KV", W["b_wkv"], W["kv_norm"], "kvb", e_idx=2)
        add_ffn(W["ffn_pre_wi"][l], W["ffn_pre_wo"][l], W["ffn_pre_norm"][l], "fpre%d" % l)
        if l < NA:
            add_qkv("A", W["a_wqkv"][l], W["mix_norm"][l], "qkva%d" % l, l=l, e_idx=l)
            add_attn_a(l, l)
            add_wo(W["a_wo"][l], "woa%d" % l)
        else:
            i = l - NA
            add_qkv("Q", W["b_wq"][i], W["mix_norm"][l], "qb%d" % i)
            add_attn_b(i)
            add_wo(W["b_wo"][i], "wob%d" % i)
        add_ffn(W["ffn_post_wi"][l], W["ffn_post_wo"][l], W["ffn_post_norm"][l], "last" if l == DEPTH - 1 else "fpost%d" % l)

    with contextlib.ExitStack() as es:
        alloc_common(cx, es)
        cx.sb_base0 = cx.sb_base
        done = set()

        def do_cv(si, bg):
            if si < len(stages) and stages[si][0] is not None and si not in done:
                stages[si][0](len(done) % 2, bg)
                done.add(si)
        do_cv(0, False)
        for si, (cv, st) in enumerate(stages):
            assert cv is None or si in done
            p.barrier()
            if si + 1 < len(stages):
                if stages[si + 1][0] is not None:
                    do_cv(si + 1, True)
                elif si + 2 < len(stages):
                    do_cv(si + 2, True)
            st()
        p.emit()
    return nc


def _tok_idx(r):
    return np.concatenate([np.arange((4 * J + r) * TT, (4 * J + r + 1) * TT) for J in range(NT)])


def _col(vec):
    return np.ascontiguousarray(np.asarray(vec, np.float32).reshape(-1, 128).T)


def _t5_bucket_np(n):
    n = np.maximum(n, 0)
    nf = np.maximum(n, 16).astype(np.float32)
    large = 16 + (np.log(nf / np.float32(16)) / np.float32(math.log(128 / 16)) * np.float32(16)).astype(np.int32)
    large = np.minimum(large, 31)
    return np.where(n < 16, n, large)


def _bias_master(rel_bias):
    s = np.arange(128)[:, None]
    u = np.arange(1024)[None, :]
    n = u - 384 - s
    bk = _t5_bucket_np(n)
    M = np.empty((NH, 128, 1024), np.float32)
    for h in range(NH):
        M[h] = np.where(n >= 0, rel_bias[bk, h], np.float32(NEG))
    return M


def _sel_table(r):
    sel = np.zeros((17, 4), np.float32)
    sel[0] = (0, 1, 0, 0) if r == 0 else (0, 0, 1, 0)
    for rp in range(4):
        for kbi in range(4):
            if rp == r:
                c = (1, 0, 0, 0)
            elif rp == r - 1 and kbi == 3:
                c = (0, 1, 0, 0)
            elif rp < r:
                c = (0, 0, 1, 0)
            else:
                c = (0, 0, 0, NEG)
            sel[1 + rp * 4 + kbi] = c
    return np.ascontiguousarray(np.broadcast_to(sel.reshape(1, -1), (128, 68)))


def _b_masks(r):
    s = np.arange(128)[:, None]
    t = np.arange(TT)[None, :]
    m01 = np.zeros((128, 16, TT), np.float32)
    for rp in range(4):
        for kbi in range(4):
            if rp < r:
                m = np.ones((128, TT), np.float32)
            elif rp > r:
                m = np.zeros((128, TT), np.float32)
            else:
                m = ((kbi * 128 + s) < t).astype(np.float32)
            m01[:, rp * 4 + kbi, :] = m
    negm = (1.0 - m01) * NEG
    bf = ml_dtypes.bfloat16
    return m01.reshape(128, -1).astype(bf), negm.reshape(128, -1).astype(bf)


def _cmat():
    j = np.arange(128)[:, None]
    s = np.arange(128)[None, :]
    negU = -(j >= s).astype(np.float32)
    ident = np.eye(128, dtype=np.float32)
    no = np.zeros((128, 128), np.float32)
    no[:, 0] = -1.0
    return np.concatenate([negU, ident, no], axis=1).astype(ml_dtypes.bfloat16)


_PROGS = {}


def _prog(key, fn, *a):
    if key not in _PROGS:
        _PROGS[key] = fn(*a)
    return _PROGS[key]


def _run(nc, in_maps):
    res = run_bass_kernel_spmd(nc, in_maps, core_ids=list(range(NCORES)))
    return res.results


def _gather_kv(kTs, vs):
    Kgs, Vgs = [], []
    for b in range(NB):
        Kg = np.empty((16, NH, 128, TT), kTs[0].dtype)
        Vg = np.empty((16, TT, D), vs[0].dtype)
        for rp in range(4):
            c = b * 4 + rp
            for J in range(NT):
                Kg[4 * J + rp] = kTs[c][J]
                Vg[4 * J + rp] = vs[c][J]
        Kgs.append(Kg)
        Vgs.append(Vg)
    return Kgs, Vgs


def _cols(mat):
    mat = np.asarray(mat, np.float32)
    return np.ascontiguousarray(mat.reshape(mat.shape[0], -1, 128).transpose(0, 2, 1))


def kernel(x, ffn_pre_norm, ffn_pre_wi, ffn_pre_wo, mix_norm, ffn_post_norm, ffn_post_wi,
           ffn_post_wo, rel_bias, a_wqkv, a_q_norm, a_k_norm, a_lambda, a_subln, a_wo,
           kv_norm, b_wkv, b_wq, b_wo):
    f32 = np.float32
    x = np.asarray(x, f32)
    rel_bias = np.asarray(rel_bias, f32)
    shared = {
        "ffn_pre_wi": np.asarray(ffn_pre_wi, f32), "ffn_pre_wo": np.asarray(ffn_pre_wo, f32),
        "ffn_post_wi": np.asarray(ffn_post_wi, f32), "ffn_post_wo": np.asarray(ffn_post_wo, f32),
        "a_wqkv": np.asarray(a_wqkv, f32), "a_wo": np.asarray(a_wo, f32), "b_wkv": np.asarray(b_wkv, f32),
        "b_wq": np.asarray(b_wq, f32), "b_wo": np.asarray(b_wo, f32),
        "ffn_pre_norm": _cols(ffn_pre_norm), "mix_norm": _cols(mix_norm), "ffn_post_norm": _cols(ffn_post_norm),
        "kv_norm": _col(kv_norm),
        "gqk": np.ascontiguousarray(np.stack([np.asarray(a_q_norm, f32), np.asarray(a_k_norm, f32)], axis=2)),
        "lam": np.ascontiguousarray(np.broadcast_to(np.asarray(a_lambda, f32).reshape(NA, 1, 512), (NA, 128, 512))),
        "gsub": _cols(a_subln),
        "biasM": _bias_master(rel_bias),
        "b31": np.ascontiguousarray(np.broadcast_to(rel_bias[31].reshape(1, NH), (128, NH))),
        "cmat": _cmat(),
    }
    in_maps = []
    for c in range(NCORES):
        b, r = divmod(c, 4)
        m = dict(shared)
        m["x"] = np.ascontiguousarray(x[b][_tok_idx(r)].T)
        m["sel"] = _sel_table(r)
        m["m01"], m["negm"] = _b_masks(r)
        in_maps.append(m)
    nc = _prog("fused", build_fused_prog)
    out = _run(nc, in_maps)
    y = np.empty((NB, SEQ, D), f32)
    for c in range(NCORES):
        b, r = divmod(c, 4)
        y[b][_tok_idx(r)] = out[c]["y"].T
    return y
```

```python
import contextlib
import math
import numpy as np
import ml_dtypes
import concourse.bass as bass
import concourse.mybir as mybir
from concourse.bass_utils import run_bass_kernel_spmd

F32 = mybir.dt.float32
BF16 = mybir.dt.bfloat16
U8 = mybir.dt.uint8
AF = mybir.ActivationFunctionType
ALU = mybir.AluOpType
AX = mybir.AxisListType

D = 2048
DC = D // 128
SEQ = 8192
NB = 2
DEPTH = 4
NA = 2
DFF = 5504
FC = DFF // 128
HD = 128
NH = 16
TT = 512
NT = 4
TL = TT * NT
EPS = 1e-6
NEG = -30000.0
NCORES = 8

PE, ACT, DVE, POOL, SP = "pe", "act", "dve", "pool", "sp"


class Res:
    __slots__ = ("name", "w", "rd", "rdma")

    def __init__(self, name):
        self.name = name
        self.w = None
        self.rd = {}
        self.rdma = []


class Op:
    __slots__ = ("eng", "fn", "deps", "dkey", "signal", "ticket", "idx", "inc")


class Prog:
    def __init__(self, nc):
        self.nc = nc
        self.ops = []
        self.res = {}
        self.last = {}
        self.pending_dma = []
        self.bar = {}
        self.dma_keys = []

    def R(self, key):
        r = self.res.get(key)
        if r is None:
            r = Res(key)
            self.res[key] = r
        return r

    def op(self, eng, fn, reads=(), writes=(), dma=None, strict=False, inc=16, bg=False):
        o = Op()
        o.eng = eng
        o.fn = fn
        o.dkey = dma
        o.signal = dma is not None
        o.ticket = 0
        o.inc = inc
        o.idx = len(self.ops)
        deps = set()
        b = self.bar.pop(eng, None)
        if b:
            deps |= b
        for k in reads:
            r = self.R(k)
            if r.w is not None:
                deps.add(r.w)
        for k in writes:
            r = self.R(k)
            if r.w is not None:
                deps.add(r.w)
            deps.update(r.rd.values())
            deps.update(r.rdma)
        for k in reads:
            r = self.R(k)
            if dma is not None:
                r.rdma.append(o.idx)
            else:
                r.rd[eng] = o.idx
        for k in writes:
            r = self.R(k)
            r.w = o.idx
            r.rd = {}
            r.rdma = []
        o.deps = [d for d in deps if strict or self.ops[d].dkey is not None or self.ops[d].eng != eng]
        self.ops.append(o)
        if dma is None:
            self.last[eng] = o.idx
        if dma is not None:
            if not bg:
                self.pending_dma.append(o.idx)
            if dma not in self.dma_keys:
                self.dma_keys.append(dma)
        return o

    def barrier(self):
        deps = set(self.last.values()) | set(self.pending_dma)
        self.pending_dma = []
        for e in (PE, ACT, DVE, POOL, SP):
            self.bar[e] = set(deps) | self.bar.get(e, set())

    def emit(self):
        nc = self.nc
        ops = self.ops
        for o in ops:
            for d in o.deps:
                ops[d].signal = True
        cnt = {}
        for o in ops:
            if o.dkey is not None:
                k = ("dma", o.dkey)
                cnt[k] = cnt.get(k, 0) + o.inc
                o.ticket = cnt[k]
            elif o.signal:
                k = ("eng", o.eng)
                cnt[k] = cnt.get(k, 0) + 1
                o.ticket = cnt[k]
        final_dma = {k[1]: v for k, v in cnt.items() if k[0] == "dma"}
        with contextlib.ExitStack() as es:
            sems = {}
            for e in (PE, ACT, DVE, POOL, SP):
                sems[("eng", e)] = es.enter_context(nc.semaphore("s_" + e))
            for i, k in enumerate(self.dma_keys):
                sems[("dma", k)] = es.enter_context(nc.semaphore("d%d" % i))
            block = es.enter_context(nc.Block())

            def run(engname):
                def body(eng):
                    waited = {}
                    for o in ops:
                        if o.eng != engname:
                            continue
                        need = {}
                        for d in o.deps:
                            p = ops[d]
                            k = ("dma", p.dkey) if p.dkey is not None else ("eng", p.eng)
                            if p.ticket > need.get(k, 0):
                                need[k] = p.ticket
                        for k, v in need.items():
                            if waited.get(k, 0) < v:
                                eng.wait_ge(sems[k], v)
                                waited[k] = v
                        ins = o.fn(eng)
                        if o.dkey is not None:
                            if o.inc == 16:
                                ins.then_inc(sems[("dma", o.dkey)], 16)
                            else:
                                ins.then_inc(sems[("dma", o.dkey)])
                        elif o.signal:
                            ins.then_inc(sems[("eng", o.eng)], 1)
                    if engname == SP:
                        for k, v in final_dma.items():
                            if waited.get(("dma", k), 0) < v:
                                eng.wait_ge(sems[("dma", k)], v)
                return body

            block.tensor(run(PE))
            block.scalar(run(ACT))
            block.vector(run(DVE))
            block.gpsimd(run(POOL))
            block.sync(run(SP))


class Sbuf:
    def __init__(self, big, cap):
        self.big = big
        self.cap = cap
        self.off = 0

    def reset(self, off=0):
        self.off = off

    def alloc(self, free_elems, dt):
        sz = {F32: 4, BF16: 2}[dt]
        nbytes = (free_elems * sz + 63) // 64 * 64
        assert self.off + nbytes <= self.cap, ("SBUF overflow", self.off, nbytes, self.cap)
        v = self.big[:, self.off:self.off + free_elems * sz].bitcast(dt)
        self.off += nbytes
        return v


class Ctx:
    pass


def mk_weight_scratch(cx, name, K, ncols, nblocks):
    return cx.nc.dram_tensor(name, [nblocks, 128, K // 128, ncols], BF16).ap()


def emit_wcvt(cx, W, scr, blocks, ncols, tag, par=0, bg=False):
    p = cx.p
    Wv = W.rearrange("(kc p) n -> p kc n", p=128)
    for bi, c0 in enumerate(blocks):
        src = Wv[:, :, c0:c0 + ncols]
        dst = scr[bi]
        p.op(POOL, (lambda e, s=src, d=dst: e.dma_start(out=d, in_=s)),
             reads=(), writes=((tag, bi),), dma=("cv", par, bi % 4), bg=bg)


def stage_norm(cx, x_tile, g_tile, hT, ones_f, ssq_ps, sq_tiles, rstd, keys, n_dc=DC, dim=D):
    p = cx.p
    kx, kh, kssq, krstd, ksq, kg = keys
    for dc in range(n_dc):
        sq = sq_tiles[dc % 2]
        p.op(ACT, (lambda e, sq=sq, dc=dc: e.activation(out=sq, in_=x_tile[:, dc, :], func=AF.Square)),
             reads=(kx,), writes=((ksq, dc % 2),))
        p.op(PE, (lambda e, sq=sq, dc=dc: e.matmul(ssq_ps, lhsT=ones_f, rhs=sq, start=(dc == 0), stop=(dc == n_dc - 1))),
             reads=((ksq, dc % 2),), writes=(kssq,))
    p.op(ACT, lambda e: e.activation(out=rstd, in_=ssq_ps, func=AF.Sqrt, scale=1.0 / dim, bias=cx.eps_t),
         reads=(kssq, "eps_t"), writes=(krstd,))
    p.op(DVE, lambda e: e.reciprocal(out=rstd, in_=rstd), reads=(krstd,), writes=(krstd,))
    for dc in range(n_dc):
        eng = DVE
        p.op(eng, (lambda e, dc=dc: e.scalar_tensor_tensor(out=hT[:, dc, :], in0=x_tile[:, dc, :], scalar=g_tile[:, dc:dc + 1],
                                                           in1=rstd, op0=ALU.mult, op1=ALU.mult)),
             reads=(kx, krstd, kg), writes=((kh, dc),))


def stage_ffn(cx, x_in, x_out, g_dram, wi_scr, wo_scr, tag, wi_tag=None, wo_tag=None):
    p, nc, sb = cx.p, cx.nc, cx.sb
    p.barrier()
    sb.reset(cx.sb_base)
    xin_v = x_in.rearrange("(dc p) t -> p dc t", p=128)
    xout_v = x_out.rearrange("(dc p) t -> p dc t", p=128)
    g_tile = sb.alloc(DC, F32)
    xt = [sb.alloc(DC * TT, F32).rearrange("p (a b) -> p a b", a=DC) for _ in range(2)]
    hT = [sb.alloc(DC * TT, BF16).rearrange("p (a b) -> p a b", a=DC) for _ in range(2)]
    aT = sb.alloc(FC * TT, BF16).rearrange("p (a b) -> p a b", a=FC)
    NWI = 3
    wi_t = [sb.alloc(2 * DC * 128, BF16).rearrange("p (g k c) -> p g k c", g=2, k=DC) for _ in range(NWI)]
    wo_t = [sb.alloc(FC * 128, BF16).rearrange("p (k c) -> p k c", k=FC) for _ in range(2)]
    sq_t = [sb.alloc(TT, F32) for _ in range(2)]
    rstd = sb.alloc(TT, F32)
    sg_t = [sb.alloc(TT, F32) for _ in range(2)]
    yo_t = [sb.alloc(TT, F32) for _ in range(2)]
    ps = cx.psum
    gate_ps = [ps[0], ps[1]]
    up_ps = [ps[2], ps[3]]
    y_ps = [ps[4], ps[5]]
    ssq_ps = ps[6]
    T = tag
    wi_tag = wi_tag or (T + "wi")
    wo_tag = wo_tag or (T + "wo")
    allw = tuple((wi_tag, i) for i in range(2 * FC)) + tuple((wo_tag, i) for i in range(DC))
    p.op(SP, lambda e: e.dma_start(out=g_tile, in_=g_dram),
         reads=allw, writes=((T, "g"),), dma=("g",))
    wi_cnt = 0
    wo_cnt = 0
    for ti in range(NT):
        xb = ti % 2
        cs = slice(ti * TT, (ti + 1) * TT)
        p.op(SP, (lambda e, xb=xb, cs=cs: e.dma_start(out=xt[xb], in_=xin_v[:, :, cs])),
             writes=((T, "x", xb),), dma=("x", xb))
        stage_norm(cx, xt[xb], g_tile, hT[xb], cx.ones_f, ssq_ps, sq_t, rstd,
                   ((T, "x", xb), (T, "h", xb), (T, "ssq"), (T, "rstd"), (T, "sq"), (T, "g")))
        hkeys = tuple(((T, "h", xb), dc) for dc in range(DC))
        for fc in range(FC):
            wb = wi_cnt % NWI
            wi_cnt += 1
            p.op(SP, (lambda e, wb=wb, fc=fc: e.dma_start(out=wi_t[wb], in_=wi_scr[2 * fc:2 * fc + 2].rearrange("g p k c -> p g k c"))),
                 reads=((wi_tag, 2 * fc), (wi_tag, 2 * fc + 1)), writes=((T, "wi", wb),), dma=("wi", wb))
            pb = fc % 2
            for dc in range(DC):
                p.op(PE, (lambda e, wb=wb, dc=dc, pb=pb, xb=xb: e.matmul(gate_ps[pb], lhsT=wi_t[wb][:, 0, dc, :], rhs=hT[xb][:, dc, :],
                                                                  start=(dc == 0), stop=(dc == DC - 1))),
                     reads=((T, "wi", wb), ((T, "h", xb), dc)), writes=((T, "gate", pb),))
            for dc in range(DC):
                p.op(PE, (lambda e, wb=wb, dc=dc, pb=pb, xb=xb: e.matmul(up_ps[pb], lhsT=wi_t[wb][:, 1, dc, :], rhs=hT[xb][:, dc, :],
                                                                  start=(dc == 0), stop=(dc == DC - 1))),
                     reads=((T, "wi", wb), ((T, "h", xb), dc)), writes=((T, "up", pb),))
            p.op(ACT, (lambda e, pb=pb: e.activation(out=sg_t[pb], in_=gate_ps[pb], func=AF.Silu)),
                 reads=((T, "gate", pb),), writes=((T, "sg", pb),))
            p.op(DVE, (lambda e, pb=pb, fc=fc: e.tensor_tensor(out=aT[:, fc, :], in0=sg_t[pb], in1=up_ps[pb], op=ALU.mult)),
                 reads=((T, "sg", pb), (T, "up", pb)), writes=((T, "a", fc),))
        for dco in range(DC):
            wb = wo_cnt % 2
            wo_cnt += 1
            p.op(SP, (lambda e, wb=wb, dco=dco: e.dma_start(out=wo_t[wb], in_=wo_scr[dco])),
                 reads=((wo_tag, dco),), writes=((T, "wo", wb),), dma=("wo", wb))
            pb = dco % 2
            for fc in range(FC):
                p.op(PE, (lambda e, wb=wb, fc=fc, pb=pb: e.matmul(y_ps[pb], lhsT=wo_t[wb][:, fc, :], rhs=aT[:, fc, :],
                                                                  start=(fc == 0), stop=(fc == FC - 1))),
                     reads=((T, "wo", wb), (T, "a", fc)), writes=((T, "y", pb),))
            p.op(DVE, (lambda e, pb=pb, dco=dco, xb=xb: e.scalar_tensor_tensor(out=yo_t[pb], in0=y_ps[pb], scalar=0.5, in1=xt[xb][:, dco, :],
                                                                                op0=ALU.mult, op1=ALU.add)),
                 reads=((T, "y", pb), (T, "x", xb)), writes=((T, "yo", pb),))
            p.op(SP, (lambda e, pb=pb, dco=dco, cs=cs: e.dma_start(out=xout_v[:, dco, cs], in_=yo_t[pb])),
                 reads=((T, "yo", pb),), writes=((T, "xout", ti, dco),), dma=("yo", pb))


def load_norm_tile(cx, T, x_v, ti, xt, hT, g_tile, sq_t, rstd, ssq_ps, xb):
    p = cx.p
    cs = slice(ti * TT, (ti + 1) * TT)
    p.op(SP, (lambda e, xb=xb, cs=cs: e.dma_start(out=xt[xb], in_=x_v[:, :, cs])),
         writes=((T, "x", xb),), dma=("x", xb))
    stage_norm(cx, xt[xb], g_tile, hT[xb], cx.ones_f, ssq_ps, sq_t, rstd,
               ((T, "x", xb), (T, "h", xb), (T, "ssq"), (T, "rstd"), (T, "sq"), (T, "g")))


def stage_qkv(cx, x_in, g_dram, tag, fm_scr, fm_specs, v_scr, v_out, fm_tag=None, v_tag=None, v_blocked=False, post_tile=None, nfm_blocks=0):
    p, sb = cx.p, cx.sb
    p.barrier()
    sb.reset(cx.sb_base)
    T = tag
    x_v = x_in.rearrange("(dc p) t -> p dc t", p=128)
    g_tile = sb.alloc(DC, F32)
    xt = [sb.alloc(DC * TT, F32).rearrange("p (a b) -> p a b", a=DC) for _ in range(2)]
    hT = [sb.alloc(DC * TT, BF16).rearrange("p (a b) -> p a b", a=DC) for _ in range(2)]
    NW = 3
    w_t = [sb.alloc(DC * 128, BF16).rearrange("p (k c) -> p k c", k=DC) for _ in range(NW)]
    wv_t = [sb.alloc(DC * 512, BF16).rearrange("p (k c) -> p k c", k=DC) for _ in range(2)]
    sq_t = [sb.alloc(TT, F32) for _ in range(2)]
    rstd = sb.alloc(TT, F32)
    sq2 = [sb.alloc(TT, F32) for _ in range(2)]
    rs2 = [sb.alloc(TT, F32) for _ in range(2)]
    st_t = [sb.alloc(TT, BF16) for _ in range(3)]
    ps = cx.psum
    acc_ps = [ps[0], ps[1], ps[2]]
    ssq2_ps = [ps[3], ps[4]]
    ssq_ps = ps[6]
    fm_tag = fm_tag or (T + "wfm")
    v_tag = v_tag or (T + "wv")
    allw = tuple((fm_tag, i) for i in range(nfm_blocks)) + (tuple((v_tag, i) for i in range(4)) if v_scr is not None else ())
    p.op(SP, lambda e: e.dma_start(out=g_tile, in_=g_dram), reads=allw, writes=((T, "g"),), dma=("g",))
    wcnt = 0
    scnt = 0
    vcnt = 0
    for ti in range(NT):
        xb = ti % 2
        load_norm_tile(cx, T, x_v, ti, xt, hT, g_tile, sq_t, rstd, ssq_ps, xb)
        for si, (bi, dst_fn, gcol, scale) in enumerate(fm_specs):
            wb = wcnt % NW
            ab = wcnt % 3
            wcnt += 1
            p.op(SP, (lambda e, wb=wb, bi=bi: e.dma_start(out=w_t[wb], in_=fm_scr[bi])),
                 reads=((fm_tag, bi),), writes=((T, "w", wb),), dma=("wi", wb))
            for dc in range(DC):
                p.op(PE, (lambda e, wb=wb, dc=dc, ab=ab, xb=xb: e.matmul(acc_ps[ab], lhsT=w_t[wb][:, dc, :], rhs=hT[xb][:, dc, :],
                                                                        start=(dc == 0), stop=(dc == DC - 1))),
                     reads=((T, "w", wb), ((T, "h", xb), dc)), writes=((T, "acc", ab),))
            sb_i = scnt % 3
            scnt += 1
            if gcol is not None:
                qb = si % 2
                p.op(ACT, (lambda e, ab=ab, qb=qb: e.activation(out=sq2[qb], in_=acc_ps[ab], func=AF.Square)),
                     reads=((T, "acc", ab),), writes=((T, "sq2", qb),))
                p.op(PE, (lambda e, qb=qb: e.matmul(ssq2_ps[qb], lhsT=cx.ones_f, rhs=sq2[qb], start=True, stop=True)),
                     reads=((T, "sq2", qb), "ones_f"), writes=((T, "ssq2", qb),))
                p.op(ACT, (lambda e, qb=qb: e.activation(out=rs2[qb], in_=ssq2_ps[qb], func=AF.Sqrt, scale=1.0 / HD, bias=cx.eps_t)),
                     reads=((T, "ssq2", qb), "eps_t"), writes=((T, "rs2", qb),))
                p.op(DVE, (lambda e, qb=qb: e.reciprocal(out=rs2[qb], in_=rs2[qb])), reads=((T, "rs2", qb),), writes=((T, "rs2", qb),))
                p.op(DVE, (lambda e, ab=ab, qb=qb, sb_i=sb_i, gcol=gcol: e.scalar_tensor_tensor(
                    out=st_t[sb_i], in0=acc_ps[ab], scalar=gcol, in1=rs2[qb], op0=ALU.mult, op1=ALU.mult)),
                     reads=((T, "acc", ab), (T, "rs2", qb), (T, "gains")), writes=((T, "st", sb_i),))
            else:
                p.op(ACT, (lambda e, ab=ab, sb_i=sb_i, scale=scale: e.activation(out=st_t[sb_i], in_=acc_ps[ab], func=AF.Copy, scale=float(scale))),
                     reads=((T, "acc", ab),), writes=((T, "st", sb_i),))
            p.op(SP, (lambda e, sb_i=sb_i, dst_fn=dst_fn, ti=ti: e.dma_start(out=dst_fn(ti), in_=st_t[sb_i])),
                 reads=((T, "st", sb_i),), writes=((T, "fmout", si, ti),), dma=("st", sb_i))
        if v_scr is not None:
            for eb in range(4):
                vb = vcnt % 2
                vcnt += 1
                p.op(SP, (lambda e, vb=vb, eb=eb: e.dma_start(out=wv_t[vb], in_=v_scr[eb])),
                     reads=((v_tag, eb),), writes=((T, "wv", vb),), dma=("wo", vb))
                for tb in range(4):
                    ab = wcnt % 3
                    wcnt += 1
                    for dc in range(DC):
                        p.op(PE, (lambda e, vb=vb, dc=dc, ab=ab, xb=xb, tb=tb: e.matmul(
                            acc_ps[ab], lhsT=hT[xb][:, dc, tb * 128:(tb + 1) * 128], rhs=wv_t[vb][:, dc, :],
                            start=(dc == 0), stop=(dc == DC - 1))),
                             reads=((T, "wv", vb), ((T, "h", xb), dc)), writes=((T, "acc", ab),))
                    sb_i = scnt % 3
                    scnt += 1
                    p.op(ACT, (lambda e, ab=ab, sb_i=sb_i: e.activation(out=st_t[sb_i], in_=acc_ps[ab], func=AF.Copy)),
                         reads=((T, "acc", ab),), writes=((T, "st", sb_i),))
                    p.op(SP, (lambda e, sb_i=sb_i, ti=ti, tb=tb, eb=eb: e.dma_start(
                        out=(v_out[ti][eb][tb * 128:(tb + 1) * 128, :] if v_blocked else v_out[ti][tb * 128:(tb + 1) * 128, eb * 512:(eb + 1) * 512]), in_=st_t[sb_i])),
                         reads=((T, "st", sb_i),), writes=((T, "vout", ti, tb, eb),), dma=("st", sb_i))
        if post_tile is not None:
            post_tile(ti)


def stage_wo(cx, x_in, x_out, oT, wo_scr, tag, w_tag=None):
    p, sb = cx.p, cx.sb
    p.barrier()
    sb.reset(cx.sb_base)
    T = tag
    xin_v = x_in.rearrange("(dc p) t -> p dc t", p=128)
    xout_v = x_out.rearrange("(dc p) t -> p dc t", p=128)
    o_v = oT.rearrange("(dc p) t -> p dc t", p=128)
    xt = [sb.alloc(DC * TT, F32).rearrange("p (a b) -> p a b", a=DC) for _ in range(2)]
    ot = [sb.alloc(DC * TT, BF16).rearrange("p (a b) -> p a b", a=DC) for _ in range(2)]
    w_t = [sb.alloc(DC * 128, BF16).rearrange("p (k c) -> p k c", k=DC) for _ in range(3)]
    yo_t = [sb.alloc(TT, F32) for _ in range(2)]
    ps = cx.psum
    y_ps = [ps[0], ps[1]]
    wcnt = 0
    w_tag = w_tag or (T + "w")
    allw = tuple((w_tag, i) for i in range(DC))
    for ti in range(NT):
        xb = ti % 2
        cs = slice(ti * TT, (ti + 1) * TT)
        p.op(SP, (lambda e, xb=xb, cs=cs: e.dma_start(out=xt[xb], in_=xin_v[:, :, cs])),
             reads=(allw if ti == 0 else ()), writes=((T, "x", xb),), dma=("x", xb))
        p.op(SP, (lambda e, xb=xb, cs=cs: e.dma_start(out=ot[xb], in_=o_v[:, :, cs])),
             writes=((T, "o", xb),), dma=("o", xb))
        for dco in range(DC):
            wb = wcnt % 3
            pb = wcnt % 2
            wcnt += 1
            p.op(SP, (lambda e, wb=wb, dco=dco: e.dma_start(out=w_t[wb], in_=wo_scr[dco])),
                 reads=((w_tag, dco),), writes=((T, "w", wb),), dma=("wi", wb))
            for ec in range(DC):
                p.op(PE, (lambda e, wb=wb, ec=ec, pb=pb, xb=xb: e.matmul(y_ps[pb], lhsT=w_t[wb][:, ec, :], rhs=ot[xb][:, ec, :],
                                                                        start=(ec == 0), stop=(ec == DC - 1))),
                     reads=((T, "w", wb), (T, "o", xb)), writes=((T, "y", pb),))
            p.op(DVE, (lambda e, pb=pb, dco=dco, xb=xb: e.tensor_tensor(out=yo_t[pb], in0=y_ps[pb], in1=xt[xb][:, dco, :], op=ALU.add)),
                 reads=((T, "y", pb), (T, "x", xb)), writes=((T, "yo", pb),))
            p.op(SP, (lambda e, pb=pb, dco=dco, cs=cs: e.dma_start(out=xout_v[:, dco, cs], in_=yo_t[pb])),
                 reads=((T, "yo", pb),), writes=((T, "xout", ti, dco),), dma=("yo", pb))


def stage_attn_a(cx, qT, Kg, Vg, biasM, sel_d, b31_d, lam_d, gsub_d, oT, lambda_init, tag, dbg=None, cc=False):
    p, sb = cx.p, cx.sb
    p.barrier()
    sb.reset(cx.sb_base)
    T = tag
    ps = cx.psum
    s_ps = [ps[0], ps[1], ps[2], ps[7]]
    o_ps = [ps[3], ps[4]]
    den_ps = ps[5]
    ssq_ps = ps[6]
    sel_t = sb.alloc(17 * 4, F32).rearrange("p (b c) -> p b c", c=4)
    b31_t = sb.alloc(NH, F32)
    lam_t = sb.alloc(4 * 128, F32).rearrange("p (a b) -> p a b", a=4)
    lprod = sb.alloc(128, F32)
    lsum = sb.alloc(2, F32)
    nlam = sb.alloc(1, F32)
    gsub_t = sb.alloc(2, F32)
    ccol = [sb.alloc(17, F32) for _ in range(2)]
    Kt = [sb.alloc(16 * TT, BF16).rearrange("p (g t) -> p g t", g=16) for _ in range(2)]
    Vt = sb.alloc(64 * 256, BF16).rearrange("p (b e) -> p b e", b=64)
    Qt = sb.alloc(2 * TL, BF16).rearrange("p (m t) -> p m t", m=2)
    Mt = [sb.alloc(1024, F32) for _ in range(2)]
    tmp_t = [sb.alloc(TT, F32) for _ in range(3)]
    pT_t = [sb.alloc(TT, BF16) for _ in range(6)]
    Oev = [sb.alloc(2 * TT, F32).rearrange("p (a b) -> p a b", a=2) for _ in range(2)]
    dev = [sb.alloc(TT, F32) for _ in range(2)]
    o_t = sb.alloc(2 * TT, F32).rearrange("p (a b) -> p a b", a=2)
    u_t = sb.alloc(TT, F32)
    sq_t = [sb.alloc(TT, F32) for _ in range(2)]
    rstd = sb.alloc(TT, F32)
    on_t = [sb.alloc(TT, BF16) for _ in range(2)]
    p.op(SP, lambda e: e.dma_start(out=sel_t, in_=sel_d.rearrange("p (b c) -> p b c", c=4)), writes=((T, "sel"),), dma=("c", 0))
    p.op(SP, lambda e: e.dma_start(out=b31_t, in_=b31_d), writes=((T, "b31"),), dma=("c", 1))
    p.op(SP, lambda e: e.dma_start(out=lam_t, in_=lam_d.rearrange("p (a b) -> p a b", a=4)), writes=((T, "lamv"),), dma=("c", 2))
    p.op(SP, lambda e: e.dma_start(out=gsub_t, in_=gsub_d), writes=((T, "gsub"),), dma=("c", 3))
    lprod2 = sb.alloc(128, F32)
    nlam0 = sb.alloc(1, F32)
    p.op(DVE, lambda e: e.tensor_tensor(out=lprod, in0=lam_t[:, 0, :], in1=lam_t[:, 1, :], op=ALU.mult),
         reads=((T, "lamv"),), writes=((T, "lp0"),))
    p.op(DVE, lambda e: e.reduce_sum(out=lsum[:, 0:1], in_=lprod, axis=AX.X), reads=((T, "lp0"),), writes=((T, "ls0"),), strict=True)
    p.op(DVE, lambda e: e.tensor_tensor(out=lprod2, in0=lam_t[:, 2, :], in1=lam_t[:, 3, :], op=ALU.mult),
         reads=((T, "lamv"),), writes=((T, "lp1"),))
    p.op(DVE, lambda e: e.reduce_sum(out=lsum[:, 1:2], in_=lprod2, axis=AX.X), reads=((T, "lp1"),), writes=((T, "ls1"),), strict=True)
    p.op(ACT, lambda e: e.activation(out=lsum, in_=lsum, func=AF.Exp), reads=((T, "ls0"), (T, "ls1")), writes=((T, "lsum"),), strict=True)
    p.op(DVE, lambda e: e.tensor_tensor(out=nlam0, in0=lsum[:, 1:2], in1=lsum[:, 0:1], op=ALU.subtract),
         reads=((T, "lsum"),), writes=((T, "nlam0"),))
    p.op(DVE, lambda e: e.tensor_scalar(out=nlam, in0=nlam0, scalar1=-float(lambda_init), scalar2=None, op0=ALU.add),
         reads=((T, "nlam0"),), writes=((T, "nlam"),), strict=True)
    p.op(DVE, lambda e: e.tensor_scalar(out=gsub_t, in0=gsub_t, scalar1=float(1.0 - lambda_init), scalar2=None, op0=ALU.mult),
         reads=((T, "gsub"),), writes=((T, "gsub"),), strict=True)
    scnt = 0
    pcnt = 0
    tcnt = 0
    for hd in (range(NH // 2) if dbg is None else [0]):
        for m in range(2):
            h = 2 * hd + m
            if cc:
                for Jl in range(NT):
                    p.op(SP, (lambda e, m=m, h=h, Jl=Jl: e.dma_start(out=Kt[m][:, 4 * Jl:4 * Jl + 4, :],
                                                                    in_=Kg[Jl, h // 4, :, (h % 4) * 128:(h % 4 + 1) * 128, :].rearrange("r d t -> d r t"))),
                         reads=(), writes=((T, "K", m, Jl),), dma=("k", m))
            else:
                p.op(SP, (lambda e, m=m, h=h: e.dma_start(out=Kt[m], in_=Kg[:, h].rearrange("g d t -> d g t"))),
                     reads=((T, "Kg"),), writes=((T, "K", m),), dma=("k", m))
            p.op(SP, (lambda e, m=m, h=h: e.dma_start(out=Qt[:, m, :], in_=qT[h])),
                 reads=((T, "qT"),), writes=((T, "Q", m),), dma=("q", m))
            p.op(SP, (lambda e, m=m, h=h: e.dma_start(out=Mt[m], in_=biasM[h])),
                 reads=(), writes=((T, "M", m),), dma=("m", m))
            p.op(DVE, (lambda e, m=m, h=h: e.scalar_tensor_tensor(out=ccol[m], in0=sel_t[:, :, 2], scalar=b31_t[:, h:h + 1],
                                                                 in1=sel_t[:, :, 3], op0=ALU.mult, op1=ALU.add)),
                 reads=((T, "sel"), (T, "b31")), writes=((T, "ccol", m),))
        if cc:
            for Jl in range(NT):
                p.op(SP, (lambda e, hd=hd, Jl=Jl: e.dma_start(out=Vt[:, 16 * Jl:16 * Jl + 16, :],
                                                            in_=Vg[Jl, hd // 2, :, :, (hd % 2) * 256:(hd % 2 + 1) * 256].rearrange("r (tb p) e -> p (r tb) e", p=128))),
                     reads=(), writes=((T, "V", Jl),), dma=("v",))
        else:
            p.op(SP, (lambda e, hd=hd: e.dma_start(out=Vt, in_=Vg[:, :, hd * 256:(hd + 1) * 256].rearrange("g (tb p) e -> p (g tb) e", p=128))),
                 reads=((T, "Vg"),), writes=((T, "V"),), dma=("v",))
        for J in (range(NT) if dbg is None else [0]):
            qs = slice(J * TT, (J + 1) * TT)
            for m in range(2):
                h = 2 * hd + m
                blocks = []
                for Jp in range(J + 1):
                    for rp in range(4):
                        for kbi in range(4):
                            if Jp == J:
                                si = 1 + rp * 4 + kbi
                            elif Jp == J - 1 and rp == 3 and kbi == 3:
                                si = 0
                            else:
                                si = None
                            blocks.append((4 * Jp + rp, kbi, si))
                nb = len(blocks)
                LA = 2
                pbs = {}

                def emit_s(bi, g, kbi, si):
                    nonlocal scnt, pcnt, tcnt
                    sbk = scnt % 4
                    scnt += 1
                    p.op(PE, (lambda e, sbk=sbk, m=m, g=g, kbi=kbi, qs=qs: e.matmul(
                        s_ps[sbk], lhsT=Kt[m][:, g, kbi * 128:(kbi + 1) * 128], rhs=Qt[:, m, qs], start=True, stop=True)),
                         reads=(((T, "K", m, g // 4) if cc else (T, "K", m)), (T, "Q", m)), writes=((T, "s", sbk),))
                    pb = pcnt % 6
                    pcnt += 1
                    pbs[bi] = pb
                    if si is None:
                        p.op(ACT, (lambda e, sbk=sbk, pb=pb, h=h: e.activation(out=pT_t[pb], in_=s_ps[sbk], func=AF.Exp, bias=b31_t[:, h:h + 1])),
                             reads=((T, "s", sbk), (T, "b31")), writes=((T, "pT", pb),))
                    else:
                        tb = tcnt % 3
                        tcnt += 1
                        ta = Mt[m][:, 384 - kbi * 128:384 - kbi * 128 + TT]
                        tbb = Mt[m][:, 512:1024]

                        def f_sel(e, tb=tb, sbk=sbk, si=si, ta=ta, tbb=tbb):
                            e.scalar_tensor_tensor(out=tmp_t[tb], in0=ta, scalar=sel_t[:, si, 0:1], in1=s_ps[sbk], op0=ALU.mult, op1=ALU.add)
                            return e.scalar_tensor_tensor(out=tmp_t[tb], in0=tbb, scalar=sel_t[:, si, 1:2], in1=tmp_t[tb], op0=ALU.mult, op1=ALU.add)
                        p.op(DVE, f_sel, reads=((T, "s", sbk), (T, "M", m), (T, "sel")), writes=((T, "tmp", tb),))
                        p.op(ACT, (lambda e, tb=tb, pb=pb, m=m, si=si: e.activation(out=pT_t[pb], in_=tmp_t[tb], func=AF.Exp, bias=ccol[m][:, si:si + 1])),
                             reads=((T, "tmp", tb), (T, "ccol", m)), writes=((T, "pT", pb),))

                def emit_pv(bi, g, kbi):
                    pb = pbs[bi]
                    vblk = g * 4 + kbi
                    for ec in range(2):
                        p.op(PE, (lambda e, pb=pb, vblk=vblk, ec=ec, bi=bi, nb=nb: e.matmul(
                            o_ps[ec], lhsT=Vt[:, vblk, ec * 128:(ec + 1) * 128], rhs=pT_t[pb], start=(bi == 0), stop=(bi == nb - 1))),
                             reads=((T, "pT", pb), ((T, "V", g // 4) if cc else (T, "V"))), writes=((T, "ops", ec),))
                    p.op(PE, (lambda e, pb=pb, bi=bi, nb=nb: e.matmul(den_ps, lhsT=cx.ones_b, rhs=pT_t[pb], start=(bi == 0), stop=(bi == nb - 1))),
                         reads=((T, "pT", pb), "ones_b"), writes=((T, "den"),))

                for step in range(nb + LA):
                    if step < nb:
                        emit_s(step, *blocks[step])
                    if step >= LA:
                        bg_, kb2, _ = blocks[step - LA]
                        emit_pv(step - LA, bg_, kb2)
                for ec in range(2):
                    p.op(ACT, (lambda e, m=m, ec=ec: e.activation(out=Oev[m][:, ec, :], in_=o_ps[ec], func=AF.Copy)),
                         reads=((T, "ops", ec),), writes=((T, "Oev", m, ec),))
                p.op(DVE, (lambda e, m=m: e.reciprocal(out=dev[m], in_=den_ps)), reads=((T, "den"),), writes=((T, "dev", m),))
            if dbg is not None:
                p.op(SP, lambda e: e.dma_start(out=dbg[5][:, 0:2], in_=lsum), reads=((T, "lsum"), (T, "nlam")), writes=(("dbg", "l"),), dma=("c", 0))
                p.op(SP, lambda e: e.dma_start(out=dbg[5][:, 2:3], in_=nlam, allow_slow_non_contiguous=True), reads=((T, "nlam"),), writes=(("dbg", "n"),), dma=("c", 0))
                p.op(SP, lambda e: e.dma_start(out=dbg[5][:, 4:6], in_=gsub_t), reads=((T, "gsub"),), writes=(("dbg", "g"),), dma=("c", 0))
                p.barrier()
                for m in range(1):
                    for ec in range(2):
                        p.op(SP, (lambda e, m=m, ec=ec: e.dma_start(out=dbg[m * 3 + ec], in_=Oev[m][:, ec, :])),
                             reads=((T, "Oev", m, ec),), writes=(("dbg", m, ec),), dma=("c", 0))
                    p.op(SP, (lambda e, m=m: e.dma_start(out=dbg[m * 3 + 2], in_=dev[m])),
                         reads=((T, "dev", m),), writes=(("dbg", m, 2),), dma=("c", 0))
                p.barrier()
            p.op(DVE, lambda e: e.tensor_scalar(out=dev[1], in0=dev[1], scalar1=nlam[:, 0:1], scalar2=None, op0=ALU.mult),
                 reads=((T, "dev", 1), (T, "nlam")), writes=((T, "dev", 1),))
            for ec in range(2):
                def f_comb(e, ec=ec):
                    e.tensor_tensor(out=o_t[:, ec, :], in0=Oev[0][:, ec, :], in1=dev[0], op=ALU.mult)
                    e.tensor_tensor(out=u_t, in0=Oev[1][:, ec, :], in1=dev[1], op=ALU.mult)
                    return e.tensor_tensor(out=o_t[:, ec, :], in0=o_t[:, ec, :], in1=u_t, op=ALU.add)
                p.op(DVE, f_comb, reads=((T, "Oev", 0, ec), (T, "Oev", 1, ec), (T, "dev", 0), (T, "dev", 1)), writes=((T, "o", ec),))
                p.op(ACT, (lambda e, ec=ec: e.activation(out=sq_t[ec], in_=o_t[:, ec, :], func=AF.Square)),
                     reads=((T, "o", ec),), writes=((T, "sq", ec),))
                p.op(PE, (lambda e, ec=ec: e.matmul(ssq_ps, lhsT=cx.ones_f, rhs=sq_t[ec], start=(ec == 0), stop=(ec == 1))),
                     reads=((T, "sq", ec), "ones_f"), writes=((T, "ssq"),))
            p.op(ACT, lambda e: e.activation(out=rstd, in_=ssq_ps, func=AF.Sqrt, scale=1.0 / 256.0, bias=cx.eps_t),
                 reads=((T, "ssq"), "eps_t"), writes=((T, "rstd"),))
            p.op(DVE, lambda e: e.reciprocal(out=rstd, in_=rstd), reads=((T, "rstd"),), writes=((T, "rstd"),))
            for ec in range(2):
                p.op(DVE, (lambda e, ec=ec: e.scalar_tensor_tensor(out=on_t[ec], in0=o_t[:, ec, :], scalar=gsub_t[:, ec:ec + 1], in1=rstd,
                                                                   op0=ALU.mult, op1=ALU.mult)),
                     reads=((T, "o", ec), (T, "rstd"), (T, "gsub")), writes=((T, "on", ec),))
                r0 = hd * 256 + ec * 128
                p.op(SP, (lambda e, ec=ec, r0=r0, qs=qs: e.dma_start(out=oT[r0:r0 + 128, qs], in_=on_t[ec])),
                     reads=((T, "on", ec),), writes=((T, "oT", hd, J, ec),), dma=("on", ec))


def stage_attn_b(cx, qT, Kg, Vg, m01_d, negm_d, cmat_d, oT, tag, cc=False):
    p, sb = cx.p, cx.sb
    p.barrier()
    sb.reset(cx.sb_base)
    T = tag
    ps = cx.psum
    z_ps = [ps[0], ps[1], ps[2], ps[3]]
    ob_ps = [ps[4], ps[5]]
    r_ps = ps[6]
    tr_ps = ps[7]
    m01_t = sb.alloc(16 * TT, BF16).rearrange("p (b t) -> p b t", b=16)
    negm_t = sb.alloc(16 * TT, BF16).rearrange("p (b t) -> p b t", b=16)
    cm_t = sb.alloc(3 * 128, BF16).rearrange("p (a b) -> p a b", a=3)
    identf = sb.alloc(128, F32)
    Kt = [sb.alloc(16 * TT, BF16).rearrange("p (g t) -> p g t", g=16) for _ in range(2)]
    Vt = [sb.alloc(64 * 128, BF16).rearrange("p (b e) -> p b e", b=64) for _ in range(2)]
    Qt = [sb.alloc(TL, BF16) for _ in range(2)]
    e_t = [sb.alloc(TT, F32) for _ in range(3)]
    sp_t = [sb.alloc(TT, BF16) for _ in range(5)]
    w_t = [sb.alloc(TT, BF16) for _ in range(3)]
    E_t = [sb.alloc(4, F32) for _ in range(3)]
    O_t = [sb.alloc(4 * 128, F32).rearrange("p (a b) -> p a b", a=4) for _ in range(2)]
    oo_t = [sb.alloc(TT, BF16) for _ in range(2)]
    p.op(SP, lambda e: e.dma_start(out=m01_t, in_=m01_d.rearrange("p (b t) -> p b t", b=16)), writes=((T, "m01"),), dma=("c", 0))
    p.op(SP, lambda e: e.dma_start(out=negm_t, in_=negm_d.rearrange("p (b t) -> p b t", b=16)), writes=((T, "negm"),), dma=("c", 1))
    p.op(SP, lambda e: e.dma_start(out=cm_t, in_=cmat_d.rearrange("p (a b) -> p a b", a=3)), writes=((T, "cm"),), dma=("c", 2))
    p.op(DVE, lambda e: e.tensor_copy(out=identf, in_=cm_t[:, 1, :]), reads=((T, "cm"),), writes=((T, "identf"),))
    negU = cm_t[:, 0, :]
    ident = cm_t[:, 1, :]
    negones = cm_t[:, 2, 0:1]
    zc = 0
    ec_ = 0
    spc = 0
    wc = 0
    Ec = 0
    obc = 0
    hj = 0
    for h in range(NH):
        kb_ = h % 2
        if cc:
            for Jl in range(NT):
                p.op(SP, (lambda e, kb_=kb_, h=h, Jl=Jl: e.dma_start(out=Kt[kb_][:, 4 * Jl:4 * Jl + 4, :],
                                                                    in_=Kg[Jl, h // 4, :, (h % 4) * 128:(h % 4 + 1) * 128, :].rearrange("r d t -> d r t"))),
                     reads=(), writes=((T, "K", kb_, Jl),), dma=("k", kb_))
        else:
            p.op(SP, (lambda e, kb_=kb_, h=h: e.dma_start(out=Kt[kb_], in_=Kg[:, h].rearrange("g d t -> d g t"))),
                 reads=((T, "Kg"),), writes=((T, "K", kb_),), dma=("k", kb_))
        p.op(SP, (lambda e, kb_=kb_, h=h: e.dma_start(out=Qt[kb_], in_=qT[h])),
             reads=((T, "qT"),), writes=((T, "Q", kb_),), dma=("q", kb_))
        if cc:
            for Jl in range(NT):
                p.op(SP, (lambda e, kb_=kb_, h=h, Jl=Jl: e.dma_start(out=Vt[kb_][:, 16 * Jl:16 * Jl + 16, :],
                                                                    in_=Vg[Jl, h // 4, :, :, (h % 4) * 128:(h % 4 + 1) * 128].rearrange("r (tb p) e -> p (r tb) e", p=128))),
                     reads=(), writes=((T, "V", kb_, Jl),), dma=("v", kb_))
        else:
            p.op(SP, (lambda e, kb_=kb_, h=h: e.dma_start(out=Vt[kb_], in_=Vg[:, :, h * 128:(h + 1) * 128].rearrange("g (tb p) e -> p (g tb) e", p=128))),
                 reads=((T, "Vg"),), writes=((T, "V", kb_),), dma=("v", kb_))
        for J in range(NT):
            qs = slice(J * TT, (J + 1) * TT)
            ob_ = hj % 2
            hj += 1
            blocks = []
            for Jp in range(J + 1):
                for rp in range(4):
                    for kbi in range(4):
                        blocks.append((4 * Jp + rp, kbi, (rp * 4 + kbi) if Jp == J else None))
            blocks = blocks[::-1]
            nb = len(blocks)
            st8 = {}
            st9 = {}
            st7 = {}

            def stage1(bi, g, kbi, mi):
                nonlocal zc, ec_, spc
                zb = zc % 4
                zc += 1
                p.op(PE, (lambda e, zb=zb, kb_=kb_, g=g, kbi=kbi, qs=qs: e.matmul(
                    z_ps[zb], lhsT=Kt[kb_][:, g, kbi * 128:(kbi + 1) * 128], rhs=Qt[kb_][:, qs], start=True, stop=True)),
                     reads=(((T, "K", kb_, g // 4) if cc else (T, "K", kb_)), (T, "Q", kb_)), writes=((T, "z", zb),))
                eb = ec_ % 3
                ec_ += 1
                p.op(ACT, (lambda e, zb=zb, eb=eb: e.activation(out=e_t[eb], in_=z_ps[zb], func=AF.Exp)),
                     reads=((T, "z", zb),), writes=((T, "e", eb),))
                sb_ = spc % 5
                spc += 1
                p.op(ACT, (lambda e, eb=eb, sb_=sb_: e.activation(out=sp_t[sb_], in_=e_t[eb], func=AF.Ln, bias=cx.ones_f[:, 0:1])),
                     reads=((T, "e", eb),), writes=((T, "sp", sb_),))
                if mi is not None:
                    p.op(DVE, (lambda e, sb_=sb_, mi=mi: e.tensor_tensor(out=sp_t[sb_], in0=sp_t[sb_], in1=m01_t[:, mi, :], op=ALU.mult)),
                         reads=((T, "sp", sb_), (T, "m01")), writes=((T, "sp", sb_),))
                st8[bi] = (zb, eb, sb_)

            def stage2(bi, g, kbi, mi):
                zb, eb, sb_ = st8.pop(bi)
                last_is_mask = mi is not None
                p.op(PE, (lambda e, zb=zb, sb_=sb_, lm=last_is_mask: e.matmul(z_ps[zb], lhsT=negU, rhs=sp_t[sb_], start=False, stop=(not lm))),
                     reads=((T, "sp", sb_), (T, "cm"), (T, "e", eb)), writes=((T, "z", zb),))
                if mi is not None:
                    p.op(PE, (lambda e, zb=zb, mi=mi: e.matmul(z_ps[zb], lhsT=ident, rhs=negm_t[:, mi, :], start=False, stop=True)),
                         reads=((T, "negm"), (T, "cm")), writes=((T, "z", zb),))
                st7[bi] = (zb, sb_)

            def stage2c(bi, g, kbi, mi):
                nonlocal wc, Ec
                zb, sb_ = st7.pop(bi)
                wb = wc % 3
                wc += 1
                p.op(ACT, (lambda e, zb=zb, wb=wb: e.activation(out=w_t[wb], in_=z_ps[zb], func=AF.Exp)),
                     reads=((T, "z", zb),), writes=((T, "w", wb),))
                Eb = Ec % 3
                if bi > 0:
                    Ec += 1
                    p.op(ACT, (lambda e, Eb=Eb: e.activation(out=E_t[Eb], in_=r_ps[:, 0:4], func=AF.Exp)),
                         reads=((T, "R"),), writes=((T, "E", Eb),))
                st9[bi] = (sb_, wb, Eb)

            def stage2b(bi, g, kbi, mi):
                nonlocal obc
                sb_, wb, Eb = st9.pop(bi)

                def f_r(e, sb_=sb_, bi=bi, nb=nb):
                    ins = None
                    for ts_ in range(4):
                        ins = e.matmul(r_ps[:, ts_:ts_ + 1], lhsT=sp_t[sb_][:, ts_ * 128:(ts_ + 1) * 128], rhs=negones,
                                       start=(bi == 0 and ts_ == 0), stop=(bi == nb - 1), skip_group_check=True)
                    return ins
                p.op(PE, f_r, reads=((T, "sp", sb_), (T, "cm")), writes=((T, "R"),))
                ob2 = obc % 2
                obc += 1
                vblk = g * 4 + kbi

                def f_pv(e, wb=wb, ob2=ob2, vblk=vblk, kb_=kb_):
                    ins = None
                    for ts_ in range(4):
                        ins = e.matmul(ob_ps[ob2][:, ts_ * 128:(ts_ + 1) * 128], lhsT=w_t[wb][:, ts_ * 128:(ts_ + 1) * 128],
                                       rhs=Vt[kb_][:, vblk, :], start=True, stop=True)
                    return ins
                p.op(PE, f_pv, reads=((T, "w", wb), ((T, "V", kb_, g // 4) if cc else (T, "V", kb_))), writes=((T, "ob", ob2),))
                if bi == 0:
                    p.op(DVE, (lambda e, ob2=ob2, ob_=ob_: e.tensor_copy(out=O_t[ob_], in_=ob_ps[ob2].rearrange("p (a b) -> p a b", a=4))),
                         reads=((T, "ob", ob2),), writes=((T, "O", ob_),))
                else:
                    def f_acc(e, ob2=ob2, ob_=ob_, Eb=Eb):
                        ins = None
                        for ts_ in range(4):
                            ins = e.scalar_tensor_tensor(out=O_t[ob_][:, ts_, :], in0=ob_ps[ob2][:, ts_ * 128:(ts_ + 1) * 128],
                                                         scalar=E_t[Eb][:, ts_:ts_ + 1], in1=O_t[ob_][:, ts_, :], op0=ALU.mult, op1=ALU.add)
                        return ins
                    p.op(DVE, f_acc, reads=((T, "ob", ob2), (T, "E", Eb)), writes=((T, "O", ob_),))

            for step in range(nb + 3):
                if step < nb:
                    stage1(step, *blocks[step])
                if 2 <= step < nb + 2:
                    stage2(step - 2, *blocks[step - 2])
                if step >= 3:
                    stage2b(step - 3, *blocks[step - 3])
                if 2 <= step < nb + 2:
                    stage2c(step - 2, *blocks[step - 2])
            oo = hj % 2
            for ts_ in range(4):
                p.op(PE, (lambda e, ts_=ts_, ob_=ob_: e.transpose(tr_ps[:, ts_ * 128:(ts_ + 1) * 128], O_t[ob_][:, ts_, :], identf)),
                     reads=((T, "O", ob_), (T, "identf")), writes=((T, "tr", ts_),))
            p.op(ACT, (lambda e, oo=oo: e.activation(out=oo_t[oo], in_=tr_ps, func=AF.Copy)),
                 reads=tuple((T, "tr", i) for i in range(4)), writes=((T, "oo", oo),))
            p.op(SP, (lambda e, oo=oo, h=h, qs=qs: e.dma_start(out=oT[h * 128:(h + 1) * 128, qs], in_=oo_t[oo])),
                 reads=((T, "oo", oo),), writes=((T, "oT", h, J),), dma=("on", oo))


def alloc_common(cx, es):
    nc = cx.nc
    cap = 200 * 1024
    big = es.enter_context(nc.sbuf_tensor("big", [128, cap], U8))
    cx.sb = Sbuf(big, cap)
    cx.psum = [es.enter_context(nc.psum_tensor("ps%d" % i, [128, 512], F32)) for i in range(8)]
    cx.psum = [t[:] for t in cx.psum]
    cx.ones_f = cx.sb.alloc(128, F32)
    cx.ones_b = cx.sb.alloc(128, BF16)
    cx.ident_b = cx.sb.alloc(128, BF16)
    cx.eps_t = cx.sb.alloc(1, F32)
    cx.sb_base = cx.sb.off
    p = cx.p
    p.op(DVE, lambda e: e.memset(cx.ones_f, 1.0), writes=("ones_f",))
    p.op(DVE, lambda e: e.memset(cx.ones_b, 1.0), writes=("ones_b",))
    p.op(DVE, lambda e: e.memset(cx.eps_t, EPS), writes=("eps_t",))


def _new_cx():
    nc = bass.Bass("TRN2", target_bir_lowering=False)
    cx = Ctx()
    cx.nc = nc
    cx.p = Prog(nc)
    return nc, cx


def _ein(nc, name, shape, dt=F32):
    return nc.dram_tensor(name, list(shape), dt, kind="ExternalInput").ap()


def _eout(nc, name, shape, dt=F32):
    return nc.dram_tensor(name, list(shape), dt, kind="ExternalOutput").ap()


def ffn_blocks():
    blocks = []
    for fc in range(FC):
        blocks += [fc * 128, DFF + fc * 128]
    return blocks


def build_ffn_prog():
    nc, cx = _new_cx()
    x = _ein(nc, "x", [D, TL])
    g = _ein(nc, "g", [128, DC])
    wi = _ein(nc, "wi", [D, 2 * DFF])
    wo = _ein(nc, "wo", [DFF, D])
    y = _eout(nc, "y", [D, TL])
    wi_scr = mk_weight_scratch(cx, "wi_b", D, 128, 2 * FC)
    wo_scr = mk_weight_scratch(cx, "wo_b", DFF, 128, DC)
    with contextlib.ExitStack() as es:
        alloc_common(cx, es)
        emit_wcvt(cx, wi, wi_scr, ffn_blocks(), 128, "fwi")
        emit_wcvt(cx, wo, wo_scr, [i * 128 for i in range(DC)], 128, "fwo")
        cx.p.barrier()
        stage_ffn(cx, x, y, g, wi_scr, wo_scr, "f")
        cx.p.emit()
    return nc


def qkv_specs(cx, tag, qT, kT, gains, nq, nk, q_block0, k_block0, scale_q):
    specs = []
    for oc in range(nq):
        specs.append((q_block0 + oc, (lambda ti, oc=oc: qT[oc][:, ti * TT:(ti + 1) * TT]),
                      gains[:, 0:1] if gains is not None else None, scale_q))
    for oc in range(nk):
        specs.append((k_block0 + oc, (lambda ti, oc=oc: kT[ti][oc]),
                      gains[:, 1:2] if gains is not None else None, 1.0))
    return specs


def build_qkv_prog(kind):
    nc, cx = _new_cx()
    x = _ein(nc, "x", [D, TL])
    g = _ein(nc, "g", [128, DC])
    ncol = {"A": 3 * D, "KV": 2 * D, "Q": D}[kind]
    w = _ein(nc, "w", [D, ncol])
    scale = HD ** -0.5
    qT = kT = v = None
    if kind in ("A", "Q"):
        qT = _eout(nc, "qT", [NH, 128, TL], BF16)
    if kind in ("A", "KV"):
        kT = _eout(nc, "kT", [NT, NH, 128, TT], BF16)
        v = _eout(nc, "v", [NT, TT, D], BF16)
    nfm = {"A": 32, "KV": 16, "Q": 16}[kind]
    fm_scr = mk_weight_scratch(cx, "wfm_b", D, 128, nfm)
    v_scr = mk_weight_scratch(cx, "wv_b", D, 512, 4) if kind != "Q" else None
    with contextlib.ExitStack() as es:
        alloc_common(cx, es)
        T = "p"
        emit_wcvt(cx, w, fm_scr, [i * 128 for i in range(nfm)], 128, T + "wfm")
        if v_scr is not None:
            emit_wcvt(cx, w, v_scr, [nfm * 128 + i * 512 for i in range(4)], 512, T + "wv")
        gains = None
        if kind == "A":
            gqk = _ein(nc, "gqk", [128, 2])
            gains = cx.sb.alloc(2, F32)
            cx.sb_base = cx.sb.off
            cx.p.op(SP, lambda e: e.dma_start(out=gains, in_=gqk), writes=((T, "gains0"),), dma=("c", 0))
            cx.p.op(DVE, lambda e: e.tensor_scalar(out=gains[:, 0:1], in0=gains[:, 0:1], scalar1=float(scale), scalar2=None, op0=ALU.mult),
                    reads=((T, "gains0"),), writes=((T, "gains"),))
        if kind == "A":
            specs = qkv_specs(cx, T, qT, kT, gains, 16, 16, 0, 16, 1.0)
        elif kind == "KV":
            specs = qkv_specs(cx, T, None, kT, None, 0, 16, 0, 0, 1.0)
        else:
            specs = qkv_specs(cx, T, qT, None, None, 16, 0, 0, 0, scale)
        cx.p.barrier()
        stage_qkv(cx, x, g, T, fm_scr, specs, v_scr, v, nfm_blocks=nfm)
        cx.p.emit()
    return nc


def build_wo_prog():
    nc, cx = _new_cx()
    x = _ein(nc, "x", [D, TL])
    oT = _ein(nc, "oT", [D, TL], BF16)
    w = _ein(nc, "w", [D, D])
    y = _eout(nc, "y", [D, TL])
    scr = mk_weight_scratch(cx, "wo_b", D, 128, DC)
    with contextlib.ExitStack() as es:
        alloc_common(cx, es)
        emit_wcvt(cx, w, scr, [i * 128 for i in range(DC)], 128, "ow")
        cx.p.barrier()
        stage_wo(cx, x, y, oT, scr, "o", w_tag="ow")
        cx.p.emit()
    return nc


def build_attn_a_prog(lambda_init, debug=False):
    nc, cx = _new_cx()
    dbg = _eout(nc, "dbg", [6, 128, TT]) if debug else None
    qT = _ein(nc, "qT", [NH, 128, TL], BF16)
    Kg = _ein(nc, "Kg", [16, NH, 128, TT], BF16)
    Vg = _ein(nc, "Vg", [16, TT, D], BF16)
    biasM = _ein(nc, "biasM", [NH, 128, 1024])
    sel = _ein(nc, "sel", [128, 17 * 4])
    b31 = _ein(nc, "b31", [128, NH])
    lam = _ein(nc, "lam", [128, 4 * 128])
    gsub = _ein(nc, "gsub", [128, 2])
    oT = _eout(nc, "oT", [D, TL], BF16)
    with contextlib.ExitStack() as es:
        alloc_common(cx, es)
        stage_attn_a(cx, qT, Kg, Vg, biasM, sel, b31, lam, gsub, oT, lambda_init, "a", dbg=dbg)
        cx.p.emit()
    return nc


def build_attn_b_prog():
    nc, cx = _new_cx()
    qT = _ein(nc, "qT", [NH, 128, TL], BF16)
    Kg = _ein(nc, "Kg", [16, NH, 128, TT], BF16)
    Vg = _ein(nc, "Vg", [16, TT, D], BF16)
    m01 = _ein(nc, "m01", [128, 16 * TT], BF16)
    negm = _ein(nc, "negm", [128, 16 * TT], BF16)
    cmat = _ein(nc, "cmat", [128, 3 * 128], BF16)
    oT = _eout(nc, "oT", [D, TL], BF16)
    with contextlib.ExitStack() as es:
        alloc_common(cx, es)
        stage_attn_b(cx, qT, Kg, Vg, m01, negm, cmat, oT, "b")
        cx.p.emit()
    return nc


def build_fused_prog():
    nc, cx = _new_cx()
    p = cx.p
    x = _ein(nc, "x", [D, TL])
    y = _eout(nc, "y", [D, TL])
    W = {}
    for nm, shp in (("ffn_pre_wi", [DEPTH, D, 2 * DFF]), ("ffn_pre_wo", [DEPTH, DFF, D]),
                    ("ffn_post_wi", [DEPTH, D, 2 * DFF]), ("ffn_post_wo", [DEPTH, DFF, D]),
                    ("a_wqkv", [NA, D, 3 * D]), ("a_wo", [NA, D, D]), ("b_wkv", [D, 2 * D]),
                    ("b_wq", [DEPTH - NA, D, D]), ("b_wo", [DEPTH - NA, D, D]),
                    ("ffn_pre_norm", [DEPTH, 128, DC]), ("mix_norm", [DEPTH, 128, DC]), ("ffn_post_norm", [DEPTH, 128, DC]),
                    ("kv_norm", [128, DC]), ("gqk", [NA, 128, 2]), ("lam", [NA, 128, 512]), ("gsub", [NA, 128, 2]),
                    ("biasM", [NH, 128, 1024]), ("sel", [128, 68]), ("b31", [128, NH])):
        W[nm] = _ein(nc, nm, shp)
    for nm, shp in (("m01", [128, 16 * TT]), ("negm", [128, 16 * TT]), ("cmat", [128, 3 * 128])):
        W[nm] = _ein(nc, nm, shp, BF16)
    xs = [nc.dram_tensor("xa", [D, TL], F32).ap(), nc.dram_tensor("xb", [D, TL], F32).ap()]
    qT = nc.dram_tensor("qT", [NH, 128, TL], BF16).ap()
    oT = nc.dram_tensor("oT", [D, TL], BF16).ap()
    kT_loc = [nc.dram_tensor("kT%d" % i, [NT, NH, 128, TT], BF16).ap() for i in range(3)]
    v_loc = [nc.dram_tensor("vl%d" % i, [NT, 4, TT, 512], BF16).ap() for i in range(3)]
    Kg = [nc.dram_tensor("Kg%d" % i, [NT, 4, 4, 512, TT], BF16).ap() for i in range(3)]
    Vg = [nc.dram_tensor("Vg%d" % i, [NT, 4, 4, TT, 512], BF16).ap() for i in range(3)]
    wi_scr = [mk_weight_scratch(cx, "wi_b%d" % i, D, 128, 2 * FC) for i in range(2)]
    wo_scr = [mk_weight_scratch(cx, "wo_b%d" % i, DFF, 128, DC) for i in range(2)]
    fm_scr = mk_weight_scratch(cx, "fm_b", D, 128, 32)
    vw_scr = mk_weight_scratch(cx, "vw_b", D, 512, 4)
    ow_scr = mk_weight_scratch(cx, "ow_b", D, 128, DC)
    groups = [[0, 1, 2, 3], [4, 5, 6, 7]]
    scale = HD ** -0.5

    stages = []
    cur = {"x": x, "i": 0, "nffn": 0, "kv": 0}

    def nxt():
        o = xs[cur["i"] % 2]
        cur["i"] += 1
        return o

    def add_ffn(wi, wo, g, name):
        k = cur["nffn"] % 2
        cur["nffn"] += 1

        def cv(par, bg, wi=wi, wo=wo, k=k):
            emit_wcvt(cx, wi, wi_scr[k], ffn_blocks(), 128, "fwi%d" % k, par, bg)
            emit_wcvt(cx, wo, wo_scr[k], [i * 128 for i in range(DC)], 128, "fwo%d" % k, par, bg)

        def st(k=k, g=g, name=name):
            xin = cur["x"]
            xo = y if name == "last" else nxt()
            stage_ffn(cx, xin, xo, g, wi_scr[k], wo_scr[k], name, wi_tag="fwi%d" % k, wo_tag="fwo%d" % k)
            cur["x"] = xo
        stages.append((cv, st))

    def gather_hook(e_idx, T):
        def hook(ti):
            for hg in range(4):
                rk = tuple((T, "fmout", si, ti) for si in hook.kspecs[hg * 4:hg * 4 + 4])
                p.op(POOL, (lambda e, ti=ti, hg=hg: e.collective_compute(
                    "AllGather", ALU.bypass, groups, [kT_loc[e_idx][ti, hg * 4:hg * 4 + 4].rearrange("h d t -> (h d) t")],
                    [Kg[e_idx][ti, hg].rearrange("r d t -> (r d) t")])),
                     reads=rk, writes=((T, "Kgath", ti, hg),), dma=("cc",), inc=1)
            for eg in range(4):
                rv = tuple((T, "vout", ti, tb, eg) for tb in range(4))
                p.op(POOL, (lambda e, ti=ti, eg=eg: e.collective_compute(
                    "AllGather", ALU.bypass, groups, [v_loc[e_idx][ti, eg]],
                    [Vg[e_idx][ti, eg].rearrange("r t e -> (r t) e")])),
                     reads=rv, writes=((T, "Vgath", ti, eg),), dma=("cc",), inc=1)
        return hook

    def add_qkv(kind, w, g, name, l=0, e_idx=0):
        nfm = {"A": 32, "KV": 16, "Q": 16}[kind]

        def cv(par, bg, w=w, nfm=nfm, kind=kind):
            emit_wcvt(cx, w, fm_scr, [i * 128 for i in range(nfm)], 128, "fm", par, bg)
            if kind != "Q":
                emit_wcvt(cx, w, vw_scr, [nfm * 128 + i * 512 for i in range(4)], 512, "vw", par, bg)

        def st(kind=kind, g=g, name=name, l=l, e_idx=e_idx, nfm=nfm):
            T = name
            gains = None
            if kind == "A":
                cx.sb.reset(cx.sb_base0)
                gains = cx.sb.alloc(2, F32)
                cx.sb_base = cx.sb.off
                p.barrier()
                p.op(SP, lambda e: e.dma_start(out=gains, in_=W["gqk"][l]), writes=((T, "gains0"),), dma=("c", 0))
                p.op(DVE, lambda e: e.tensor_scalar(out=gains[:, 0:1], in0=gains[:, 0:1], scalar1=float(scale), scalar2=None, op0=ALU.mult),
                     reads=((T, "gains0"),), writes=((T, "gains"),))
                specs = qkv_specs(cx, T, qT, kT_loc[e_idx], gains, 16, 16, 0, 16, 1.0)
                kspecs = list(range(16, 32))
            elif kind == "KV":
                specs = qkv_specs(cx, T, None, kT_loc[e_idx], None, 0, 16, 0, 0, 1.0)
                kspecs = list(range(0, 16))
            else:
                specs = qkv_specs(cx, T, qT, None, None, 16, 0, 0, 0, scale)
                kspecs = None
            hook = None
            if kind != "Q":
                hook = gather_hook(e_idx, T)
                hook.kspecs = kspecs
            stage_qkv(cx, cur["x"], g, T, fm_scr, specs, vw_scr if kind != "Q" else None,
                      v_loc[e_idx] if kind != "Q" else None, fm_tag="fm", v_tag="vw", v_blocked=True,
                      post_tile=hook, nfm_blocks=nfm)
            cx.sb_base = cx.sb_base0
        stages.append((cv, st))

    def add_wo(w, name):
        def cv(par, bg, w=w):
            emit_wcvt(cx, w, ow_scr, [i * 128 for i in range(DC)], 128, "ow", par, bg)

        def st(name=name):
            xin = cur["x"]
            xo = nxt()
            stage_wo(cx, xin, xo, oT, ow_scr, name, w_tag="ow")
            cur["x"] = xo
        stages.append((cv, st))

    def add_attn_a(l, e_idx):
        lambda_init = 0.8 - 0.6 * math.exp(-0.3 * l)

        def st(l=l, e_idx=e_idx, lambda_init=lambda_init):
            stage_attn_a(cx, qT, Kg[e_idx], Vg[e_idx], W["biasM"], W["sel"], W["b31"], W["lam"][l], W["gsub"][l], oT,
                         lambda_init, "aa%d" % l, cc=True)
        stages.append((None, st))

    def add_attn_b(i):
        def st(i=i):
            stage_attn_b(cx, qT, Kg[2], Vg[2], W["m01"], W["negm"], W["cmat"], oT, "ab%d" % i, cc=True)
        stages.append((None, st))

    for l in range(DEPTH):
        if l == NA:
            add_qkv("KV", W["b_wkv"], W["kv_norm"], "kvb", e_idx=2)
        add_ffn(W["ffn_pre_wi"][l], W["ffn_pre_wo"][l], W["ffn_pre_norm"][l], "fpre%d" % l)
        if l < NA:
            add_qkv("A", W["a_wqkv"][l], W["mix_norm"][l], "qkva%d" % l, l=l, e_idx=l)
            add_attn_a(l, l)
            add_wo(W["a_wo"][l], "woa%d" % l)
        else:
            i = l - NA
            add_qkv("Q", W["b_wq"][i], W["mix_norm"][l], "qb%d" % i)
            add_attn_b(i)
            add_wo(W["b_wo"][i], "wob%d" % i)
        add_ffn(W["ffn_post_wi"][l], W["ffn_post_wo"][l], W["ffn_post_norm"][l], "last" if l == DEPTH - 1 else "fpost%d" % l)

    with contextlib.ExitStack() as es:
        alloc_common(cx, es)
        cx.sb_base0 = cx.sb_base
        done = set()

        def do_cv(si, bg):
            if si < len(stages) and stages[si][0] is not None and si not in done:
                stages[si][0](len(done) % 2, bg)
                done.add(si)
        do_cv(0, False)
        for si, (cv, st) in enumerate(stages):
            assert cv is None or si in done
            p.barrier()
            if si + 1 < len(stages):
                if stages[si + 1][0] is not None:
                    do_cv(si + 1, True)
                elif si + 2 < len(stages):
                    do_cv(si + 2, True)
            st()
        p.emit()
    return nc


def _tok_idx(r):
    return np.concatenate([np.arange((4 * J + r) * TT, (4 * J + r + 1) * TT) for J in range(NT)])


def _col(vec):
    return np.ascontiguousarray(np.asarray(vec, np.float32).reshape(-1, 128).T)


def _t5_bucket_np(n):
    n = np.maximum(n, 0)
    nf = np.maximum(n, 16).astype(np.float32)
    large = 16 + (np.log(nf / np.float32(16)) / np.float32(math.log(128 / 16)) * np.float32(16)).astype(np.int32)
    large = np.minimum(large, 31)
    return np.where(n < 16, n, large)


def _bias_master(rel_bias):
    s = np.arange(128)[:, None]
    u = np.arange(1024)[None, :]
    n = u - 384 - s
    bk = _t5_bucket_np(n)
    M = np.empty((NH, 128, 1024), np.float32)
    for h in range(NH):
        M[h] = np.where(n >= 0, rel_bias[bk, h], np.float32(NEG))
    return M


def _sel_table(r):
    sel = np.zeros((17, 4), np.float32)
    sel[0] = (0, 1, 0, 0) if r == 0 else (0, 0, 1, 0)
    for rp in range(4):
        for kbi in range(4):
            if rp == r:
                c = (1, 0, 0, 0)
            elif rp == r - 1 and kbi == 3:
                c = (0, 1, 0, 0)
            elif rp < r:
                c = (0, 0, 1, 0)
            else:
                c = (0, 0, 0, NEG)
            sel[1 + rp * 4 + kbi] = c
    return np.ascontiguousarray(np.broadcast_to(sel.reshape(1, -1), (128, 68)))


def _b_masks(r):
    s = np.arange(128)[:, None]
    t = np.arange(TT)[None, :]
    m01 = np.zeros((128, 16, TT), np.float32)
    for rp in range(4):
        for kbi in range(4):
            if rp < r:
                m = np.ones((128, TT), np.float32)
            elif rp > r:
                m = np.zeros((128, TT), np.float32)
            else:
                m = ((kbi * 128 + s) < t).astype(np.float32)
            m01[:, rp * 4 + kbi, :] = m
    negm = (1.0 - m01) * NEG
    bf = ml_dtypes.bfloat16
    return m01.reshape(128, -1).astype(bf), negm.reshape(128, -1).astype(bf)


def _cmat():
    j = np.arange(128)[:, None]
    s = np.arange(128)[None, :]
    negU = -(j >= s).astype(np.float32)
    ident = np.eye(128, dtype=np.float32)
    no = np.zeros((128, 128), np.float32)
    no[:, 0] = -1.0
    return np.concatenate([negU, ident, no], axis=1).astype(ml_dtypes.bfloat16)


_PROGS = {}


def _prog(key, fn, *a):
    if key not in _PROGS:
        _PROGS[key] = fn(*a)
    return _PROGS[key]


def _run(nc, in_maps):
    res = run_bass_kernel_spmd(nc, in_maps, core_ids=list(range(NCORES)))
    return res.results


def _gather_kv(kTs, vs):
    Kgs, Vgs = [], []
    for b in range(NB):
        Kg = np.empty((16, NH, 128, TT), kTs[0].dtype)
        Vg = np.empty((16, TT, D), vs[0].dtype)
        for rp in range(4):
            c = b * 4 + rp
            for J in range(NT):
                Kg[4 * J + rp] = kTs[c][J]
                Vg[4 * J + rp] = vs[c][J]
        Kgs.append(Kg)
        Vgs.append(Vg)
    return Kgs, Vgs


def _cols(mat):
    mat = np.asarray(mat, np.float32)
    return np.ascontiguousarray(mat.reshape(mat.shape[0], -1, 128).transpose(0, 2, 1))


def kernel(x, ffn_pre_norm, ffn_pre_wi, ffn_pre_wo, mix_norm, ffn_post_norm, ffn_post_wi,
           ffn_post_wo, rel_bias, a_wqkv, a_q_norm, a_k_norm, a_lambda, a_subln, a_wo,
           kv_norm, b_wkv, b_wq, b_wo):
    f32 = np.float32
    x = np.asarray(x, f32)
    rel_bias = np.asarray(rel_bias, f32)
    shared = {
        "ffn_pre_wi": np.asarray(ffn_pre_wi, f32), "ffn_pre_wo": np.asarray(ffn_pre_wo, f32),
        "ffn_post_wi": np.asarray(ffn_post_wi, f32), "ffn_post_wo": np.asarray(ffn_post_wo, f32),
        "a_wqkv": np.asarray(a_wqkv, f32), "a_wo": np.asarray(a_wo, f32), "b_wkv": np.asarray(b_wkv, f32),
        "b_wq": np.asarray(b_wq, f32), "b_wo": np.asarray(b_wo, f32),
        "ffn_pre_norm": _cols(ffn_pre_norm), "mix_norm": _cols(mix_norm), "ffn_post_norm": _cols(ffn_post_norm),
        "kv_norm": _col(kv_norm),
        "gqk": np.ascontiguousarray(np.stack([np.asarray(a_q_norm, f32), np.asarray(a_k_norm, f32)], axis=2)),
        "lam": np.ascontiguousarray(np.broadcast_to(np.asarray(a_lambda, f32).reshape(NA, 1, 512), (NA, 128, 512))),
        "gsub": _cols(a_subln),
        "biasM": _bias_master(rel_bias),
        "b31": np.ascontiguousarray(np.broadcast_to(rel_bias[31].reshape(1, NH), (128, NH))),
        "cmat": _cmat(),
    }
    in_maps = []
    for c in range(NCORES):
        b, r = divmod(c, 4)
        m = dict(shared)
        m["x"] = np.ascontiguousarray(x[b][_tok_idx(r)].T)
        m["sel"] = _sel_table(r)
        m["m01"], m["negm"] = _b_masks(r)
        in_maps.append(m)
    nc = _prog("fused", build_fused_prog)
    out = _run(nc, in_maps)
    y = np.empty((NB, SEQ, D), f32)
    for c in range(NCORES):
        b, r = divmod(c, 4)
        y[b][_tok_idx(r)] = out[c]["y"].T
    return y
```

```python
import contextlib
import math
import numpy as np
import ml_dtypes
import concourse.bass as bass
import concourse.mybir as mybir
from concourse.bass_utils import run_bass_kernel_spmd

F32 = mybir.dt.float32
BF16 = mybir.dt.bfloat16
U8 = mybir.dt.uint8
AF = mybir.ActivationFunctionType
ALU = mybir.AluOpType
AX = mybir.AxisListType

D = 2048
DC = D // 128
SEQ = 8192
NB = 2
DEPTH = 4
NA = 2
DFF = 5504
FC = DFF // 128
HD = 128
NH = 16
TT = 512
NT = 4
TL = TT * NT
EPS = 1e-6
NEG = -30000.0
NCORES = 8

PE, ACT, DVE, POOL, SP = "pe", "act", "dve", "pool", "sp"


class Res:
    __slots__ = ("name", "w", "rd", "rdma")

    def __init__(self, name):
        self.name = name
        self.w = None
        self.rd = {}
        self.rdma = []


class Op:
    __slots__ = ("eng", "fn", "deps", "dkey", "signal", "ticket", "idx", "inc")


class Prog:
    def __init__(self, nc):
        self.nc = nc
        self.ops = []
        self.res = {}
        self.last = {}
        self.pending_dma = []
        self.bar = {}
        self.dma_keys = []

    def R(self, key):
        r = self.res.get(key)
        if r is None:
            r = Res(key)
            self.res[key] = r
        return r

    def op(self, eng, fn, reads=(), writes=(), dma=None, strict=False, inc=16, bg=False):
        o = Op()
        o.eng = eng
        o.fn = fn
        o.dkey = dma
        o.signal = dma is not None
        o.ticket = 0
        o.inc = inc
        o.idx = len(self.ops)
        deps = set()
        b = self.bar.pop(eng, None)
        if b:
            deps |= b
        for k in reads:
            r = self.R(k)
            if r.w is not None:
                deps.add(r.w)
        for k in writes:
            r = self.R(k)
            if r.w is not None:
                deps.add(r.w)
            deps.update(r.rd.values())
            deps.update(r.rdma)
        for k in reads:
            r = self.R(k)
            if dma is not None:
                r.rdma.append(o.idx)
            else:
                r.rd[eng] = o.idx
        for k in writes:
            r = self.R(k)
            r.w = o.idx
            r.rd = {}
            r.rdma = []
        o.deps = [d for d in deps if strict or self.ops[d].dkey is not None or self.ops[d].eng != eng]
        self.ops.append(o)
        if dma is None:
            self.last[eng] = o.idx
        if dma is not None:
            if not bg:
                self.pending_dma.append(o.idx)
            if dma not in self.dma_keys:
                self.dma_keys.append(dma)
        return o

    def barrier(self):
        deps = set(self.last.values()) | set(self.pending_dma)
        self.pending_dma = []
        for e in (PE, ACT, DVE, POOL, SP):
            self.bar[e] = set(deps) | self.bar.get(e, set())

    def emit(self):
        nc = self.nc
        ops = self.ops
        for o in ops:
            for d in o.deps:
                ops[d].signal = True
        cnt = {}
        for o in ops:
            if o.dkey is not None:
                k = ("dma", o.dkey)
                cnt[k] = cnt.get(k, 0) + o.inc
                o.ticket = cnt[k]
            elif o.signal:
                k = ("eng", o.eng)
                cnt[k] = cnt.get(k, 0) + 1
                o.ticket = cnt[k]
        final_dma = {k[1]: v for k, v in cnt.items() if k[0] == "dma"}
        with contextlib.ExitStack() as es:
            sems = {}
            for e in (PE, ACT, DVE, POOL, SP):
                sems[("eng", e)] = es.enter_context(nc.semaphore("s_" + e))
            for i, k in enumerate(self.dma_keys):
                sems[("dma", k)] = es.enter_context(nc.semaphore("d%d" % i))
            block = es.enter_context(nc.Block())

            def run(engname):
                def body(eng):
                    waited = {}
                    for o in ops:
                        if o.eng != engname:
                            continue
                        need = {}
                        for d in o.deps:
                            p = ops[d]
                            k = ("dma", p.dkey) if p.dkey is not None else ("eng", p.eng)
                            if p.ticket > need.get(k, 0):
                                need[k] = p.ticket
                        for k, v in need.items():
                            if waited.get(k, 0) < v:
                                eng.wait_ge(sems[k], v)
                                waited[k] = v
                        ins = o.fn(eng)
                        if o.dkey is not None:
                            if o.inc == 16:
                                ins.then_inc(sems[("dma", o.dkey)], 16)
                            else:
                                ins.then_inc(sems[("dma", o.dkey)])
                        elif o.signal:
                            ins.then_inc(sems[("eng", o.eng)], 1)
                    if engname == SP:
                        for k, v in final_dma.items():
                            if waited.get(("dma", k), 0) < v:
                                eng.wait_ge(sems[("dma", k)], v)
                return body

            block.tensor(run(PE))
            block.scalar(run(ACT))
            block.vector(run(DVE))
            block.gpsimd(run(POOL))
            block.sync(run(SP))


class Sbuf:
    def __init__(self, big, cap):
        self.big = big
        self.cap = cap
        self.off = 0

    def reset(self, off=0):
        self.off = off

    def alloc(self, free_elems, dt):
        sz = {F32: 4, BF16: 2}[dt]
        nbytes = (free_elems * sz + 63) // 64 * 64
        assert self.off + nbytes <= self.cap, ("SBUF overflow", self.off, nbytes, self.cap)
        v = self.big[:, self.off:self.off + free_elems * sz].bitcast(dt)
        self.off += nbytes
        return v


class Ctx:
    pass


def mk_weight_scratch(cx, name, K, ncols, nblocks):
    return cx.nc.dram_tensor(name, [nblocks, 128, K // 128, ncols], BF16).ap()


def emit_wcvt(cx, W, scr, blocks, ncols, tag, par=0, bg=False):
    p = cx.p
    Wv = W.rearrange("(kc p) n -> p kc n", p=128)
    for bi, c0 in enumerate(blocks):
        src = Wv[:, :, c0:c0 + ncols]
        dst = scr[bi]
        p.op(POOL, (lambda e, s=src, d=dst: e.dma_start(out=d, in_=s)),
             reads=(), writes=((tag, bi),), dma=("cv", par, bi % 4), bg=bg)


def stage_norm(cx, x_tile, g_tile, hT, ones_f, ssq_ps, sq_tiles, rstd, keys, n_dc=DC, dim=D):
    p = cx.p
    kx, kh, kssq, krstd, ksq, kg = keys
    for dc in range(n_dc):
        sq = sq_tiles[dc % 2]
        p.op(ACT, (lambda e, sq=sq, dc=dc: e.activation(out=sq, in_=x_tile[:, dc, :], func=AF.Square)),
             reads=(kx,), writes=((ksq, dc % 2),))
        p.op(PE, (lambda e, sq=sq, dc=dc: e.matmul(ssq_ps, lhsT=ones_f, rhs=sq, start=(dc == 0), stop=(dc == n_dc - 1))),
             reads=((ksq, dc % 2),), writes=(kssq,))
    p.op(ACT, lambda e: e.activation(out=rstd, in_=ssq_ps, func=AF.Sqrt, scale=1.0 / dim, bias=cx.eps_t),
         reads=(kssq, "eps_t"), writes=(krstd,))
    p.op(DVE, lambda e: e.reciprocal(out=rstd, in_=rstd), reads=(krstd,), writes=(krstd,))
    for dc in range(n_dc):
        eng = DVE
        p.op(eng, (lambda e, dc=dc: e.scalar_tensor_tensor(out=hT[:, dc, :], in0=x_tile[:, dc, :], scalar=g_tile[:, dc:dc + 1],
                                                           in1=rstd, op0=ALU.mult, op1=ALU.mult)),
             reads=(kx, krstd, kg), writes=((kh, dc),))


def stage_ffn(cx, x_in, x_out, g_dram, wi_scr, wo_scr, tag, wi_tag=None, wo_tag=None):
    p, nc, sb = cx.p, cx.nc, cx.sb
    p.barrier()
    sb.reset(cx.sb_base)
    xin_v = x_in.rearrange("(dc p) t -> p dc t", p=128)
    xout_v = x_out.rearrange("(dc p) t -> p dc t", p=128)
    g_tile = sb.alloc(DC, F32)
    xt = [sb.alloc(DC * TT, F32).rearrange("p (a b) -> p a b", a=DC) for _ in range(2)]
    hT = [sb.alloc(DC * TT, BF16).rearrange("p (a b) -> p a b", a=DC) for _ in range(2)]
    aT = sb.alloc(FC * TT, BF16).rearrange("p (a b) -> p a b", a=FC)
    NWI = 3
    wi_t = [sb.alloc(2 * DC * 128, BF16).rearrange("p (g k c) -> p g k c", g=2, k=DC) for _ in range(NWI)]
    wo_t = [sb.alloc(FC * 128, BF16).rearrange("p (k c) -> p k c", k=FC) for _ in range(2)]
    sq_t = [sb.alloc(TT, F32) for _ in range(2)]
    rstd = sb.alloc(TT, F32)
    sg_t = [sb.alloc(TT, F32) for _ in range(2)]
    yo_t = [sb.alloc(TT, F32) for _ in range(2)]
    ps = cx.psum
    gate_ps = [ps[0], ps[1]]
    up_ps = [ps[2], ps[3]]
    y_ps = [ps[4], ps[5]]
    ssq_ps = ps[6]
    T = tag
    wi_tag = wi_tag or (T + "wi")
    wo_tag = wo_tag or (T + "wo")
    allw = tuple((wi_tag, i) for i in range(2 * FC)) + tuple((wo_tag, i) for i in range(DC))
    p.op(SP, lambda e: e.dma_start(out=g_tile, in_=g_dram),
         reads=allw, writes=((T, "g"),), dma=("g",))
    wi_cnt = 0
    wo_cnt = 0
    for ti in range(NT):
        xb = ti % 2
        cs = slice(ti * TT, (ti + 1) * TT)
        p.op(SP, (lambda e, xb=xb, cs=cs: e.dma_start(out=xt[xb], in_=xin_v[:, :, cs])),
             writes=((T, "x", xb),), dma=("x", xb))
        stage_norm(cx, xt[xb], g_tile, hT[xb], cx.ones_f, ssq_ps, sq_t, rstd,
                   ((T, "x", xb), (T, "h", xb), (T, "ssq"), (T, "rstd"), (T, "sq"), (T, "g")))
        hkeys = tuple(((T, "h", xb), dc) for dc in range(DC))
        for fc in range(FC):
            wb = wi_cnt % NWI
            wi_cnt += 1
            p.op(SP, (lambda e, wb=wb, fc=fc: e.dma_start(out=wi_t[wb], in_=wi_scr[2 * fc:2 * fc + 2].rearrange("g p k c -> p g k c"))),
                 reads=((wi_tag, 2 * fc), (wi_tag, 2 * fc + 1)), writes=((T, "wi", wb),), dma=("wi", wb))
            pb = fc % 2
            for dc in range(DC):
                p.op(PE, (lambda e, wb=wb, dc=dc, pb=pb, xb=xb: e.matmul(gate_ps[pb], lhsT=wi_t[wb][:, 0, dc, :], rhs=hT[xb][:, dc, :],
                                                                  start=(dc == 0), stop=(dc == DC - 1))),
                     reads=((T, "wi", wb), ((T, "h", xb), dc)), writes=((T, "gate", pb),))
            for dc in range(DC):
                p.op(PE, (lambda e, wb=wb, dc=dc, pb=pb, xb=xb: e.matmul(up_ps[pb], lhsT=wi_t[wb][:, 1, dc, :], rhs=hT[xb][:, dc, :],
                                                                  start=(dc == 0), stop=(dc == DC - 1))),
                     reads=((T, "wi", wb), ((T, "h", xb), dc)), writes=((T, "up", pb),))
            p.op(ACT, (lambda e, pb=pb: e.activation(out=sg_t[pb], in_=gate_ps[pb], func=AF.Silu)),
                 reads=((T, "gate", pb),), writes=((T, "sg", pb),))
            p.op(DVE, (lambda e, pb=pb, fc=fc: e.tensor_tensor(out=aT[:, fc, :], in0=sg_t[pb], in1=up_ps[pb], op=ALU.mult)),
                 reads=((T, "sg", pb), (T, "up", pb)), writes=((T, "a", fc),))
        for dco in range(DC):
            wb = wo_cnt % 2
            wo_cnt += 1
            p.op(SP, (lambda e, wb=wb, dco=dco: e.dma_start(out=wo_t[wb], in_=wo_scr[dco])),
                 reads=((wo_tag, dco),), writes=((T, "wo", wb),), dma=("wo", wb))
            pb = dco % 2
            for fc in range(FC):
                p.op(PE, (lambda e, wb=wb, fc=fc, pb=pb: e.matmul(y_ps[pb], lhsT=wo_t[wb][:, fc, :], rhs=aT[:, fc, :],
                                                                  start=(fc == 0), stop=(fc == FC - 1))),
                     reads=((T, "wo", wb), (T, "a", fc)), writes=((T, "y", pb),))
            p.op(DVE, (lambda e, pb=pb, dco=dco, xb=xb: e.scalar_tensor_tensor(out=yo_t[pb], in0=y_ps[pb], scalar=0.5, in1=xt[xb][:, dco, :],
                                                                                op0=ALU.mult, op1=ALU.add)),
                 reads=((T, "y", pb), (T, "x", xb)), writes=((T, "yo", pb),))
            p.op(SP, (lambda e, pb=pb, dco=dco, cs=cs: e.dma_start(out=xout_v[:, dco, cs], in_=yo_t[pb])),
                 reads=((T, "yo", pb),), writes=((T, "xout", ti, dco),), dma=("yo", pb))


def stage_ffn2(cx, x_in, x_out, g_dram, wi_scr, wo_scr, tag, wi_tag=None, wo_tag=None):
    p, nc, sb = cx.p, cx.nc, cx.sb
    p.barrier()
    sb.reset(cx.sb_base)
    T = tag
    TB = 1024
    NP = 256
    xin_v = x_in.rearrange("(dc p) t -> p dc t", p=128)
    xout_v = x_out.rearrange("(dc p) t -> p dc t", p=128)
    g_tile = sb.alloc(DC, F32)
    xt = sb.alloc(DC * NP, F32).rearrange("p (a b) -> p a b", a=DC)
    hT = sb.alloc(DC * TB, BF16).rearrange("p (a b) -> p a b", a=DC)
    aT = sb.alloc(FC * TB, BF16).rearrange("p (a b) -> p a b", a=FC)
    NWI = 3
    wi_t = [sb.alloc(2 * DC * 128, BF16).rearrange("p (g k c) -> p g k c", g=2, k=DC) for _ in range(NWI)]
    wo_t = [sb.alloc(FC * 128, BF16).rearrange("p (k c) -> p k c", k=FC) for _ in range(2)]
    sq_t = [sb.alloc(NP, F32) for _ in range(2)]
    rstd = sb.alloc(NP, F32)
    sg_t = [sb.alloc(TT, F32) for _ in range(2)]
    yo_t = [sb.alloc(TT, F32) for _ in range(2)]
    xr_t = [sb.alloc(TT, F32) for _ in range(2)]
    ps = cx.psum
    gate_ps = [ps[0], ps[1]]
    up_ps = [ps[2], ps[3]]
    y_ps = [ps[4], ps[5]]
    ssq_ps = ps[6]
    wi_tag = wi_tag or (T + "wi")
    wo_tag = wo_tag or (T + "wo")
    allw = tuple((wi_tag, i) for i in range(2 * FC)) + tuple((wo_tag, i) for i in range(DC))
    p.op(SP, lambda e: e.dma_start(out=g_tile, in_=g_dram), reads=allw, writes=((T, "g"),), dma=("g",))
    wi_cnt = 0
    wo_cnt = 0
    pcnt = 0
    ycnt = 0
    for ti in range(TL // TB):
        t0 = ti * TB
        for pi in range(TB // NP):
            c0 = pi * NP
            p.op(SP, (lambda e, c0=c0, t0=t0: e.dma_start(out=xt, in_=xin_v[:, :, t0 + c0:t0 + c0 + NP])),
                 writes=((T, "x"),), dma=("x", 0))
            for dc in range(DC):
                sq = sq_t[dc % 2]
                p.op(ACT, (lambda e, sq=sq, dc=dc: e.activation(out=sq, in_=xt[:, dc, :], func=AF.Square)),
                     reads=((T, "x"),), writes=((T, "sq", dc % 2),))
                p.op(PE, (lambda e, sq=sq, dc=dc: e.matmul(ssq_ps[:, 0:NP], lhsT=cx.ones_f, rhs=sq, start=(dc == 0), stop=(dc == DC - 1))),
                     reads=((T, "sq", dc % 2), "ones_f"), writes=((T, "ssq"),))
            p.op(ACT, lambda e: e.activation(out=rstd, in_=ssq_ps[:, 0:NP], func=AF.Sqrt, scale=1.0 / D, bias=cx.eps_t),
                 reads=((T, "ssq"), "eps_t"), writes=((T, "rstd"),))
            p.op(DVE, lambda e: e.reciprocal(out=rstd, in_=rstd), reads=((T, "rstd"),), writes=((T, "rstd"),))
            for dc in range(DC):
                p.op(DVE, (lambda e, dc=dc, c0=c0: e.scalar_tensor_tensor(out=hT[:, dc, c0:c0 + NP], in0=xt[:, dc, :], scalar=g_tile[:, dc:dc + 1],
                                                                         in1=rstd, op0=ALU.mult, op1=ALU.mult)),
                     reads=((T, "x"), (T, "rstd"), (T, "g")), writes=((T, "h", pi // 2),))
        for fc in range(FC):
            wb = wi_cnt % NWI
            wi_cnt += 1
            p.op(SP, (lambda e, wb=wb, fc=fc: e.dma_start(out=wi_t[wb], in_=wi_scr[2 * fc:2 * fc + 2].rearrange("g p k c -> p g k c"))),
                 reads=((wi_tag, 2 * fc), (wi_tag, 2 * fc + 1)), writes=((T, "wi", wb),), dma=("wi", wb))
            for sub in range(2):
                pb = pcnt % 2
                pcnt += 1
                hs = slice(sub * TT, (sub + 1) * TT)
                for dc in range(DC):
                    p.op(PE, (lambda e, wb=wb, dc=dc, pb=pb, hs=hs: e.matmul(gate_ps[pb], lhsT=wi_t[wb][:, 0, dc, :], rhs=hT[:, dc, hs],
                                                                            start=(dc == 0), stop=(dc == DC - 1))),
                         reads=((T, "wi", wb), (T, "h", sub)), writes=((T, "gate", pb),))
                for dc in range(DC):
                    p.op(PE, (lambda e, wb=wb, dc=dc, pb=pb, hs=hs: e.matmul(up_ps[pb], lhsT=wi_t[wb][:, 1, dc, :], rhs=hT[:, dc, hs],
                                                                            start=(dc == 0), stop=(dc == DC - 1))),
                         reads=((T, "wi", wb), (T, "h", sub)), writes=((T, "up", pb),))
                p.op(ACT, (lambda e, pb=pb: e.activation(out=sg_t[pb], in_=gate_ps[pb], func=AF.Silu)),
                     reads=((T, "gate", pb),), writes=((T, "sg", pb),))
                p.op(DVE, (lambda e, pb=pb, fc=fc, hs=hs: e.tensor_tensor(out=aT[:, fc, hs], in0=sg_t[pb], in1=up_ps[pb], op=ALU.mult)),
                     reads=((T, "sg", pb), (T, "up", pb)), writes=((T, "a", fc, sub),))
        for dco in range(DC):
            wb = wo_cnt % 2
            wo_cnt += 1
            p.op(SP, (lambda e, wb=wb, dco=dco: e.dma_start(out=wo_t[wb], in_=wo_scr[dco])),
                 reads=((wo_tag, dco),), writes=((T, "wo", wb),), dma=("wo", wb))
            for sub in range(2):
                pb = ycnt % 2
                ycnt += 1
                hs = slice(sub * TT, (sub + 1) * TT)
                cs = slice(t0 + sub * TT, t0 + (sub + 1) * TT)
                p.op(SP, (lambda e, pb=pb, dco=dco, cs=cs: e.dma_start(out=xr_t[pb], in_=xin_v[:, dco, cs])),
                     writes=((T, "xr", pb),), dma=("xr", pb))
                for fc in range(FC):
                    p.op(PE, (lambda e, wb=wb, fc=fc, pb=pb, hs=hs: e.matmul(y_ps[pb], lhsT=wo_t[wb][:, fc, :], rhs=aT[:, fc, hs],
                                                                            start=(fc == 0), stop=(fc == FC - 1))),
                         reads=((T, "wo", wb), (T, "a", fc, sub)), writes=((T, "y", pb),))
                p.op(DVE, (lambda e, pb=pb: e.scalar_tensor_tensor(out=yo_t[pb], in0=y_ps[pb], scalar=0.5, in1=xr_t[pb],
                                                                   op0=ALU.mult, op1=ALU.add)),
                     reads=((T, "y", pb), (T, "xr", pb)), writes=((T, "yo", pb),))
                p.op(SP, (lambda e, pb=pb, dco=dco, cs=cs: e.dma_start(out=xout_v[:, dco, cs], in_=yo_t[pb])),
                     reads=((T, "yo", pb),), writes=((T, "xout", ti, dco, sub),), dma=("yo", pb))


def load_norm_tile(cx, T, x_v, ti, xt, hT, g_tile, sq_t, rstd, ssq_ps, xb):
    p = cx.p
    cs = slice(ti * TT, (ti + 1) * TT)
    p.op(SP, (lambda e, xb=xb, cs=cs: e.dma_start(out=xt[xb], in_=x_v[:, :, cs])),
         writes=((T, "x", xb),), dma=("x", xb))
    stage_norm(cx, xt[xb], g_tile, hT[xb], cx.ones_f, ssq_ps, sq_t, rstd,
               ((T, "x", xb), (T, "h", xb), (T, "ssq"), (T, "rstd"), (T, "sq"), (T, "g")))


def stage_qkv(cx, x_in, g_dram, tag, fm_scr, fm_specs, v_scr, v_out, fm_tag=None, v_tag=None, v_blocked=False, post_tile=None, nfm_blocks=0):
    p, sb = cx.p, cx.sb
    p.barrier()
    sb.reset(cx.sb_base)
    T = tag
    x_v = x_in.rearrange("(dc p) t -> p dc t", p=128)
    g_tile = sb.alloc(DC, F32)
    xt = [sb.alloc(DC * TT, F32).rearrange("p (a b) -> p a b", a=DC) for _ in range(2)]
    hT = [sb.alloc(DC * TT, BF16).rearrange("p (a b) -> p a b", a=DC) for _ in range(2)]
    NW = 3
    w_t = [sb.alloc(DC * 128, BF16).rearrange("p (k c) -> p k c", k=DC) for _ in range(NW)]
    wv_t = [sb.alloc(DC * 512, BF16).rearrange("p (k c) -> p k c", k=DC) for _ in range(2)]
    sq_t = [sb.alloc(TT, F32) for _ in range(2)]
    rstd = sb.alloc(TT, F32)
    sq2 = [sb.alloc(TT, F32) for _ in range(2)]
    rs2 = [sb.alloc(TT, F32) for _ in range(2)]
    st_t = [sb.alloc(TT, BF16) for _ in range(3)]
    ps = cx.psum
    acc_ps = [ps[0], ps[1], ps[2]]
    ssq2_ps = [ps[3], ps[4]]
    ssq_ps = ps[6]
    fm_tag = fm_tag or (T + "wfm")
    v_tag = v_tag or (T + "wv")
    allw = tuple((fm_tag, i) for i in range(nfm_blocks)) + (tuple((v_tag, i) for i in range(4)) if v_scr is not None else ())
    p.op(SP, lambda e: e.dma_start(out=g_tile, in_=g_dram), reads=allw, writes=((T, "g"),), dma=("g",))
    wcnt = 0
    scnt = 0
    vcnt = 0
    for ti in range(NT):
        xb = ti % 2
        load_norm_tile(cx, T, x_v, ti, xt, hT, g_tile, sq_t, rstd, ssq_ps, xb)
        for si, (bi, dst_fn, gcol, scale) in enumerate(fm_specs):
            wb = wcnt % NW
            ab = wcnt % 3
            wcnt += 1
            p.op(SP, (lambda e, wb=wb, bi=bi: e.dma_start(out=w_t[wb], in_=fm_scr[bi])),
                 reads=((fm_tag, bi),), writes=((T, "w", wb),), dma=("wi", wb))
            for dc in range(DC):
                p.op(PE, (lambda e, wb=wb, dc=dc, ab=ab, xb=xb: e.matmul(acc_ps[ab], lhsT=w_t[wb][:, dc, :], rhs=hT[xb][:, dc, :],
                                                                        start=(dc == 0), stop=(dc == DC - 1))),
                     reads=((T, "w", wb), ((T, "h", xb), dc)), writes=((T, "acc", ab),))
            sb_i = scnt % 3
            scnt += 1
            if gcol is not None:
                qb = si % 2
                p.op(ACT, (lambda e, ab=ab, qb=qb: e.activation(out=sq2[qb], in_=acc_ps[ab], func=AF.Square)),
                     reads=((T, "acc", ab),), writes=((T, "sq2", qb),))
                p.op(PE, (lambda e, qb=qb: e.matmul(ssq2_ps[qb], lhsT=cx.ones_f, rhs=sq2[qb], start=True, stop=True)),
                     reads=((T, "sq2", qb), "ones_f"), writes=((T, "ssq2", qb),))
                p.op(ACT, (lambda e, qb=qb: e.activation(out=rs2[qb], in_=ssq2_ps[qb], func=AF.Sqrt, scale=1.0 / HD, bias=cx.eps_t)),
                     reads=((T, "ssq2", qb), "eps_t"), writes=((T, "rs2", qb),))
                p.op(DVE, (lambda e, qb=qb: e.reciprocal(out=rs2[qb], in_=rs2[qb])), reads=((T, "rs2", qb),), writes=((T, "rs2", qb),))
                p.op(DVE, (lambda e, ab=ab, qb=qb, sb_i=sb_i, gcol=gcol: e.scalar_tensor_tensor(
                    out=st_t[sb_i], in0=acc_ps[ab], scalar=gcol, in1=rs2[qb], op0=ALU.mult, op1=ALU.mult)),
                     reads=((T, "acc", ab), (T, "rs2", qb), (T, "gains")), writes=((T, "st", sb_i),))
            else:
                p.op(ACT, (lambda e, ab=ab, sb_i=sb_i, scale=scale: e.activation(out=st_t[sb_i], in_=acc_ps[ab], func=AF.Copy, scale=float(scale))),
                     reads=((T, "acc", ab),), writes=((T, "st", sb_i),))
            p.op(SP, (lambda e, sb_i=sb_i, dst_fn=dst_fn, ti=ti: e.dma_start(out=dst_fn(ti), in_=st_t[sb_i])),
                 reads=((T, "st", sb_i),), writes=((T, "fmout", si, ti),), dma=("st", sb_i))
        if v_scr is not None:
            for eb in range(4):
                vb = vcnt % 2
                vcnt += 1
                p.op(SP, (lambda e, vb=vb, eb=eb: e.dma_start(out=wv_t[vb], in_=v_scr[eb])),
                     reads=((v_tag, eb),), writes=((T, "wv", vb),), dma=("wo", vb))
                for tb in range(4):
                    ab = wcnt % 3
                    wcnt += 1
                    for dc in range(DC):
                        p.op(PE, (lambda e, vb=vb, dc=dc, ab=ab, xb=xb, tb=tb: e.matmul(
                            acc_ps[ab], lhsT=hT[xb][:, dc, tb * 128:(tb + 1) * 128], rhs=wv_t[vb][:, dc, :],
                            start=(dc == 0), stop=(dc == DC - 1))),
                             reads=((T, "wv", vb), ((T, "h", xb), dc)), writes=((T, "acc", ab),))
                    sb_i = scnt % 3
                    scnt += 1
                    p.op(ACT, (lambda e, ab=ab, sb_i=sb_i: e.activation(out=st_t[sb_i], in_=acc_ps[ab], func=AF.Copy)),
                         reads=((T, "acc", ab),), writes=((T, "st", sb_i),))
                    p.op(SP, (lambda e, sb_i=sb_i, ti=ti, tb=tb, eb=eb: e.dma_start(
                        out=(v_out[ti][eb][tb * 128:(tb + 1) * 128, :] if v_blocked else v_out[ti][tb * 128:(tb + 1) * 128, eb * 512:(eb + 1) * 512]), in_=st_t[sb_i])),
                         reads=((T, "st", sb_i),), writes=((T, "vout", ti, tb, eb),), dma=("st", sb_i))
        if post_tile is not None:
            post_tile(ti)


def stage_wo(cx, x_in, x_out, oT, wo_scr, tag, w_tag=None):
    p, sb = cx.p, cx.sb
    p.barrier()
    sb.reset(cx.sb_base)
    T = tag
    xin_v = x_in.rearrange("(dc p) t -> p dc t", p=128)
    xout_v = x_out.rearrange("(dc p) t -> p dc t", p=128)
    o_v = oT.rearrange("(dc p) t -> p dc t", p=128)
    xt = [sb.alloc(DC * TT, F32).rearrange("p (a b) -> p a b", a=DC) for _ in range(2)]
    ot = [sb.alloc(DC * TT, BF16).rearrange("p (a b) -> p a b", a=DC) for _ in range(2)]
    w_t = [sb.alloc(DC * 128, BF16).rearrange("p (k c) -> p k c", k=DC) for _ in range(3)]
    yo_t = [sb.alloc(TT, F32) for _ in range(2)]
    ps = cx.psum
    y_ps = [ps[0], ps[1]]
    wcnt = 0
    w_tag = w_tag or (T + "w")
    allw = tuple((w_tag, i) for i in range(DC))
    for ti in range(NT):
        xb = ti % 2
        cs = slice(ti * TT, (ti + 1) * TT)
        p.op(SP, (lambda e, xb=xb, cs=cs: e.dma_start(out=xt[xb], in_=xin_v[:, :, cs])),
             reads=(allw if ti == 0 else ()), writes=((T, "x", xb),), dma=("x", xb))
        p.op(SP, (lambda e, xb=xb, cs=cs: e.dma_start(out=ot[xb], in_=o_v[:, :, cs])),
             writes=((T, "o", xb),), dma=("o", xb))
        for dco in range(DC):
            wb = wcnt % 3
            pb = wcnt % 2
            wcnt += 1
            p.op(SP, (lambda e, wb=wb, dco=dco: e.dma_start(out=w_t[wb], in_=wo_scr[dco])),
                 reads=((w_tag, dco),), writes=((T, "w", wb),), dma=("wi", wb))
            for ec in range(DC):
                p.op(PE, (lambda e, wb=wb, ec=ec, pb=pb, xb=xb: e.matmul(y_ps[pb], lhsT=w_t[wb][:, ec, :], rhs=ot[xb][:, ec, :],
                                                                        start=(ec == 0), stop=(ec == DC - 1))),
                     reads=((T, "w", wb), (T, "o", xb)), writes=((T, "y", pb),))
            p.op(DVE, (lambda e, pb=pb, dco=dco, xb=xb: e.tensor_tensor(out=yo_t[pb], in0=y_ps[pb], in1=xt[xb][:, dco, :], op=ALU.add)),
                 reads=((T, "y", pb), (T, "x", xb)), writes=((T, "yo", pb),))
            p.op(SP, (lambda e, pb=pb, dco=dco, cs=cs: e.dma_start(out=xout_v[:, dco, cs], in_=yo_t[pb])),
                 reads=((T, "yo", pb),), writes=((T, "xout", ti, dco),), dma=("yo", pb))


def stage_attn_a(cx, qT, Kg, Vg, biasM, sel_d, b31_d, lam_d, gsub_d, oT, lambda_init, tag, dbg=None, cc=False):
    p, sb = cx.p, cx.sb
    p.barrier()
    sb.reset(cx.sb_base)
    T = tag
    ps = cx.psum
    s_ps = [ps[0], ps[1], ps[2], ps[7]]
    o_ps = [ps[3], ps[4]]
    den_ps = ps[5]
    ssq_ps = ps[6]
    sel_t = sb.alloc(17 * 4, F32).rearrange("p (b c) -> p b c", c=4)
    b31_t = sb.alloc(NH, F32)
    lam_t = sb.alloc(4 * 128, F32).rearrange("p (a b) -> p a b", a=4)
    lprod = sb.alloc(128, F32)
    lsum = sb.alloc(2, F32)
    nlam = sb.alloc(1, F32)
    gsub_t = sb.alloc(2, F32)
    ccol = [sb.alloc(17, F32) for _ in range(2)]
    Kt = [sb.alloc(16 * TT, BF16).rearrange("p (g t) -> p g t", g=16) for _ in range(2)]
    Vt = sb.alloc(64 * 256, BF16).rearrange("p (b e) -> p b e", b=64)
    Qt = sb.alloc(2 * TL, BF16).rearrange("p (m t) -> p m t", m=2)
    Mt = [sb.alloc(1024, F32) for _ in range(2)]
    tmp_t = [sb.alloc(TT, F32) for _ in range(3)]
    pT_t = [sb.alloc(TT, BF16) for _ in range(6)]
    Oev = [sb.alloc(2 * TT, F32).rearrange("p (a b) -> p a b", a=2) for _ in range(2)]
    dev = [sb.alloc(TT, F32) for _ in range(2)]
    o_t = sb.alloc(2 * TT, F32).rearrange("p (a b) -> p a b", a=2)
    u_t = sb.alloc(TT, F32)
    sq_t = [sb.alloc(TT, F32) for _ in range(2)]
    rstd = sb.alloc(TT, F32)
    on_t = [sb.alloc(TT, BF16) for _ in range(2)]
    p.op(SP, lambda e: e.dma_start(out=sel_t, in_=sel_d.rearrange("p (b c) -> p b c", c=4)), writes=((T, "sel"),), dma=("c", 0))
    p.op(SP, lambda e: e.dma_start(out=b31_t, in_=b31_d), writes=((T, "b31"),), dma=("c", 1))
    p.op(SP, lambda e: e.dma_start(out=lam_t, in_=lam_d.rearrange("p (a b) -> p a b", a=4)), writes=((T, "lamv"),), dma=("c", 2))
    p.op(SP, lambda e: e.dma_start(out=gsub_t, in_=gsub_d), writes=((T, "gsub"),), dma=("c", 3))
    lprod2 = sb.alloc(128, F32)
    nlam0 = sb.alloc(1, F32)
    p.op(DVE, lambda e: e.tensor_tensor(out=lprod, in0=lam_t[:, 0, :], in1=lam_t[:, 1, :], op=ALU.mult),
         reads=((T, "lamv"),), writes=((T, "lp0"),))
    p.op(DVE, lambda e: e.reduce_sum(out=lsum[:, 0:1], in_=lprod, axis=AX.X), reads=((T, "lp0"),), writes=((T, "ls0"),), strict=True)
    p.op(DVE, lambda e: e.tensor_tensor(out=lprod2, in0=lam_t[:, 2, :], in1=lam_t[:, 3, :], op=ALU.mult),
         reads=((T, "lamv"),), writes=((T, "lp1"),))
    p.op(DVE, lambda e: e.reduce_sum(out=lsum[:, 1:2], in_=lprod2, axis=AX.X), reads=((T, "lp1"),), writes=((T, "ls1"),), strict=True)
    p.op(ACT, lambda e: e.activation(out=lsum, in_=lsum, func=AF.Exp), reads=((T, "ls0"), (T, "ls1")), writes=((T, "lsum"),), strict=True)
    p.op(DVE, lambda e: e.tensor_tensor(out=nlam0, in0=lsum[:, 1:2], in1=lsum[:, 0:1], op=ALU.subtract),
         reads=((T, "lsum"),), writes=((T, "nlam0"),))
    p.op(DVE, lambda e: e.tensor_scalar(out=nlam, in0=nlam0, scalar1=-float(lambda_init), scalar2=None, op0=ALU.add),
         reads=((T, "nlam0"),), writes=((T, "nlam"),), strict=True)
    p.op(DVE, lambda e: e.tensor_scalar(out=gsub_t, in0=gsub_t, scalar1=float(1.0 - lambda_init), scalar2=None, op0=ALU.mult),
         reads=((T, "gsub"),), writes=((T, "gsub"),), strict=True)
    scnt = 0
    pcnt = 0
    tcnt = 0
    for hd in (range(NH // 2) if dbg is None else [0]):
        for m in range(2):
            h = 2 * hd + m
            if cc:
                for Jl in range(NT):
                    p.op(SP, (lambda e, m=m, h=h, Jl=Jl: e.dma_start(out=Kt[m][:, 4 * Jl:4 * Jl + 4, :],
                                                                    in_=Kg[Jl, h // 4, :, (h % 4) * 128:(h % 4 + 1) * 128, :].rearrange("r d t -> d r t"))),
                         reads=(), writes=((T, "K", m, Jl),), dma=("k", m))
            else:
                p.op(SP, (lambda e, m=m, h=h: e.dma_start(out=Kt[m], in_=Kg[:, h].rearrange("g d t -> d g t"))),
                     reads=((T, "Kg"),), writes=((T, "K", m),), dma=("k", m))
            p.op(SP, (lambda e, m=m, h=h: e.dma_start(out=Qt[:, m, :], in_=qT[h])),
                 reads=((T, "qT"),), writes=((T, "Q", m),), dma=("q", m))
            p.op(SP, (lambda e, m=m, h=h: e.dma_start(out=Mt[m], in_=biasM[h])),
                 reads=(), writes=((T, "M", m),), dma=("m", m))
            p.op(DVE, (lambda e, m=m, h=h: e.scalar_tensor_tensor(out=ccol[m], in0=sel_t[:, :, 2], scalar=b31_t[:, h:h + 1],
                                                                 in1=sel_t[:, :, 3], op0=ALU.mult, op1=ALU.add)),
                 reads=((T, "sel"), (T, "b31")), writes=((T, "ccol", m),))
        if cc:
            for Jl in range(NT):
                p.op(SP, (lambda e, hd=hd, Jl=Jl: e.dma_start(out=Vt[:, 16 * Jl:16 * Jl + 16, :],
                                                            in_=Vg[Jl, hd // 2, :, :, (hd % 2) * 256:(hd % 2 + 1) * 256].rearrange("r (tb p) e -> p (r tb) e", p=128))),
                     reads=(), writes=((T, "V", Jl),), dma=("v",))
        else:
            p.op(SP, (lambda e, hd=hd: e.dma_start(out=Vt, in_=Vg[:, :, hd * 256:(hd + 1) * 256].rearrange("g (tb p) e -> p (g tb) e", p=128))),
                 reads=((T, "Vg"),), writes=((T, "V"),), dma=("v",))
        for J in (range(NT) if dbg is None else [0]):
            qs = slice(J * TT, (J + 1) * TT)
            for m in range(2):
                h = 2 * hd + m
                blocks = []
                for Jp in range(J + 1):
                    for rp in range(4):
                        for kbi in range(4):
                            if Jp == J:
                                si = 1 + rp * 4 + kbi
                            elif Jp == J - 1 and rp == 3 and kbi == 3:
                                si = 0
                            else:
                                si = None
                            blocks.append((4 * Jp + rp, kbi, si))
                nb = len(blocks)
                LA = 2
                pbs = {}

                def emit_s(bi, g, kbi, si):
                    nonlocal scnt, pcnt, tcnt
                    sbk = scnt % 4
                    scnt += 1
                    p.op(PE, (lambda e, sbk=sbk, m=m, g=g, kbi=kbi, qs=qs: e.matmul(
                        s_ps[sbk], lhsT=Kt[m][:, g, kbi * 128:(kbi + 1) * 128], rhs=Qt[:, m, qs], start=True, stop=True)),
                         reads=(((T, "K", m, g // 4) if cc else (T, "K", m)), (T, "Q", m)), writes=((T, "s", sbk),))
                    pb = pcnt % 6
                    pcnt += 1
                    pbs[bi] = pb
                    if si is None:
                        p.op(ACT, (lambda e, sbk=sbk, pb=pb, h=h: e.activation(out=pT_t[pb], in_=s_ps[sbk], func=AF.Exp, bias=b31_t[:, h:h + 1])),
                             reads=((T, "s", sbk), (T, "b31")), writes=((T, "pT", pb),))
                    else:
                        tb = tcnt % 3
                        tcnt += 1
                        ta = Mt[m][:, 384 - kbi * 128:384 - kbi * 128 + TT]
                        tbb = Mt[m][:, 512:1024]

                        def f_sel(e, tb=tb, sbk=sbk, si=si, ta=ta, tbb=tbb):
                            e.scalar_tensor_tensor(out=tmp_t[tb], in0=ta, scalar=sel_t[:, si, 0:1], in1=s_ps[sbk], op0=ALU.mult, op1=ALU.add)
                            return e.scalar_tensor_tensor(out=tmp_t[tb], in0=tbb, scalar=sel_t[:, si, 1:2], in1=tmp_t[tb], op0=ALU.mult, op1=ALU.add)
                        p.op(DVE, f_sel, reads=((T, "s", sbk), (T, "M", m), (T, "sel")), writes=((T, "tmp", tb),))
                        p.op(ACT, (lambda e, tb=tb, pb=pb, m=m, si=si: e.activation(out=pT_t[pb], in_=tmp_t[tb], func=AF.Exp, bias=ccol[m][:, si:si + 1])),
                             reads=((T, "tmp", tb), (T, "ccol", m)), writes=((T, "pT", pb),))

                def emit_pv(bi, g, kbi):
                    pb = pbs[bi]
                    vblk = g * 4 + kbi
                    for ec in range(2):
                        p.op(PE, (lambda e, pb=pb, vblk=vblk, ec=ec, bi=bi, nb=nb: e.matmul(
                            o_ps[ec], lhsT=Vt[:, vblk, ec * 128:(ec + 1) * 128], rhs=pT_t[pb], start=(bi == 0), stop=(bi == nb - 1))),
                             reads=((T, "pT", pb), ((T, "V", g // 4) if cc else (T, "V"))), writes=((T, "ops", ec),))
                    p.op(PE, (lambda e, pb=pb, bi=bi, nb=nb: e.matmul(den_ps, lhsT=cx.ones_b, rhs=pT_t[pb], start=(bi == 0), stop=(bi == nb - 1))),
                         reads=((T, "pT", pb), "ones_b"), writes=((T, "den"),))

                for step in range(nb + LA):
                    if step < nb:
                        emit_s(step, *blocks[step])
                    if step >= LA:
                        bg_, kb2, _ = blocks[step - LA]
                        emit_pv(step - LA, bg_, kb2)
                for ec in range(2):
                    p.op(ACT, (lambda e, m=m, ec=ec: e.activation(out=Oev[m][:, ec, :], in_=o_ps[ec], func=AF.Copy)),
                         reads=((T, "ops", ec),), writes=((T, "Oev", m, ec),))
                p.op(DVE, (lambda e, m=m: e.reciprocal(out=dev[m], in_=den_ps)), reads=((T, "den"),), writes=((T, "dev", m),))
            if dbg is not None:
                p.op(SP, lambda e: e.dma_start(out=dbg[5][:, 0:2], in_=lsum), reads=((T, "lsum"), (T, "nlam")), writes=(("dbg", "l"),), dma=("c", 0))
                p.op(SP, lambda e: e.dma_start(out=dbg[5][:, 2:3], in_=nlam, allow_slow_non_contiguous=True), reads=((T, "nlam"),), writes=(("dbg", "n"),), dma=("c", 0))
                p.op(SP, lambda e: e.dma_start(out=dbg[5][:, 4:6], in_=gsub_t), reads=((T, "gsub"),), writes=(("dbg", "g"),), dma=("c", 0))
                p.barrier()
                for m in range(1):
                    for ec in range(2):
                        p.op(SP, (lambda e, m=m, ec=ec: e.dma_start(out=dbg[m * 3 + ec], in_=Oev[m][:, ec, :])),
                             reads=((T, "Oev", m, ec),), writes=(("dbg", m, ec),), dma=("c", 0))
                    p.op(SP, (lambda e, m=m: e.dma_start(out=dbg[m * 3 + 2], in_=dev[m])),
                         reads=((T, "dev", m),), writes=(("dbg", m, 2),), dma=("c", 0))
                p.barrier()
            p.op(DVE, lambda e: e.tensor_scalar(out=dev[1], in0=dev[1], scalar1=nlam[:, 0:1], scalar2=None, op0=ALU.mult),
                 reads=((T, "dev", 1), (T, "nlam")), writes=((T, "dev", 1),))
            for ec in range(2):
                def f_comb(e, ec=ec):
                    e.tensor_tensor(out=o_t[:, ec, :], in0=Oev[0][:, ec, :], in1=dev[0], op=ALU.mult)
                    e.tensor_tensor(out=u_t, in0=Oev[1][:, ec, :], in1=dev[1], op=ALU.mult)
                    return e.tensor_tensor(out=o_t[:, ec, :], in0=o_t[:, ec, :], in1=u_t, op=ALU.add)
                p.op(DVE, f_comb, reads=((T, "Oev", 0, ec), (T, "Oev", 1, ec), (T, "dev", 0), (T, "dev", 1)), writes=((T, "o", ec),))
                p.op(ACT, (lambda e, ec=ec: e.activation(out=sq_t[ec], in_=o_t[:, ec, :], func=AF.Square)),
                     reads=((T, "o", ec),), writes=((T, "sq", ec),))
                p.op(PE, (lambda e, ec=ec: e.matmul(ssq_ps, lhsT=cx.ones_f, rhs=sq_t[ec], start=(ec == 0), stop=(ec == 1))),
                     reads=((T, "sq", ec), "ones_f"), writes=((T, "ssq"),))
            p.op(ACT, lambda e: e.activation(out=rstd, in_=ssq_ps, func=AF.Sqrt, scale=1.0 / 256.0, bias=cx.eps_t),
                 reads=((T, "ssq"), "eps_t"), writes=((T, "rstd"),))
            p.op(DVE, lambda e: e.reciprocal(out=rstd, in_=rstd), reads=((T, "rstd"),), writes=((T, "rstd"),))
            for ec in range(2):
                p.op(DVE, (lambda e, ec=ec: e.scalar_tensor_tensor(out=on_t[ec], in0=o_t[:, ec, :], scalar=gsub_t[:, ec:ec + 1], in1=rstd,
                                                                   op0=ALU.mult, op1=ALU.mult)),
                     reads=((T, "o", ec), (T, "rstd"), (T, "gsub")), writes=((T, "on", ec),))
                r0 = hd * 256 + ec * 128
                p.op(SP, (lambda e, ec=ec, r0=r0, qs=qs: e.dma_start(out=oT[r0:r0 + 128, qs], in_=on_t[ec])),
                     reads=((T, "on", ec),), writes=((T, "oT", hd, J, ec),), dma=("on", ec))


def stage_attn_b(cx, qT, Kg, Vg, m01_d, negm_d, cmat_d, oT, tag, cc=False):
    p, sb = cx.p, cx.sb
    p.barrier()
    sb.reset(cx.sb_base)
    T = tag
    ps = cx.psum
    z_ps = [ps[0], ps[1], ps[2], ps[3]]
    ob_ps = [ps[4], ps[5]]
    r_ps = ps[6]
    tr_ps = ps[7]
    m01_t = sb.alloc(16 * TT, BF16).rearrange("p (b t) -> p b t", b=16)
    negm_t = sb.alloc(16 * TT, BF16).rearrange("p (b t) -> p b t", b=16)
    cm_t = sb.alloc(3 * 128, BF16).rearrange("p (a b) -> p a b", a=3)
    identf = sb.alloc(128, F32)
    Kt = [sb.alloc(16 * TT, BF16).rearrange("p (g t) -> p g t", g=16) for _ in range(2)]
    Vt = [sb.alloc(64 * 128, BF16).rearrange("p (b e) -> p b e", b=64) for _ in range(2)]
    Qt = [sb.alloc(TL, BF16) for _ in range(2)]
    e_t = [sb.alloc(TT, F32) for _ in range(3)]
    sp_t = [sb.alloc(TT, BF16) for _ in range(5)]
    w_t = [sb.alloc(TT, BF16) for _ in range(3)]
    E_t = [sb.alloc(4, F32) for _ in range(3)]
    O_t = [sb.alloc(4 * 128, F32).rearrange("p (a b) -> p a b", a=4) for _ in range(2)]
    oo_t = [sb.alloc(TT, BF16) for _ in range(2)]
    p.op(SP, lambda e: e.dma_start(out=m01_t, in_=m01_d.rearrange("p (b t) -> p b t", b=16)), writes=((T, "m01"),), dma=("c", 0))
    p.op(SP, lambda e: e.dma_start(out=negm_t, in_=negm_d.rearrange("p (b t) -> p b t", b=16)), writes=((T, "negm"),), dma=("c", 1))
    p.op(SP, lambda e: e.dma_start(out=cm_t, in_=cmat_d.rearrange("p (a b) -> p a b", a=3)), writes=((T, "cm"),), dma=("c", 2))
    p.op(DVE, lambda e: e.tensor_copy(out=identf, in_=cm_t[:, 1, :]), reads=((T, "cm"),), writes=((T, "identf"),))
    negU = cm_t[:, 0, :]
    ident = cm_t[:, 1, :]
    negones = cm_t[:, 2, 0:1]
    zc = 0
    ec_ = 0
    spc = 0
    wc = 0
    Ec = 0
    obc = 0
    hj = 0
    for h in range(NH):
        kb_ = h % 2
        if cc:
            for Jl in range(NT):
                p.op(SP, (lambda e, kb_=kb_, h=h, Jl=Jl: e.dma_start(out=Kt[kb_][:, 4 * Jl:4 * Jl + 4, :],
                                                                    in_=Kg[Jl, h // 4, :, (h % 4) * 128:(h % 4 + 1) * 128, :].rearrange("r d t -> d r t"))),
                     reads=(), writes=((T, "K", kb_, Jl),), dma=("k", kb_))
        else:
            p.op(SP, (lambda e, kb_=kb_, h=h: e.dma_start(out=Kt[kb_], in_=Kg[:, h].rearrange("g d t -> d g t"))),
                 reads=((T, "Kg"),), writes=((T, "K", kb_),), dma=("k", kb_))
        p.op(SP, (lambda e, kb_=kb_, h=h: e.dma_start(out=Qt[kb_], in_=qT[h])),
             reads=((T, "qT"),), writes=((T, "Q", kb_),), dma=("q", kb_))
        if cc:
            for Jl in range(NT):
                p.op(SP, (lambda e, kb_=kb_, h=h, Jl=Jl: e.dma_start(out=Vt[kb_][:, 16 * Jl:16 * Jl + 16, :],
                                                                    in_=Vg[Jl, h // 4, :, :, (h % 4) * 128:(h % 4 + 1) * 128].rearrange("r (tb p) e -> p (r tb) e", p=128))),
                     reads=(), writes=((T, "V", kb_, Jl),), dma=("v", kb_))
        else:
            p.op(SP, (lambda e, kb_=kb_, h=h: e.dma_start(out=Vt[kb_], in_=Vg[:, :, h * 128:(h + 1) * 128].rearrange("g (tb p) e -> p (g tb) e", p=128))),
                 reads=((T, "Vg"),), writes=((T, "V", kb_),), dma=("v", kb_))
        for J in range(NT):
            qs = slice(J * TT, (J + 1) * TT)
            ob_ = hj % 2
            hj += 1
            blocks = []
            for Jp in range(J + 1):
                for rp in range(4):
                    for kbi in range(4):
                        blocks.append((4 * Jp + rp, kbi, (rp * 4 + kbi) if Jp == J else None))
            blocks = blocks[::-1]
            nb = len(blocks)
            st8 = {}
            st9 = {}
            st7 = {}

            def stage1(bi, g, kbi, mi):
                nonlocal zc, ec_, spc
                zb = zc % 4
                zc += 1
                p.op(PE, (lambda e, zb=zb, kb_=kb_, g=g, kbi=kbi, qs=qs: e.matmul(
                    z_ps[zb], lhsT=Kt[kb_][:, g, kbi * 128:(kbi + 1) * 128], rhs=Qt[kb_][:, qs], start=True, stop=True)),
                     reads=(((T, "K", kb_, g // 4) if cc else (T, "K", kb_)), (T, "Q", kb_)), writes=((T, "z", zb),))
                eb = ec_ % 3
                ec_ += 1
                p.op(ACT, (lambda e, zb=zb, eb=eb: e.activation(out=e_t[eb], in_=z_ps[zb], func=AF.Exp)),
                     reads=((T, "z", zb),), writes=((T, "e", eb),))
                sb_ = spc % 5
                spc += 1
                p.op(ACT, (lambda e, eb=eb, sb_=sb_: e.activation(out=sp_t[sb_], in_=e_t[eb], func=AF.Ln, bias=cx.ones_f[:, 0:1])),
                     reads=((T, "e", eb),), writes=((T, "sp", sb_),))
                if mi is not None:
                    p.op(DVE, (lambda e, sb_=sb_, mi=mi: e.tensor_tensor(out=sp_t[sb_], in0=sp_t[sb_], in1=m01_t[:, mi, :], op=ALU.mult)),
                         reads=((T, "sp", sb_), (T, "m01")), writes=((T, "sp", sb_),))
                st8[bi] = (zb, eb, sb_)

            def stage2(bi, g, kbi, mi):
                zb, eb, sb_ = st8.pop(bi)
                last_is_mask = mi is not None
                p.op(PE, (lambda e, zb=zb, sb_=sb_, lm=last_is_mask: e.matmul(z_ps[zb], lhsT=negU, rhs=sp_t[sb_], start=False, stop=(not lm))),
                     reads=((T, "sp", sb_), (T, "cm"), (T, "e", eb)), writes=((T, "z", zb),))
                if mi is not None:
                    p.op(PE, (lambda e, zb=zb, mi=mi: e.matmul(z_ps[zb], lhsT=ident, rhs=negm_t[:, mi, :], start=False, stop=True)),
                         reads=((T, "negm"), (T, "cm")), writes=((T, "z", zb),))
                st7[bi] = (zb, sb_)

            def stage2c(bi, g, kbi, mi):
                nonlocal wc, Ec
                zb, sb_ = st7.pop(bi)
                wb = wc % 3
                wc += 1
                p.op(ACT, (lambda e, zb=zb, wb=wb: e.activation(out=w_t[wb], in_=z_ps[zb], func=AF.Exp)),
                     reads=((T, "z", zb),), writes=((T, "w", wb),))
                Eb = Ec % 3
                if bi > 0:
                    Ec += 1
                    p.op(ACT, (lambda e, Eb=Eb: e.activation(out=E_t[Eb], in_=r_ps[:, 0:4], func=AF.Exp)),
                         reads=((T, "R"),), writes=((T, "E", Eb),))
                st9[bi] = (sb_, wb, Eb)

            def stage2b(bi, g, kbi, mi):
                nonlocal obc
                sb_, wb, Eb = st9.pop(bi)

                def f_r(e, sb_=sb_, bi=bi, nb=nb):
                    ins = None
                    for ts_ in range(4):
                        ins = e.matmul(r_ps[:, ts_:ts_ + 1], lhsT=sp_t[sb_][:, ts_ * 128:(ts_ + 1) * 128], rhs=negones,
                                       start=(bi == 0 and ts_ == 0), stop=(bi == nb - 1), skip_group_check=True)
                    return ins
                p.op(PE, f_r, reads=((T, "sp", sb_), (T, "cm")), writes=((T, "R"),))
                ob2 = obc % 2
                obc += 1
                vblk = g * 4 + kbi

                def f_pv(e, wb=wb, ob2=ob2, vblk=vblk, kb_=kb_):
                    ins = None
                    for ts_ in range(4):
                        ins = e.matmul(ob_ps[ob2][:, ts_ * 128:(ts_ + 1) * 128], lhsT=w_t[wb][:, ts_ * 128:(ts_ + 1) * 128],
                                       rhs=Vt[kb_][:, vblk, :], start=True, stop=True)
                    return ins
                p.op(PE, f_pv, reads=((T, "w", wb), ((T, "V", kb_, g // 4) if cc else (T, "V", kb_))), writes=((T, "ob", ob2),))
                if bi == 0:
                    p.op(DVE, (lambda e, ob2=ob2, ob_=ob_: e.tensor_copy(out=O_t[ob_], in_=ob_ps[ob2].rearrange("p (a b) -> p a b", a=4))),
                         reads=((T, "ob", ob2),), writes=((T, "O", ob_),))
                else:
                    def f_acc(e, ob2=ob2, ob_=ob_, Eb=Eb):
                        ins = None
                        for ts_ in range(4):
                            ins = e.scalar_tensor_tensor(out=O_t[ob_][:, ts_, :], in0=ob_ps[ob2][:, ts_ * 128:(ts_ + 1) * 128],
                                                         scalar=E_t[Eb][:, ts_:ts_ + 1], in1=O_t[ob_][:, ts_, :], op0=ALU.mult, op1=ALU.add)
                        return ins
                    p.op(DVE, f_acc, reads=((T, "ob", ob2), (T, "E", Eb)), writes=((T, "O", ob_),))

            for step in range(nb + 3):
                if step < nb:
                    stage1(step, *blocks[step])
                if 2 <= step < nb + 2:
                    stage2(step - 2, *blocks[step - 2])
                if step >= 3:
                    stage2b(step - 3, *blocks[step - 3])
                if 2 <= step < nb + 2:
                    stage2c(step - 2, *blocks[step - 2])
            oo = hj % 2
            for ts_ in range(4):
                p.op(PE, (lambda e, ts_=ts_, ob_=ob_: e.transpose(tr_ps[:, ts_ * 128:(ts_ + 1) * 128], O_t[ob_][:, ts_, :], identf)),
                     reads=((T, "O", ob_), (T, "identf")), writes=((T, "tr", ts_),))
            p.op(ACT, (lambda e, oo=oo: e.activation(out=oo_t[oo], in_=tr_ps, func=AF.Copy)),
                 reads=tuple((T, "tr", i) for i in range(4)), writes=((T, "oo", oo),))
            p.op(SP, (lambda e, oo=oo, h=h, qs=qs: e.dma_start(out=oT[h * 128:(h + 1) * 128, qs], in_=oo_t[oo])),
                 reads=((T, "oo", oo),), writes=((T, "oT", h, J),), dma=("on", oo))


def alloc_common(cx, es):
    nc = cx.nc
    cap = 200 * 1024
    big = es.enter_context(nc.sbuf_tensor("big", [128, cap], U8))
    cx.sb = Sbuf(big, cap)
    cx.psum = [es.enter_context(nc.psum_tensor("ps%d" % i, [128, 512], F32)) for i in range(8)]
    cx.psum = [t[:] for t in cx.psum]
    cx.ones_f = cx.sb.alloc(128, F32)
    cx.ones_b = cx.sb.alloc(128, BF16)
    cx.ident_b = cx.sb.alloc(128, BF16)
    cx.eps_t = cx.sb.alloc(1, F32)
    cx.sb_base = cx.sb.off
    p = cx.p
    p.op(DVE, lambda e: e.memset(cx.ones_f, 1.0), writes=("ones_f",))
    p.op(DVE, lambda e: e.memset(cx.ones_b, 1.0), writes=("ones_b",))
    p.op(DVE, lambda e: e.memset(cx.eps_t, EPS), writes=("eps_t",))


def _new_cx():
    nc = bass.Bass("TRN2", target_bir_lowering=False)
    cx = Ctx()
    cx.nc = nc
    cx.p = Prog(nc)
    return nc, cx


def _ein(nc, name, shape, dt=F32):
    return nc.dram_tensor(name, list(shape), dt, kind="ExternalInput").ap()


def _eout(nc, name, shape, dt=F32):
    return nc.dram_tensor(name, list(shape), dt, kind="ExternalOutput").ap()


def ffn_blocks():
    blocks = []
    for fc in range(FC):
        blocks += [fc * 128, DFF + fc * 128]
    return blocks


def build_ffn_prog():
    nc, cx = _new_cx()
    x = _ein(nc, "x", [D, TL])
    g = _ein(nc, "g", [128, DC])
    wi = _ein(nc, "wi", [D, 2 * DFF])
    wo = _ein(nc, "wo", [DFF, D])
    y = _eout(nc, "y", [D, TL])
    wi_scr = mk_weight_scratch(cx, "wi_b", D, 128, 2 * FC)
    wo_scr = mk_weight_scratch(cx, "wo_b", DFF, 128, DC)
    with contextlib.ExitStack() as es:
        alloc_common(cx, es)
        emit_wcvt(cx, wi, wi_scr, ffn_blocks(), 128, "fwi")
        emit_wcvt(cx, wo, wo_scr, [i * 128 for i in range(DC)], 128, "fwo")
        cx.p.barrier()
        stage_ffn2(cx, x, y, g, wi_scr, wo_scr, "f", wi_tag="fwi", wo_tag="fwo")
        cx.p.emit()
    return nc


def qkv_specs(cx, tag, qT, kT, gains, nq, nk, q_block0, k_block0, scale_q):
    specs = []
    for oc in range(nq):
        specs.append((q_block0 + oc, (lambda ti, oc=oc: qT[oc][:, ti * TT:(ti + 1) * TT]),
                      gains[:, 0:1] if gains is not None else None, scale_q))
    for oc in range(nk):
        specs.append((k_block0 + oc, (lambda ti, oc=oc: kT[ti][oc]),
                      gains[:, 1:2] if gains is not None else None, 1.0))
    return specs


def build_qkv_prog(kind):
    nc, cx = _new_cx()
    x = _ein(nc, "x", [D, TL])
    g = _ein(nc, "g", [128, DC])
    ncol = {"A": 3 * D, "KV": 2 * D, "Q": D}[kind]
    w = _ein(nc, "w", [D, ncol])
    scale = HD ** -0.5
    qT = kT = v = None
    if kind in ("A", "Q"):
        qT = _eout(nc, "qT", [NH, 128, TL], BF16)
    if kind in ("A", "KV"):
        kT = _eout(nc, "kT", [NT, NH, 128, TT], BF16)
        v = _eout(nc, "v", [NT, TT, D], BF16)
    nfm = {"A": 32, "KV": 16, "Q": 16}[kind]
    fm_scr = mk_weight_scratch(cx, "wfm_b", D, 128, nfm)
    v_scr = mk_weight_scratch(cx, "wv_b", D, 512, 4) if kind != "Q" else None
    with contextlib.ExitStack() as es:
        alloc_common(cx, es)
        T = "p"
        emit_wcvt(cx, w, fm_scr, [i * 128 for i in range(nfm)], 128, T + "wfm")
        if v_scr is not None:
            emit_wcvt(cx, w, v_scr, [nfm * 128 + i * 512 for i in range(4)], 512, T + "wv")
        gains = None
        if kind == "A":
            gqk = _ein(nc, "gqk", [128, 2])
            gains = cx.sb.alloc(2, F32)
            cx.sb_base = cx.sb.off
            cx.p.op(SP, lambda e: e.dma_start(out=gains, in_=gqk), writes=((T, "gains0"),), dma=("c", 0))
            cx.p.op(DVE, lambda e: e.tensor_scalar(out=gains[:, 0:1], in0=gains[:, 0:1], scalar1=float(scale), scalar2=None, op0=ALU.mult),
                    reads=((T, "gains0"),), writes=((T, "gains"),))
        if kind == "A":
            specs = qkv_specs(cx, T, qT, kT, gains, 16, 16, 0, 16, 1.0)
        elif kind == "KV":
            specs = qkv_specs(cx, T, None, kT, None, 0, 16, 0, 0, 1.0)
        else:
            specs = qkv_specs(cx, T, qT, None, None, 16, 0, 0, 0, scale)
        cx.p.barrier()
        stage_qkv(cx, x, g, T, fm_scr, specs, v_scr, v, nfm_blocks=nfm)
        cx.p.emit()
    return nc


def build_wo_prog():
    nc, cx = _new_cx()
    x = _ein(nc, "x", [D, TL])
    oT = _ein(nc, "oT", [D, TL], BF16)
    w = _ein(nc, "w", [D, D])
    y = _eout(nc, "y", [D, TL])
    scr = mk_weight_scratch(cx, "wo_b", D, 128, DC)
    with contextlib.ExitStack() as es:
        alloc_common(cx, es)
        emit_wcvt(cx, w, scr, [i * 128 for i in range(DC)], 128, "ow")
        cx.p.barrier()
        stage_wo(cx, x, y, oT, scr, "o", w_tag="ow")
        cx.p.emit()
    return nc


def build_attn_a_prog(lambda_init, debug=False):
    nc, cx = _new_cx()
    dbg = _eout(nc, "dbg", [6, 128, TT]) if debug else None
    qT = _ein(nc, "qT", [NH, 128, TL], BF16)
    Kg = _ein(nc, "Kg", [16, NH, 128, TT], BF16)
    Vg = _ein(nc, "Vg", [16, TT, D], BF16)
    biasM = _ein(nc, "biasM", [NH, 128, 1024])
    sel = _ein(nc, "sel", [128, 17 * 4])
    b31 = _ein(nc, "b31", [128, NH])
    lam = _ein(nc, "lam", [128, 4 * 128])
    gsub = _ein(nc, "gsub", [128, 2])
    oT = _eout(nc, "oT", [D, TL], BF16)
    with contextlib.ExitStack() as es:
        alloc_common(cx, es)
        stage_attn_a(cx, qT, Kg, Vg, biasM, sel, b31, lam, gsub, oT, lambda_init, "a", dbg=dbg)
        cx.p.emit()
    return nc


def build_attn_b_prog():
    nc, cx = _new_cx()
    qT = _ein(nc, "qT", [NH, 128, TL], BF16)
    Kg = _ein(nc, "Kg", [16, NH, 128, TT], BF16)
    Vg = _ein(nc, "Vg", [16, TT, D], BF16)
    m01 = _ein(nc, "m01", [128, 16 * TT], BF16)
    negm = _ein(nc, "negm", [128, 16 * TT], BF16)
    cmat = _ein(nc, "cmat", [128, 3 * 128], BF16)
    oT = _eout(nc, "oT", [D, TL], BF16)
    with contextlib.ExitStack() as es:
        alloc_common(cx, es)
        stage_attn_b(cx, qT, Kg, Vg, m01, negm, cmat, oT, "b")
        cx.p.emit()
    return nc


def build_fused_prog():
    nc, cx = _new_cx()
    p = cx.p
    x = _ein(nc, "x", [D, TL])
    y = _eout(nc, "y", [D, TL])
    W = {}
    for nm, shp in (("ffn_pre_wi", [DEPTH, D, 2 * DFF]), ("ffn_pre_wo", [DEPTH, DFF, D]),
                    ("ffn_post_wi", [DEPTH, D, 2 * DFF]), ("ffn_post_wo", [DEPTH, DFF, D]),
                    ("a_wqkv", [NA, D, 3 * D]), ("a_wo", [NA, D, D]), ("b_wkv", [D, 2 * D]),
                    ("b_wq", [DEPTH - NA, D, D]), ("b_wo", [DEPTH - NA, D, D]),
                    ("ffn_pre_norm", [DEPTH, 128, DC]), ("mix_norm", [DEPTH, 128, DC]), ("ffn_post_norm", [DEPTH, 128, DC]),
                    ("kv_norm", [128, DC]), ("gqk", [NA, 128, 2]), ("lam", [NA, 128, 512]), ("gsub", [NA, 128, 2]),
                    ("biasM", [NH, 128, 1024]), ("sel", [128, 68]), ("b31", [128, NH])):
        W[nm] = _ein(nc, nm, shp)
    for nm, shp in (("m01", [128, 16 * TT]), ("negm", [128, 16 * TT]), ("cmat", [128, 3 * 128])):
        W[nm] = _ein(nc, nm, shp, BF16)
    xs = [nc.dram_tensor("xa", [D, TL], F32).ap(), nc.dram_tensor("xb", [D, TL], F32).ap()]
    qT = nc.dram_tensor("qT", [NH, 128, TL], BF16).ap()
    oT = nc.dram_tensor("oT", [D, TL], BF16).ap()
    kT_loc = [nc.dram_tensor("kT%d" % i, [NT, NH, 128, TT], BF16).ap() for i in range(3)]
    v_loc = [nc.dram_tensor("vl%d" % i, [NT, 4, TT, 512], BF16).ap() for i in range(3)]
    Kg = [nc.dram_tensor("Kg%d" % i, [NT, 4, 4, 512, TT], BF16).ap() for i in range(3)]
    Vg = [nc.dram_tensor("Vg%d" % i, [NT, 4, 4, TT, 512], BF16).ap() for i in range(3)]
    wi_scr = [mk_weight_scratch(cx, "wi_b%d" % i, D, 128, 2 * FC) for i in range(2)]
    wo_scr = [mk_weight_scratch(cx, "wo_b%d" % i, DFF, 128, DC) for i in range(2)]
    fm_scr = mk_weight_scratch(cx, "fm_b", D, 128, 32)
    vw_scr = mk_weight_scratch(cx, "vw_b", D, 512, 4)
    ow_scr = mk_weight_scratch(cx, "ow_b", D, 128, DC)
    groups = [[0, 1, 2, 3], [4, 5, 6, 7]]
    scale = HD ** -0.5

    stages = []
    cur = {"x": x, "i": 0, "nffn": 0, "kv": 0}

    def nxt():
        o = xs[cur["i"] % 2]
        cur["i"] += 1
        return o

    def add_ffn(wi, wo, g, name):
        k = cur["nffn"] % 2
        cur["nffn"] += 1

        def cv(par, bg, wi=wi, wo=wo, k=k):
            emit_wcvt(cx, wi, wi_scr[k], ffn_blocks(), 128, "fwi%d" % k, par, bg)
            emit_wcvt(cx, wo, wo_scr[k], [i * 128 for i in range(DC)], 128, "fwo%d" % k, par, bg)

        def st(k=k, g=g, name=name):
            xin = cur["x"]
            xo = y if name == "last" else nxt()
            stage_ffn2(cx, xin, xo, g, wi_scr[k], wo_scr[k], name, wi_tag="fwi%d" % k, wo_tag="fwo%d" % k)
            cur["x"] = xo
        stages.append((cv, st))

    def gather_hook(e_idx, T):
        def hook(ti):
            for hg in range(4):
                rk = tuple((T, "fmout", si, ti) for si in hook.kspecs[hg * 4:hg * 4 + 4])
                p.op(POOL, (lambda e, ti=ti, hg=hg: e.collective_compute(
                    "AllGather", ALU.bypass, groups, [kT_loc[e_idx][ti, hg * 4:hg * 4 + 4].rearrange("h d t -> (h d) t")],
                    [Kg[e_idx][ti, hg].rearrange("r d t -> (r d) t")])),
                     reads=rk, writes=((T, "Kgath", ti, hg),), dma=("cc",), inc=1)
            for eg in range(4):
                rv = tuple((T, "vout", ti, tb, eg) for tb in range(4))
                p.op(POOL, (lambda e, ti=ti, eg=eg: e.collective_compute(
                    "AllGather", ALU.bypass, groups, [v_loc[e_idx][ti, eg]],
                    [Vg[e_idx][ti, eg].rearrange("r t e -> (r t) e")])),
                     reads=rv, writes=((T, "Vgath", ti, eg),), dma=("cc",), inc=1)
        return hook

    def add_qkv(kind, w, g, name, l=0, e_idx=0):
        nfm = {"A": 32, "KV": 16, "Q": 16}[kind]

        def cv(par, bg, w=w, nfm=nfm, kind=kind):
            emit_wcvt(cx, w, fm_scr, [i * 128 for i in range(nfm)], 128, "fm", par, bg)
            if kind != "Q":
                emit_wcvt(cx, w, vw_scr, [nfm * 128 + i * 512 for i in range(4)], 512, "vw", par, bg)

        def st(kind=kind, g=g, name=name, l=l, e_idx=e_idx, nfm=nfm):
            T = name
            gains = None
            if kind == "A":
                cx.sb.reset(cx.sb_base0)
                gains = cx.sb.alloc(2, F32)
                cx.sb_base = cx.sb.off
                p.barrier()
                p.op(SP, lambda e: e.dma_start(out=gains, in_=W["gqk"][l]), writes=((T, "gains0"),), dma=("c", 0))
                p.op(DVE, lambda e: e.tensor_scalar(out=gains[:, 0:1], in0=gains[:, 0:1], scalar1=float(scale), scalar2=None, op0=ALU.mult),
                     reads=((T, "gains0"),), writes=((T, "gains"),))
                specs = qkv_specs(cx, T, qT, kT_loc[e_idx], gains, 16, 16, 0, 16, 1.0)
                kspecs = list(range(16, 32))
            elif kind == "KV":
                specs = qkv_specs(cx, T, None, kT_loc[e_idx], None, 0, 16, 0, 0, 1.0)
                kspecs = list(range(0, 16))
            else:
                specs = qkv_specs(cx, T, qT, None, None, 16, 0, 0, 0, scale)
                kspecs = None
            hook = None
            if kind != "Q":
                hook = gather_hook(e_idx, T)
                hook.kspecs = kspecs
            stage_qkv(cx, cur["x"], g, T, fm_scr, specs, vw_scr if kind != "Q" else None,
                      v_loc[e_idx] if kind != "Q" else None, fm_tag="fm", v_tag="vw", v_blocked=True,
                      post_tile=hook, nfm_blocks=nfm)
            cx.sb_base = cx.sb_base0
        stages.append((cv, st))

    def add_wo(w, name):
        def cv(par, bg, w=w):
            emit_wcvt(cx, w, ow_scr, [i * 128 for i in range(DC)], 128, "ow", par, bg)

        def st(name=name):
            xin = cur["x"]
            xo = nxt()
            stage_wo(cx, xin, xo, oT, ow_scr, name, w_tag="ow")
            cur["x"] = xo
        stages.append((cv, st))

    def add_attn_a(l, e_idx):
        lambda_init = 0.8 - 0.6 * math.exp(-0.3 * l)

        def st(l=l, e_idx=e_idx, lambda_init=lambda_init):
            stage_attn_a(cx, qT, Kg[e_idx], Vg[e_idx], W["biasM"], W["sel"], W["b31"], W["lam"][l], W["gsub"][l], oT,
                         lambda_init, "aa%d" % l, cc=True)
        stages.append((None, st))

    def add_attn_b(i):
        def st(i=i):
            stage_attn_b(cx, qT, Kg[2], Vg[2], W["m01"], W["negm"], W["cmat"], oT, "ab%d" % i, cc=True)
        stages.append((None, st))

    for l in range(DEPTH):
        if l == NA:
            add_qkv("KV", W["b_wkv"], W["kv_norm"], "kvb", e_idx=2)
        add_ffn(W["ffn_pre_wi"][l], W["ffn_pre_wo"][l], W["ffn_pre_norm"][l], "fpre%d" % l)
        if l < NA:
            add_qkv("A", W["a_wqkv"][l], W["mix_norm"][l], "qkva%d" % l, l=l, e_idx=l)
            add_attn_a(l, l)
            add_wo(W["a_wo"][l], "woa%d" % l)
        else:
            i = l - NA
            add_qkv("Q", W["b_wq"][i], W["mix_norm"][l], "qb%d" % i)
            add_attn_b(i)
            add_wo(W["b_wo"][i], "wob%d" % i)
        add_ffn(W["ffn_post_wi"][l], W["ffn_post_wo"][l], W["ffn_post_norm"][l], "last" if l == DEPTH - 1 else "fpost%d" % l)

    with contextlib.ExitStack() as es:
        alloc_common(cx, es)
        cx.sb_base0 = cx.sb_base
        done = set()

        def do_cv(si, bg):
            if si < len(stages) and stages[si][0] is not None and si not in done:
                stages[si][0](len(done) % 2, bg)
                done.add(si)
        do_cv(0, False)
        for si, (cv, st) in enumerate(stages):
            assert cv is None or si in done
            p.barrier()
            if si + 1 < len(stages):
                if stages[si + 1][0] is not None:
                    do_cv(si + 1, True)
                elif si + 2 < len(stages):
                    do_cv(si + 2, True)
            st()
        p.emit()
    return nc


def _tok_idx(r):
    return np.concatenate([np.arange((4 * J + r) * TT, (4 * J + r + 1) * TT) for J in range(NT)])


def _col(vec):
    return np.ascontiguousarray(np.asarray(vec, np.float32).reshape(-1, 128).T)


def _t5_bucket_np(n):
    n = np.maximum(n, 0)
    nf = np.maximum(n, 16).astype(np.float32)
    large = 16 + (np.log(nf / np.float32(16)) / np.float32(math.log(128 / 16)) * np.float32(16)).astype(np.int32)
    large = np.minimum(large, 31)
    return np.where(n < 16, n, large)


def _bias_master(rel_bias):
    s = np.arange(128)[:, None]
    u = np.arange(1024)[None, :]
    n = u - 384 - s
    bk = _t5_bucket_np(n)
    M = np.empty((NH, 128, 1024), np.float32)
    for h in range(NH):
        M[h] = np.where(n >= 0, rel_bias[bk, h], np.float32(NEG))
    return M


def _sel_table(r):
    sel = np.zeros((17, 4), np.float32)
    sel[0] = (0, 1, 0, 0) if r == 0 else (0, 0, 1, 0)
    for rp in range(4):
        for kbi in range(4):
            if rp == r:
                c = (1, 0, 0, 0)
            elif rp == r - 1 and kbi == 3:
                c = (0, 1, 0, 0)
            elif rp < r:
                c = (0, 0, 1, 0)
            else:
                c = (0, 0, 0, NEG)
            sel[1 + rp * 4 + kbi] = c
    return np.ascontiguousarray(np.broadcast_to(sel.reshape(1, -1), (128, 68)))


def _b_masks(r):
    s = np.arange(128)[:, None]
    t = np.arange(TT)[None, :]
    m01 = np.zeros((128, 16, TT), np.float32)
    for rp in range(4):
        for kbi in range(4):
            if rp < r:
                m = np.ones((128, TT), np.float32)
            elif rp > r:
                m = np.zeros((128, TT), np.float32)
            else:
                m = ((kbi * 128 + s) < t).astype(np.float32)
            m01[:, rp * 4 + kbi, :] = m
    negm = (1.0 - m01) * NEG
    bf = ml_dtypes.bfloat16
    return m01.reshape(128, -1).astype(bf), negm.reshape(128, -1).astype(bf)


def _cmat():
    j = np.arange(128)[:, None]
    s = np.arange(128)[None, :]
    negU = -(j >= s).astype(np.float32)
    ident = np.eye(128, dtype=np.float32)
    no = np.zeros((128, 128), np.float32)
    no[:, 0] = -1.0
    return np.concatenate([negU, ident, no], axis=1).astype(ml_dtypes.bfloat16)


_PROGS = {}


def _prog(key, fn, *a):
    if key not in _PROGS:
        _PROGS[key] = fn(*a)
    return _PROGS[key]


def _run(nc, in_maps):
    res = run_bass_kernel_spmd(nc, in_maps, core_ids=list(range(NCORES)))
    return res.results


def _gather_kv(kTs, vs):
    Kgs, Vgs = [], []
    for b in range(NB):
        Kg = np.empty((16, NH, 128, TT), kTs[0].dtype)
        Vg = np.empty((16, TT, D), vs[0].dtype)
        for rp in range(4):
            c = b * 4 + rp
            for J in range(NT):
                Kg[4 * J + rp] = kTs[c][J]
                Vg[4 * J + rp] = vs[c][J]
        Kgs.append(Kg)
        Vgs.append(Vg)
    return Kgs, Vgs


def _cols(mat):
    mat = np.asarray(mat, np.float32)
    return np.ascontiguousarray(mat.reshape(mat.shape[0], -1, 128).transpose(0, 2, 1))


def kernel(x, ffn_pre_norm, ffn_pre_wi, ffn_pre_wo, mix_norm, ffn_post_norm, ffn_post_wi,
           ffn_post_wo, rel_bias, a_wqkv, a_q_norm, a_k_norm, a_lambda, a_subln, a_wo,
           kv_norm, b_wkv, b_wq, b_wo):
    f32 = np.float32
    x = np.asarray(x, f32)
    rel_bias = np.asarray(rel_bias, f32)
    shared = {
        "ffn_pre_wi": np.asarray(ffn_pre_wi, f32), "ffn_pre_wo": np.asarray(ffn_pre_wo, f32),
        "ffn_post_wi": np.asarray(ffn_post_wi, f32), "ffn_post_wo": np.asarray(ffn_post_wo, f32),
        "a_wqkv": np.asarray(a_wqkv, f32), "a_wo": np.asarray(a_wo, f32), "b_wkv": np.asarray(b_wkv, f32),
        "b_wq": np.asarray(b_wq, f32), "b_wo": np.asarray(b_wo, f32),
        "ffn_pre_norm": _cols(ffn_pre_norm), "mix_norm": _cols(mix_norm), "ffn_post_norm": _cols(ffn_post_norm),
        "kv_norm": _col(kv_norm),
        "gqk": np.ascontiguousarray(np.stack([np.asarray(a_q_norm, f32), np.asarray(a_k_norm, f32)], axis=2)),
        "lam": np.ascontiguousarray(np.broadcast_to(np.asarray(a_lambda, f32).reshape(NA, 1, 512), (NA, 128, 512))),
        "gsub": _cols(a_subln),
        "biasM": _bias_master(rel_bias),
        "b31": np.ascontiguousarray(np.broadcast_to(rel_bias[31].reshape(1, NH), (128, NH))),
        "cmat": _cmat(),
    }
    in_maps = []
    for c in range(NCORES):
        b, r = divmod(c, 4)
        m = dict(shared)
        m["x"] = np.ascontiguousarray(x[b][_tok_idx(r)].T)
        m["sel"] = _sel_table(r)
        m["m01"], m["negm"] = _b_masks(r)
        in_maps.append(m)
    nc = _prog("fused", build_fused_prog)
    out = _run(nc, in_maps)
    y = np.empty((NB, SEQ, D), f32)
    for c in range(NCORES):
        b, r = divmod(c, 4)
        y[b][_tok_idx(r)] = out[c]["y"].T
    return y
```

```python
import contextlib
import math
import numpy as np
import ml_dtypes
import concourse.bass as bass
import concourse.mybir as mybir
from concourse.bass_utils import run_bass_kernel_spmd

F32 = mybir.dt.float32
BF16 = mybir.dt.bfloat16
U8 = mybir.dt.uint8
AF = mybir.ActivationFunctionType
ALU = mybir.AluOpType
AX = mybir.AxisListType

D = 2048
DC = D // 128
SEQ = 8192
NB = 2
DEPTH = 4
NA = 2
DFF = 5504
FC = DFF // 128
HD = 128
NH = 16
TT = 512
NT = 4
TL = TT * NT
EPS = 1e-6
NEG = -30000.0
NCORES = 8

PE, ACT, DVE, POOL, SP = "pe", "act", "dve", "pool", "sp"


class Res:
    __slots__ = ("name", "w", "rd", "rdma")

    def __init__(self, name):
        self.name = name
        self.w = None
        self.rd = {}
        self.rdma = []


class Op:
    __slots__ = ("eng", "fn", "deps", "dkey", "signal", "ticket", "idx", "inc")


class Prog:
    def __init__(self, nc):
        self.nc = nc
        self.ops = []
        self.res = {}
        self.last = {}
        self.pending_dma = []
        self.bar = {}
        self.dma_keys = []

    def R(self, key):
        r = self.res.get(key)
        if r is None:
            r = Res(key)
            self.res[key] = r
        return r

    def op(self, eng, fn, reads=(), writes=(), dma=None, strict=False, inc=16, bg=False):
        o = Op()
        o.eng = eng
        o.fn = fn
        o.dkey = dma
        o.signal = dma is not None
        o.ticket = 0
        o.inc = inc
        o.idx = len(self.ops)
        deps = set()
        b = self.bar.pop(eng, None)
        if b:
            deps |= b
        for k in reads:
            r = self.R(k)
            if r.w is not None:
                deps.add(r.w)
        for k in writes:
            r = self.R(k)
            if r.w is not None:
                deps.add(r.w)
            deps.update(r.rd.values())
            deps.update(r.rdma)
        for k in reads:
            r = self.R(k)
            if dma is not None:
                r.rdma.append(o.idx)
            else:
                r.rd[eng] = o.idx
        for k in writes:
            r = self.R(k)
            r.w = o.idx
            r.rd = {}
            r.rdma = []
        o.deps = [d for d in deps if strict or self.ops[d].dkey is not None or self.ops[d].eng != eng]
        self.ops.append(o)
        if dma is None:
            self.last[eng] = o.idx
        if dma is not None:
            if not bg:
                self.pending_dma.append(o.idx)
            if dma not in self.dma_keys:
                self.dma_keys.append(dma)
        return o

    def barrier(self):
        deps = set(self.last.values()) | set(self.pending_dma)
        self.pending_dma = []
        for e in (PE, ACT, DVE, POOL, SP):
            self.bar[e] = set(deps) | self.bar.get(e, set())

    def emit(self):
        nc = self.nc
        ops = self.ops
        for o in ops:
            for d in o.deps:
                ops[d].signal = True
        cnt = {}
        for o in ops:
            if o.dkey is not None:
                k = ("dma", o.dkey)
                cnt[k] = cnt.get(k, 0) + o.inc
                o.ticket = cnt[k]
            elif o.signal:
                k = ("eng", o.eng)
                cnt[k] = cnt.get(k, 0) + 1
                o.ticket = cnt[k]
        final_dma = {k[1]: v for k, v in cnt.items() if k[0] == "dma"}
        with contextlib.ExitStack() as es:
            sems = {}
            for e in (PE, ACT, DVE, POOL, SP):
                sems[("eng", e)] = es.enter_context(nc.semaphore("s_" + e))
            for i, k in enumerate(self.dma_keys):
                sems[("dma", k)] = es.enter_context(nc.semaphore("d%d" % i))
            block = es.enter_context(nc.Block())

            def run(engname):
                def body(eng):
                    waited = {}
                    for o in ops:
                        if o.eng != engname:
                            continue
                        need = {}
                        for d in o.deps:
                            p = ops[d]
                            k = ("dma", p.dkey) if p.dkey is not None else ("eng", p.eng)
                            if p.ticket > need.get(k, 0):
                                need[k] = p.ticket
                        for k, v in need.items():
                            if waited.get(k, 0) < v:
                                eng.wait_ge(sems[k], v)
                                waited[k] = v
                        ins = o.fn(eng)
                        if o.dkey is not None:
                            if o.inc == 16:
                                ins.then_inc(sems[("dma", o.dkey)], 16)
                            else:
                                ins.then_inc(sems[("dma", o.dkey)])
                        elif o.signal:
                            ins.then_inc(sems[("eng", o.eng)], 1)
                    if engname == SP:
                        for k, v in final_dma.items():
                            if waited.get(("dma", k), 0) < v:
                                eng.wait_ge(sems[("dma", k)], v)
                return body

            block.tensor(run(PE))
            block.scalar(run(ACT))
            block.vector(run(DVE))
            block.gpsimd(run(POOL))
            block.sync(run(SP))


class Sbuf:
    def __init__(self, big, cap):
        self.big = big
        self.cap = cap
        self.off = 0

    def reset(self, off=0):
        self.off = off

    def alloc(self, free_elems, dt):
        sz = {F32: 4, BF16: 2}[dt]
        nbytes = (free_elems * sz + 63) // 64 * 64
        assert self.off + nbytes <= self.cap, ("SBUF overflow", self.off, nbytes, self.cap)
        v = self.big[:, self.off:self.off + free_elems * sz].bitcast(dt)
        self.off += nbytes
        return v


class Ctx:
    pass


def mk_weight_scratch(cx, name, K, ncols, nblocks):
    return cx.nc.dram_tensor(name, [nblocks, 128, K // 128, ncols], BF16).ap()


def emit_wcvt(cx, W, scr, blocks, ncols, tag, par=0, bg=False):
    p = cx.p
    Wv = W.rearrange("(kc p) n -> p kc n", p=128)
    for bi, c0 in enumerate(blocks):
        src = Wv[:, :, c0:c0 + ncols]
        dst = scr[bi]
        p.op(POOL, (lambda e, s=src, d=dst: e.dma_start(out=d, in_=s)),
             reads=(), writes=((tag, bi),), dma=("cv", par, bi % 4), bg=bg)


def stage_norm(cx, x_tile, g_tile, hT, ones_f, ssq_ps, sq_tiles, rstd, keys, n_dc=DC, dim=D):
    p = cx.p
    kx, kh, kssq, krstd, ksq, kg = keys
    for dc in range(n_dc):
        sq = sq_tiles[dc % 2]
        p.op(ACT, (lambda e, sq=sq, dc=dc: e.activation(out=sq, in_=x_tile[:, dc, :], func=AF.Square)),
             reads=(kx,), writes=((ksq, dc % 2),))
        p.op(PE, (lambda e, sq=sq, dc=dc: e.matmul(ssq_ps, lhsT=ones_f, rhs=sq, start=(dc == 0), stop=(dc == n_dc - 1))),
             reads=((ksq, dc % 2),), writes=(kssq,))
    p.op(ACT, lambda e: e.activation(out=rstd, in_=ssq_ps, func=AF.Sqrt, scale=1.0 / dim, bias=cx.eps_t),
         reads=(kssq, "eps_t"), writes=(krstd,))
    p.op(DVE, lambda e: e.reciprocal(out=rstd, in_=rstd), reads=(krstd,), writes=(krstd,))
    for dc in range(n_dc):
        eng = DVE
        p.op(eng, (lambda e, dc=dc: e.scalar_tensor_tensor(out=hT[:, dc, :], in0=x_tile[:, dc, :], scalar=g_tile[:, dc:dc + 1],
                                                           in1=rstd, op0=ALU.mult, op1=ALU.mult)),
             reads=(kx, krstd, kg), writes=((kh, dc),))


def stage_ffn(cx, x_in, x_out, g_dram, wi_scr, wo_scr, tag, wi_tag=None, wo_tag=None):
    p, nc, sb = cx.p, cx.nc, cx.sb
    p.barrier()
    sb.reset(cx.sb_base)
    xin_v = x_in.rearrange("(dc p) t -> p dc t", p=128)
    xout_v = x_out.rearrange("(dc p) t -> p dc t", p=128)
    g_tile = sb.alloc(DC, F32)
    xt = [sb.alloc(DC * TT, F32).rearrange("p (a b) -> p a b", a=DC) for _ in range(2)]
    hT = [sb.alloc(DC * TT, BF16).rearrange("p (a b) -> p a b", a=DC) for _ in range(2)]
    aT = sb.alloc(FC * TT, BF16).rearrange("p (a b) -> p a b", a=FC)
    NWI = 3
    wi_t = [sb.alloc(2 * DC * 128, BF16).rearrange("p (g k c) -> p g k c", g=2, k=DC) for _ in range(NWI)]
    wo_t = [sb.alloc(FC * 128, BF16).rearrange("p (k c) -> p k c", k=FC) for _ in range(2)]
    sq_t = [sb.alloc(TT, F32) for _ in range(2)]
    rstd = sb.alloc(TT, F32)
    sg_t = [sb.alloc(TT, F32) for _ in range(2)]
    yo_t = [sb.alloc(TT, F32) for _ in range(2)]
    ps = cx.psum
    gate_ps = [ps[0], ps[1]]
    up_ps = [ps[2], ps[3]]
    y_ps = [ps[4], ps[5]]
    ssq_ps = ps[6]
    T = tag
    wi_tag = wi_tag or (T + "wi")
    wo_tag = wo_tag or (T + "wo")
    allw = tuple((wi_tag, i) for i in range(2 * FC)) + tuple((wo_tag, i) for i in range(DC))
    p.op(SP, lambda e: e.dma_start(out=g_tile, in_=g_dram),
         reads=allw, writes=((T, "g"),), dma=("g",))
    wi_cnt = 0
    wo_cnt = 0
    for ti in range(NT):
        xb = ti % 2
        cs = slice(ti * TT, (ti + 1) * TT)
        p.op(SP, (lambda e, xb=xb, cs=cs: e.dma_start(out=xt[xb], in_=xin_v[:, :, cs])),
             writes=((T, "x", xb),), dma=("x", xb))
        stage_norm(cx, xt[xb], g_tile, hT[xb], cx.ones_f, ssq_ps, sq_t, rstd,
                   ((T, "x", xb), (T, "h", xb), (T, "ssq"), (T, "rstd"), (T, "sq"), (T, "g")))
        hkeys = tuple(((T, "h", xb), dc) for dc in range(DC))
        for fc in range(FC):
            wb = wi_cnt % NWI
            wi_cnt += 1
            p.op(SP, (lambda e, wb=wb, fc=fc: e.dma_start(out=wi_t[wb], in_=wi_scr[2 * fc:2 * fc + 2].rearrange("g p k c -> p g k c"))),
                 reads=((wi_tag, 2 * fc), (wi_tag, 2 * fc + 1)), writes=((T, "wi", wb),), dma=("wi", wb))
            pb = fc % 2
            for dc in range(DC):
                p.op(PE, (lambda e, wb=wb, dc=dc, pb=pb, xb=xb: e.matmul(gate_ps[pb], lhsT=wi_t[wb][:, 0, dc, :], rhs=hT[xb][:, dc, :],
                                                                  start=(dc == 0), stop=(dc == DC - 1))),
                     reads=((T, "wi", wb), ((T, "h", xb), dc)), writes=((T, "gate", pb),))
            for dc in range(DC):
                p.op(PE, (lambda e, wb=wb, dc=dc, pb=pb, xb=xb: e.matmul(up_ps[pb], lhsT=wi_t[wb][:, 1, dc, :], rhs=hT[xb][:, dc, :],
                                                                  start=(dc == 0), stop=(dc == DC - 1))),
                     reads=((T, "wi", wb), ((T, "h", xb), dc)), writes=((T, "up", pb),))
            p.op(ACT, (lambda e, pb=pb: e.activation(out=sg_t[pb], in_=gate_ps[pb], func=AF.Silu)),
                 reads=((T, "gate", pb),), writes=((T, "sg", pb),))
            p.op(DVE, (lambda e, pb=pb, fc=fc: e.tensor_tensor(out=aT[:, fc, :], in0=sg_t[pb], in1=up_ps[pb], op=ALU.mult)),
                 reads=((T, "sg", pb), (T, "up", pb)), writes=((T, "a", fc),))
        for dco in range(DC):
            wb = wo_cnt % 2
            wo_cnt += 1
            p.op(SP, (lambda e, wb=wb, dco=dco: e.dma_start(out=wo_t[wb], in_=wo_scr[dco])),
                 reads=((wo_tag, dco),), writes=((T, "wo", wb),), dma=("wo", wb))
            pb = dco % 2
            for fc in range(FC):
                p.op(PE, (lambda e, wb=wb, fc=fc, pb=pb: e.matmul(y_ps[pb], lhsT=wo_t[wb][:, fc, :], rhs=aT[:, fc, :],
                                                                  start=(fc == 0), stop=(fc == FC - 1))),
                     reads=((T, "wo", wb), (T, "a", fc)), writes=((T, "y", pb),))
            p.op(DVE, (lambda e, pb=pb, dco=dco, xb=xb: e.scalar_tensor_tensor(out=yo_t[pb], in0=y_ps[pb], scalar=0.5, in1=xt[xb][:, dco, :],
                                                                                op0=ALU.mult, op1=ALU.add)),
                 reads=((T, "y", pb), (T, "x", xb)), writes=((T, "yo", pb),))
            p.op(POOL, (lambda e, pb=pb, dco=dco, cs=cs: e.dma_start(out=xout_v[:, dco, cs], in_=yo_t[pb])),
                 reads=((T, "yo", pb),), writes=((T, "xout", ti, dco),), dma=("yo", pb))


def stage_ffn2(cx, x_in, x_out, g_dram, wi_scr, wo_scr, tag, wi_tag=None, wo_tag=None):
    p, nc, sb = cx.p, cx.nc, cx.sb
    p.barrier()
    sb.reset(cx.sb_base)
    T = tag
    TB = 1024
    NP = 256
    xin_v = x_in.rearrange("(dc p) t -> p dc t", p=128)
    xout_v = x_out.rearrange("(dc p) t -> p dc t", p=128)
    g_tile = sb.alloc(DC, F32)
    xt = sb.alloc(DC * NP, F32).rearrange("p (a b) -> p a b", a=DC)
    hT = sb.alloc(DC * TB, BF16).rearrange("p (a b) -> p a b", a=DC)
    aT = sb.alloc(FC * TB, BF16).rearrange("p (a b) -> p a b", a=FC)
    NWI = 3
    wi_t = [sb.alloc(2 * DC * 128, BF16).rearrange("p (g k c) -> p g k c", g=2, k=DC) for _ in range(NWI)]
    wo_t = [sb.alloc(FC * 128, BF16).rearrange("p (k c) -> p k c", k=FC) for _ in range(2)]
    sq_t = [sb.alloc(NP, F32) for _ in range(2)]
    rstd = sb.alloc(NP, F32)
    sg_t = [sb.alloc(TT, F32) for _ in range(2)]
    yo_t = [sb.alloc(TT, F32) for _ in range(2)]
    xr_t = [sb.alloc(TT, F32) for _ in range(2)]
    ps = cx.psum
    gate_ps = [ps[0], ps[1]]
    up_ps = [ps[2], ps[3]]
    y_ps = [ps[4], ps[5]]
    ssq_ps = ps[6]
    wi_tag = wi_tag or (T + "wi")
    wo_tag = wo_tag or (T + "wo")
    allw = tuple((wi_tag, i) for i in range(2 * FC)) + tuple((wo_tag, i) for i in range(DC))
    p.op(SP, lambda e: e.dma_start(out=g_tile, in_=g_dram), reads=allw, writes=((T, "g"),), dma=("g",))
    wi_cnt = 0
    wo_cnt = 0
    pcnt = 0
    ycnt = 0
    for ti in range(TL // TB):
        t0 = ti * TB
        for pi in range(TB // NP):
            c0 = pi * NP
            p.op(SP, (lambda e, c0=c0, t0=t0: e.dma_start(out=xt, in_=xin_v[:, :, t0 + c0:t0 + c0 + NP])),
                 writes=((T, "x"),), dma=("x", 0))
            for dc in range(DC):
                sq = sq_t[dc % 2]
                p.op(ACT, (lambda e, sq=sq, dc=dc: e.activation(out=sq, in_=xt[:, dc, :], func=AF.Square)),
                     reads=((T, "x"),), writes=((T, "sq", dc % 2),))
                p.op(PE, (lambda e, sq=sq, dc=dc: e.matmul(ssq_ps[:, 0:NP], lhsT=cx.ones_f, rhs=sq, start=(dc == 0), stop=(dc == DC - 1))),
                     reads=((T, "sq", dc % 2), "ones_f"), writes=((T, "ssq"),))
            p.op(ACT, lambda e: e.activation(out=rstd, in_=ssq_ps[:, 0:NP], func=AF.Sqrt, scale=1.0 / D, bias=cx.eps_t),
                 reads=((T, "ssq"), "eps_t"), writes=((T, "rstd"),))
            p.op(DVE, lambda e: e.reciprocal(out=rstd, in_=rstd), reads=((T, "rstd"),), writes=((T, "rstd"),))
            for dc in range(DC):
                p.op(DVE, (lambda e, dc=dc, c0=c0: e.scalar_tensor_tensor(out=hT[:, dc, c0:c0 + NP], in0=xt[:, dc, :], scalar=g_tile[:, dc:dc + 1],
                                                                         in1=rstd, op0=ALU.mult, op1=ALU.mult)),
                     reads=((T, "x"), (T, "rstd"), (T, "g")), writes=((T, "h", pi // 2),))
        for fc in range(FC):
            wb = wi_cnt % NWI
            wi_cnt += 1
            p.op(SP, (lambda e, wb=wb, fc=fc: e.dma_start(out=wi_t[wb], in_=wi_scr[2 * fc:2 * fc + 2].rearrange("g p k c -> p g k c"))),
                 reads=((wi_tag, 2 * fc), (wi_tag, 2 * fc + 1)), writes=((T, "wi", wb),), dma=("wi", wb))
            for sub in range(2):
                pb = pcnt % 2
                pcnt += 1
                hs = slice(sub * TT, (sub + 1) * TT)
                for dc in range(DC):
                    p.op(PE, (lambda e, wb=wb, dc=dc, pb=pb, hs=hs: e.matmul(gate_ps[pb], lhsT=wi_t[wb][:, 0, dc, :], rhs=hT[:, dc, hs],
                                                                            start=(dc == 0), stop=(dc == DC - 1))),
                         reads=((T, "wi", wb), (T, "h", sub)), writes=((T, "gate", pb),))
                for dc in range(DC):
                    p.op(PE, (lambda e, wb=wb, dc=dc, pb=pb, hs=hs: e.matmul(up_ps[pb], lhsT=wi_t[wb][:, 1, dc, :], rhs=hT[:, dc, hs],
                                                                            start=(dc == 0), stop=(dc == DC - 1))),
                         reads=((T, "wi", wb), (T, "h", sub)), writes=((T, "up", pb),))
                p.op(ACT, (lambda e, pb=pb: e.activation(out=sg_t[pb], in_=gate_ps[pb], func=AF.Silu)),
                     reads=((T, "gate", pb),), writes=((T, "sg", pb),))
                p.op(DVE, (lambda e, pb=pb, fc=fc, hs=hs: e.tensor_tensor(out=aT[:, fc, hs], in0=sg_t[pb], in1=up_ps[pb], op=ALU.mult)),
                     reads=((T, "sg", pb), (T, "up", pb)), writes=((T, "a", fc, sub),))
        for dco in range(DC):
            wb = wo_cnt % 2
            wo_cnt += 1
            p.op(SP, (lambda e, wb=wb, dco=dco: e.dma_start(out=wo_t[wb], in_=wo_scr[dco])),
                 reads=((wo_tag, dco),), writes=((T, "wo", wb),), dma=("wo", wb))
            for sub in range(2):
                pb = ycnt % 2
                ycnt += 1
                hs = slice(sub * TT, (sub + 1) * TT)
                cs = slice(t0 + sub * TT, t0 + (sub + 1) * TT)
                p.op(SP, (lambda e, pb=pb, dco=dco, cs=cs: e.dma_start(out=xr_t[pb], in_=xin_v[:, dco, cs])),
                     writes=((T, "xr", pb),), dma=("xr", pb))
                for fc in range(FC):
                    p.op(PE, (lambda e, wb=wb, fc=fc, pb=pb, hs=hs: e.matmul(y_ps[pb], lhsT=wo_t[wb][:, fc, :], rhs=aT[:, fc, hs],
                                                                            start=(fc == 0), stop=(fc == FC - 1))),
                         reads=((T, "wo", wb), (T, "a", fc, sub)), writes=((T, "y", pb),))
                p.op(DVE, (lambda e, pb=pb: e.scalar_tensor_tensor(out=yo_t[pb], in0=y_ps[pb], scalar=0.5, in1=xr_t[pb],
                                                                   op0=ALU.mult, op1=ALU.add)),
                     reads=((T, "y", pb), (T, "xr", pb)), writes=((T, "yo", pb),))
                p.op(POOL, (lambda e, pb=pb, dco=dco, cs=cs: e.dma_start(out=xout_v[:, dco, cs], in_=yo_t[pb])),
                     reads=((T, "yo", pb),), writes=((T, "xout", ti, dco, sub),), dma=("yo", pb))


def load_norm_tile(cx, T, x_v, ti, xt, hT, g_tile, sq_t, rstd, ssq_ps, xb):
    p = cx.p
    cs = slice(ti * TT, (ti + 1) * TT)
    p.op(SP, (lambda e, xb=xb, cs=cs: e.dma_start(out=xt[xb], in_=x_v[:, :, cs])),
         writes=((T, "x", xb),), dma=("x", xb))
    stage_norm(cx, xt[xb], g_tile, hT[xb], cx.ones_f, ssq_ps, sq_t, rstd,
               ((T, "x", xb), (T, "h", xb), (T, "ssq"), (T, "rstd"), (T, "sq"), (T, "g")))


def stage_qkv(cx, x_in, g_dram, tag, fm_scr, fm_specs, v_scr, v_out, fm_tag=None, v_tag=None, v_blocked=False, post_tile=None, nfm_blocks=0):
    p, sb = cx.p, cx.sb
    p.barrier()
    sb.reset(cx.sb_base)
    T = tag
    x_v = x_in.rearrange("(dc p) t -> p dc t", p=128)
    g_tile = sb.alloc(DC, F32)
    xt = [sb.alloc(DC * TT, F32).rearrange("p (a b) -> p a b", a=DC) for _ in range(2)]
    hT = [sb.alloc(DC * TT, BF16).rearrange("p (a b) -> p a b", a=DC) for _ in range(2)]
    NW = 3
    w_t = [sb.alloc(DC * 128, BF16).rearrange("p (k c) -> p k c", k=DC) for _ in range(NW)]
    wv_t = [sb.alloc(DC * 512, BF16).rearrange("p (k c) -> p k c", k=DC) for _ in range(2)]
    sq_t = [sb.alloc(TT, F32) for _ in range(2)]
    rstd = sb.alloc(TT, F32)
    sq2 = [sb.alloc(TT, F32) for _ in range(2)]
    rs2 = [sb.alloc(TT, F32) for _ in range(2)]
    st_t = [sb.alloc(TT, BF16) for _ in range(3)]
    ps = cx.psum
    acc_ps = [ps[0], ps[1], ps[2]]
    ssq2_ps = [ps[3], ps[4]]
    ssq_ps = ps[6]
    fm_tag = fm_tag or (T + "wfm")
    v_tag = v_tag or (T + "wv")
    allw = tuple((fm_tag, i) for i in range(nfm_blocks)) + (tuple((v_tag, i) for i in range(4)) if v_scr is not None else ())
    p.op(SP, lambda e: e.dma_start(out=g_tile, in_=g_dram), reads=allw, writes=((T, "g"),), dma=("g",))
    wcnt = 0
    scnt = 0
    vcnt = 0
    for ti in range(NT):
        xb = ti % 2
        load_norm_tile(cx, T, x_v, ti, xt, hT, g_tile, sq_t, rstd, ssq_ps, xb)
        for si, (bi, dst_fn, gcol, scale) in enumerate(fm_specs):
            wb = wcnt % NW
            ab = wcnt % 3
            wcnt += 1
            p.op(SP, (lambda e, wb=wb, bi=bi: e.dma_start(out=w_t[wb], in_=fm_scr[bi])),
                 reads=((fm_tag, bi),), writes=((T, "w", wb),), dma=("wi", wb))
            for dc in range(DC):
                p.op(PE, (lambda e, wb=wb, dc=dc, ab=ab, xb=xb: e.matmul(acc_ps[ab], lhsT=w_t[wb][:, dc, :], rhs=hT[xb][:, dc, :],
                                                                        start=(dc == 0), stop=(dc == DC - 1))),
                     reads=((T, "w", wb), ((T, "h", xb), dc)), writes=((T, "acc", ab),))
            sb_i = scnt % 3
            scnt += 1
            if gcol is not None:
                qb = si % 2
                p.op(ACT, (lambda e, ab=ab, qb=qb: e.activation(out=sq2[qb], in_=acc_ps[ab], func=AF.Square)),
                     reads=((T, "acc", ab),), writes=((T, "sq2", qb),))
                p.op(PE, (lambda e, qb=qb: e.matmul(ssq2_ps[qb], lhsT=cx.ones_f, rhs=sq2[qb], start=True, stop=True)),
                     reads=((T, "sq2", qb), "ones_f"), writes=((T, "ssq2", qb),))
                p.op(ACT, (lambda e, qb=qb: e.activation(out=rs2[qb], in_=ssq2_ps[qb], func=AF.Sqrt, scale=1.0 / HD, bias=cx.eps_t)),
                     reads=((T, "ssq2", qb), "eps_t"), writes=((T, "rs2", qb),))
                p.op(DVE, (lambda e, qb=qb: e.reciprocal(out=rs2[qb], in_=rs2[qb])), reads=((T, "rs2", qb),), writes=((T, "rs2", qb),))
                p.op(DVE, (lambda e, ab=ab, qb=qb, sb_i=sb_i, gcol=gcol: e.scalar_tensor_tensor(
                    out=st_t[sb_i], in0=acc_ps[ab], scalar=gcol, in1=rs2[qb], op0=ALU.mult, op1=ALU.mult)),
                     reads=((T, "acc", ab), (T, "rs2", qb), (T, "gains")), writes=((T, "st", sb_i),))
            else:
                p.op(ACT, (lambda e, ab=ab, sb_i=sb_i, scale=scale: e.activation(out=st_t[sb_i], in_=acc_ps[ab], func=AF.Copy, scale=float(scale))),
                     reads=((T, "acc", ab),), writes=((T, "st", sb_i),))
            p.op(POOL, (lambda e, sb_i=sb_i, dst_fn=dst_fn, ti=ti: e.dma_start(out=dst_fn(ti), in_=st_t[sb_i])),
                 reads=((T, "st", sb_i),), writes=((T, "fmout", si, ti),), dma=("st", sb_i))
        if v_scr is not None:
            for eb in range(4):
                vb = vcnt % 2
                vcnt += 1
                p.op(SP, (lambda e, vb=vb, eb=eb: e.dma_start(out=wv_t[vb], in_=v_scr[eb])),
                     reads=((v_tag, eb),), writes=((T, "wv", vb),), dma=("wo", vb))
                for tb in range(4):
                    ab = wcnt % 3
                    wcnt += 1
                    for dc in range(DC):
                        p.op(PE, (lambda e, vb=vb, dc=dc, ab=ab, xb=xb, tb=tb: e.matmul(
                            acc_ps[ab], lhsT=hT[xb][:, dc, tb * 128:(tb + 1) * 128], rhs=wv_t[vb][:, dc, :],
                            start=(dc == 0), stop=(dc == DC - 1))),
                             reads=((T, "wv", vb), ((T, "h", xb), dc)), writes=((T, "acc", ab),))
                    sb_i = scnt % 3
                    scnt += 1
                    p.op(ACT, (lambda e, ab=ab, sb_i=sb_i: e.activation(out=st_t[sb_i], in_=acc_ps[ab], func=AF.Copy)),
                         reads=((T, "acc", ab),), writes=((T, "st", sb_i),))
                    p.op(POOL, (lambda e, sb_i=sb_i, ti=ti, tb=tb, eb=eb: e.dma_start(
                        out=(v_out[ti][eb][tb * 128:(tb + 1) * 128, :] if v_blocked else v_out[ti][tb * 128:(tb + 1) * 128, eb * 512:(eb + 1) * 512]), in_=st_t[sb_i])),
                         reads=((T, "st", sb_i),), writes=((T, "vout", ti, tb, eb),), dma=("st", sb_i))
        if post_tile is not None:
            post_tile(ti)


def stage_wo(cx, x_in, x_out, oT, wo_scr, tag, w_tag=None):
    p, sb = cx.p, cx.sb
    p.barrier()
    sb.reset(cx.sb_base)
    T = tag
    xin_v = x_in.rearrange("(dc p) t -> p dc t", p=128)
    xout_v = x_out.rearrange("(dc p) t -> p dc t", p=128)
    o_v = oT.rearrange("(dc p) t -> p dc t", p=128)
    xt = [sb.alloc(DC * TT, F32).rearrange("p (a b) -> p a b", a=DC) for _ in range(2)]
    ot = [sb.alloc(DC * TT, BF16).rearrange("p (a b) -> p a b", a=DC) for _ in range(2)]
    w_t = [sb.alloc(DC * 128, BF16).rearrange("p (k c) -> p k c", k=DC) for _ in range(3)]
    yo_t = [sb.alloc(TT, F32) for _ in range(2)]
    ps = cx.psum
    y_ps = [ps[0], ps[1]]
    wcnt = 0
    w_tag = w_tag or (T + "w")
    allw = tuple((w_tag, i) for i in range(DC))
    for ti in range(NT):
        xb = ti % 2
        cs = slice(ti * TT, (ti + 1) * TT)
        p.op(SP, (lambda e, xb=xb, cs=cs: e.dma_start(out=xt[xb], in_=xin_v[:, :, cs])),
             reads=(allw if ti == 0 else ()), writes=((T, "x", xb),), dma=("x", xb))
        p.op(SP, (lambda e, xb=xb, cs=cs: e.dma_start(out=ot[xb], in_=o_v[:, :, cs])),
             writes=((T, "o", xb),), dma=("o", xb))
        for dco in range(DC):
            wb = wcnt % 3
            pb = wcnt % 2
            wcnt += 1
            p.op(SP, (lambda e, wb=wb, dco=dco: e.dma_start(out=w_t[wb], in_=wo_scr[dco])),
                 reads=((w_tag, dco),), writes=((T, "w", wb),), dma=("wi", wb))
            for ec in range(DC):
                p.op(PE, (lambda e, wb=wb, ec=ec, pb=pb, xb=xb: e.matmul(y_ps[pb], lhsT=w_t[wb][:, ec, :], rhs=ot[xb][:, ec, :],
                                                                        start=(ec == 0), stop=(ec == DC - 1))),
                     reads=((T, "w", wb), (T, "o", xb)), writes=((T, "y", pb),))
            p.op(DVE, (lambda e, pb=pb, dco=dco, xb=xb: e.tensor_tensor(out=yo_t[pb], in0=y_ps[pb], in1=xt[xb][:, dco, :], op=ALU.add)),
                 reads=((T, "y", pb), (T, "x", xb)), writes=((T, "yo", pb),))
            p.op(POOL, (lambda e, pb=pb, dco=dco, cs=cs: e.dma_start(out=xout_v[:, dco, cs], in_=yo_t[pb])),
                 reads=((T, "yo", pb),), writes=((T, "xout", ti, dco),), dma=("yo", pb))


def stage_attn_a(cx, qT, Kg, Vg, biasM, sel_d, b31_d, lam_d, gsub_d, oT, lambda_init, tag, dbg=None, cc=False):
    p, sb = cx.p, cx.sb
    p.barrier()
    sb.reset(cx.sb_base)
    T = tag
    ps = cx.psum
    s_ps = [ps[0], ps[1], ps[2], ps[7]]
    o_ps = [ps[3], ps[4]]
    den_ps = ps[5]
    ssq_ps = ps[6]
    sel_t = sb.alloc(17 * 4, F32).rearrange("p (b c) -> p b c", c=4)
    b31_t = sb.alloc(NH, F32)
    lam_t = sb.alloc(4 * 128, F32).rearrange("p (a b) -> p a b", a=4)
    lprod = sb.alloc(128, F32)
    lsum = sb.alloc(2, F32)
    nlam = sb.alloc(1, F32)
    gsub_t = sb.alloc(2, F32)
    ccol = [sb.alloc(17, F32) for _ in range(2)]
    Kt = [sb.alloc(16 * TT, BF16).rearrange("p (g t) -> p g t", g=16) for _ in range(2)]
    Vt = sb.alloc(64 * 256, BF16).rearrange("p (b e) -> p b e", b=64)
    Qt = sb.alloc(2 * TL, BF16).rearrange("p (m t) -> p m t", m=2)
    Mt = [sb.alloc(1024, F32) for _ in range(2)]
    tmp_t = [sb.alloc(TT, F32) for _ in range(3)]
    pT_t = [sb.alloc(TT, BF16) for _ in range(6)]
    Oev = [sb.alloc(2 * TT, F32).rearrange("p (a b) -> p a b", a=2) for _ in range(2)]
    dev = [sb.alloc(TT, F32) for _ in range(2)]
    o_t = sb.alloc(2 * TT, F32).rearrange("p (a b) -> p a b", a=2)
    u_t = sb.alloc(TT, F32)
    sq_t = [sb.alloc(TT, F32) for _ in range(2)]
    rstd = sb.alloc(TT, F32)
    on_t = [sb.alloc(TT, BF16) for _ in range(2)]
    p.op(SP, lambda e: e.dma_start(out=sel_t, in_=sel_d.rearrange("p (b c) -> p b c", c=4)), writes=((T, "sel"),), dma=("c", 0))
    p.op(SP, lambda e: e.dma_start(out=b31_t, in_=b31_d), writes=((T, "b31"),), dma=("c", 1))
    p.op(SP, lambda e: e.dma_start(out=lam_t, in_=lam_d.rearrange("p (a b) -> p a b", a=4)), writes=((T, "lamv"),), dma=("c", 2))
    p.op(SP, lambda e: e.dma_start(out=gsub_t, in_=gsub_d), writes=((T, "gsub"),), dma=("c", 3))
    lprod2 = sb.alloc(128, F32)
    nlam0 = sb.alloc(1, F32)
    p.op(DVE, lambda e: e.tensor_tensor(out=lprod, in0=lam_t[:, 0, :], in1=lam_t[:, 1, :], op=ALU.mult),
         reads=((T, "lamv"),), writes=((T, "lp0"),))
    p.op(DVE, lambda e: e.reduce_sum(out=lsum[:, 0:1], in_=lprod, axis=AX.X), reads=((T, "lp0"),), writes=((T, "ls0"),), strict=True)
    p.op(DVE, lambda e: e.tensor_tensor(out=lprod2, in0=lam_t[:, 2, :], in1=lam_t[:, 3, :], op=ALU.mult),
         reads=((T, "lamv"),), writes=((T, "lp1"),))
    p.op(DVE, lambda e: e.reduce_sum(out=lsum[:, 1:2], in_=lprod2, axis=AX.X), reads=((T, "lp1"),), writes=((T, "ls1"),), strict=True)
    p.op(ACT, lambda e: e.activation(out=lsum, in_=lsum, func=AF.Exp), reads=((T, "ls0"), (T, "ls1")), writes=((T, "lsum"),), strict=True)
    p.op(DVE, lambda e: e.tensor_tensor(out=nlam0, in0=lsum[:, 1:2], in1=lsum[:, 0:1], op=ALU.subtract),
         reads=((T, "lsum"),), writes=((T, "nlam0"),))
    p.op(DVE, lambda e: e.tensor_scalar(out=nlam, in0=nlam0, scalar1=-float(lambda_init), scalar2=None, op0=ALU.add),
         reads=((T, "nlam0"),), writes=((T, "nlam"),), strict=True)
    p.op(DVE, lambda e: e.tensor_scalar(out=gsub_t, in0=gsub_t, scalar1=float(1.0 - lambda_init), scalar2=None, op0=ALU.mult),
         reads=((T, "gsub"),), writes=((T, "gsub"),), strict=True)
    scnt = 0
    pcnt = 0
    tcnt = 0
    for hd in (range(NH // 2) if dbg is None else [0]):
        for m in range(2):
            h = 2 * hd + m
            if cc:
                for Jl in range(NT):
                    p.op(SP, (lambda e, m=m, h=h, Jl=Jl: e.dma_start(out=Kt[m][:, 4 * Jl:4 * Jl + 4, :],
                                                                    in_=Kg[Jl, h // 4, :, (h % 4) * 128:(h % 4 + 1) * 128, :].rearrange("r d t -> d r t"))),
                         reads=(), writes=((T, "K", m, Jl),), dma=("k", m))
            else:
                p.op(SP, (lambda e, m=m, h=h: e.dma_start(out=Kt[m], in_=Kg[:, h].rearrange("g d t -> d g t"))),
                     reads=((T, "Kg"),), writes=((T, "K", m),), dma=("k", m))
            p.op(SP, (lambda e, m=m, h=h: e.dma_start(out=Qt[:, m, :], in_=qT[h])),
                 reads=((T, "qT"),), writes=((T, "Q", m),), dma=("q", m))
            p.op(SP, (lambda e, m=m, h=h: e.dma_start(out=Mt[m], in_=biasM[h])),
                 reads=(), writes=((T, "M", m),), dma=("m", m))
            p.op(DVE, (lambda e, m=m, h=h: e.scalar_tensor_tensor(out=ccol[m], in0=sel_t[:, :, 2], scalar=b31_t[:, h:h + 1],
                                                                 in1=sel_t[:, :, 3], op0=ALU.mult, op1=ALU.add)),
                 reads=((T, "sel"), (T, "b31")), writes=((T, "ccol", m),))
        if cc:
            for Jl in range(NT):
                p.op(SP, (lambda e, hd=hd, Jl=Jl: e.dma_start(out=Vt[:, 16 * Jl:16 * Jl + 16, :],
                                                            in_=Vg[Jl, hd // 2, :, :, (hd % 2) * 256:(hd % 2 + 1) * 256].rearrange("r (tb p) e -> p (r tb) e", p=128))),
                     reads=(), writes=((T, "V", Jl),), dma=("v",))
        else:
            p.op(SP, (lambda e, hd=hd: e.dma_start(out=Vt, in_=Vg[:, :, hd * 256:(hd + 1) * 256].rearrange("g (tb p) e -> p (g tb) e", p=128))),
                 reads=((T, "Vg"),), writes=((T, "V"),), dma=("v",))
        for J in (range(NT) if dbg is None else [0]):
            qs = slice(J * TT, (J + 1) * TT)
            for m in range(2):
                h = 2 * hd + m
                blocks = []
                for Jp in range(J + 1):
                    for rp in range(4):
                        for kbi in range(4):
                            if Jp == J:
                                si = 1 + rp * 4 + kbi
                            elif Jp == J - 1 and rp == 3 and kbi == 3:
                                si = 0
                            else:
                                si = None
                            blocks.append((4 * Jp + rp, kbi, si))
                nb = len(blocks)
                LA = 2
                pbs = {}

                def emit_s(bi, g, kbi, si):
                    nonlocal scnt, pcnt, tcnt
                    sbk = scnt % 4
                    scnt += 1
                    p.op(PE, (lambda e, sbk=sbk, m=m, g=g, kbi=kbi, qs=qs: e.matmul(
                        s_ps[sbk], lhsT=Kt[m][:, g, kbi * 128:(kbi + 1) * 128], rhs=Qt[:, m, qs], start=True, stop=True)),
                         reads=(((T, "K", m, g // 4) if cc else (T, "K", m)), (T, "Q", m)), writes=((T, "s", sbk),))
                    pb = pcnt % 6
                    pcnt += 1
                    pbs[bi] = pb
                    if si is None:
                        p.op(ACT, (lambda e, sbk=sbk, pb=pb, h=h: e.activation(out=pT_t[pb], in_=s_ps[sbk], func=AF.Exp, bias=b31_t[:, h:h + 1])),
                             reads=((T, "s", sbk), (T, "b31")), writes=((T, "pT", pb),))
                    else:
                        tb = tcnt % 3
                        tcnt += 1
                        ta = Mt[m][:, 384 - kbi * 128:384 - kbi * 128 + TT]
                        tbb = Mt[m][:, 512:1024]

                        def f_sel(e, tb=tb, sbk=sbk, si=si, ta=ta, tbb=tbb):
                            e.scalar_tensor_tensor(out=tmp_t[tb], in0=ta, scalar=sel_t[:, si, 0:1], in1=s_ps[sbk], op0=ALU.mult, op1=ALU.add)
                            return e.scalar_tensor_tensor(out=tmp_t[tb], in0=tbb, scalar=sel_t[:, si, 1:2], in1=tmp_t[tb], op0=ALU.mult, op1=ALU.add)
                        p.op(DVE, f_sel, reads=((T, "s", sbk), (T, "M", m), (T, "sel")), writes=((T, "tmp", tb),))
                        p.op(ACT, (lambda e, tb=tb, pb=pb, m=m, si=si: e.activation(out=pT_t[pb], in_=tmp_t[tb], func=AF.Exp, bias=ccol[m][:, si:si + 1])),
                             reads=((T, "tmp", tb), (T, "ccol", m)), writes=((T, "pT", pb),))

                def emit_pv(bi, g, kbi):
                    pb = pbs[bi]
                    vblk = g * 4 + kbi
                    for ec in range(2):
                        p.op(PE, (lambda e, pb=pb, vblk=vblk, ec=ec, bi=bi, nb=nb: e.matmul(
                            o_ps[ec], lhsT=Vt[:, vblk, ec * 128:(ec + 1) * 128], rhs=pT_t[pb], start=(bi == 0), stop=(bi == nb - 1))),
                             reads=((T, "pT", pb), ((T, "V", g // 4) if cc else (T, "V"))), writes=((T, "ops", ec),))
                    p.op(PE, (lambda e, pb=pb, bi=bi, nb=nb: e.matmul(den_ps, lhsT=cx.ones_b, rhs=pT_t[pb], start=(bi == 0), stop=(bi == nb - 1))),
                         reads=((T, "pT", pb), "ones_b"), writes=((T, "den"),))

                for step in range(nb + LA):
                    if step < nb:
                        emit_s(step, *blocks[step])
                    if step >= LA:
                        bg_, kb2, _ = blocks[step - LA]
                        emit_pv(step - LA, bg_, kb2)
                for ec in range(2):
                    p.op(ACT, (lambda e, m=m, ec=ec: e.activation(out=Oev[m][:, ec, :], in_=o_ps[ec], func=AF.Copy)),
                         reads=((T, "ops", ec),), writes=((T, "Oev", m, ec),))
                p.op(DVE, (lambda e, m=m: e.reciprocal(out=dev[m], in_=den_ps)), reads=((T, "den"),), writes=((T, "dev", m),))
            if dbg is not None:
                p.op(SP, lambda e: e.dma_start(out=dbg[5][:, 0:2], in_=lsum), reads=((T, "lsum"), (T, "nlam")), writes=(("dbg", "l"),), dma=("c", 0))
                p.op(SP, lambda e: e.dma_start(out=dbg[5][:, 2:3], in_=nlam, allow_slow_non_contiguous=True), reads=((T, "nlam"),), writes=(("dbg", "n"),), dma=("c", 0))
                p.op(SP, lambda e: e.dma_start(out=dbg[5][:, 4:6], in_=gsub_t), reads=((T, "gsub"),), writes=(("dbg", "g"),), dma=("c", 0))
                p.barrier()
                for m in range(1):
                    for ec in range(2):
                        p.op(SP, (lambda e, m=m, ec=ec: e.dma_start(out=dbg[m * 3 + ec], in_=Oev[m][:, ec, :])),
                             reads=((T, "Oev", m, ec),), writes=(("dbg", m, ec),), dma=("c", 0))
                    p.op(SP, (lambda e, m=m: e.dma_start(out=dbg[m * 3 + 2], in_=dev[m])),
                         reads=((T, "dev", m),), writes=(("dbg", m, 2),), dma=("c", 0))
                p.barrier()
            p.op(DVE, lambda e: e.tensor_scalar(out=dev[1], in0=dev[1], scalar1=nlam[:, 0:1], scalar2=None, op0=ALU.mult),
                 reads=((T, "dev", 1), (T, "nlam")), writes=((T, "dev", 1),))
            for ec in range(2):
                def f_comb(e, ec=ec):
                    e.tensor_tensor(out=o_t[:, ec, :], in0=Oev[0][:, ec, :], in1=dev[0], op=ALU.mult)
                    e.tensor_tensor(out=u_t, in0=Oev[1][:, ec, :], in1=dev[1], op=ALU.mult)
                    return e.tensor_tensor(out=o_t[:, ec, :], in0=o_t[:, ec, :], in1=u_t, op=ALU.add)
                p.op(DVE, f_comb, reads=((T, "Oev", 0, ec), (T, "Oev", 1, ec), (T, "dev", 0), (T, "dev", 1)), writes=((T, "o", ec),))
                p.op(ACT, (lambda e, ec=ec: e.activation(out=sq_t[ec], in_=o_t[:, ec, :], func=AF.Square)),
                     reads=((T, "o", ec),), writes=((T, "sq", ec),))
                p.op(PE, (lambda e, ec=ec: e.matmul(ssq_ps, lhsT=cx.ones_f, rhs=sq_t[ec], start=(ec == 0), stop=(ec == 1))),
                     reads=((T, "sq", ec), "ones_f"), writes=((T, "ssq"),))
            p.op(ACT, lambda e: e.activation(out=rstd, in_=ssq_ps, func=AF.Sqrt, scale=1.0 / 256.0, bias=cx.eps_t),
                 reads=((T, "ssq"), "eps_t"), writes=((T, "rstd"),))
            p.op(DVE, lambda e: e.reciprocal(out=rstd, in_=rstd), reads=((T, "rstd"),), writes=((T, "rstd"),))
            for ec in range(2):
                p.op(DVE, (lambda e, ec=ec: e.scalar_tensor_tensor(out=on_t[ec], in0=o_t[:, ec, :], scalar=gsub_t[:, ec:ec + 1], in1=rstd,
                                                                   op0=ALU.mult, op1=ALU.mult)),
                     reads=((T, "o", ec), (T, "rstd"), (T, "gsub")), writes=((T, "on", ec),))
                r0 = hd * 256 + ec * 128
                p.op(POOL, (lambda e, ec=ec, r0=r0, qs=qs: e.dma_start(out=oT[r0:r0 + 128, qs], in_=on_t[ec])),
                     reads=((T, "on", ec),), writes=((T, "oT", hd, J, ec),), dma=("on", ec))


def stage_attn_b(cx, qT, Kg, Vg, m01_d, negm_d, cmat_d, oT, tag, cc=False):
    p, sb = cx.p, cx.sb
    p.barrier()
    sb.reset(cx.sb_base)
    T = tag
    ps = cx.psum
    z_ps = [ps[0], ps[1], ps[2], ps[3]]
    ob_ps = [ps[4], ps[5]]
    r_ps = ps[6]
    tr_ps = ps[7]
    m01_t = sb.alloc(16 * TT, BF16).rearrange("p (b t) -> p b t", b=16)
    negm_t = sb.alloc(16 * TT, BF16).rearrange("p (b t) -> p b t", b=16)
    cm_t = sb.alloc(3 * 128, BF16).rearrange("p (a b) -> p a b", a=3)
    identf = sb.alloc(128, F32)
    Kt = [sb.alloc(16 * TT, BF16).rearrange("p (g t) -> p g t", g=16) for _ in range(2)]
    Vt = [sb.alloc(64 * 128, BF16).rearrange("p (b e) -> p b e", b=64) for _ in range(2)]
    Qt = [sb.alloc(TL, BF16) for _ in range(2)]
    e_t = [sb.alloc(TT, F32) for _ in range(3)]
    sp_t = [sb.alloc(TT, BF16) for _ in range(5)]
    w_t = [sb.alloc(TT, BF16) for _ in range(3)]
    E_t = [sb.alloc(4, F32) for _ in range(3)]
    O_t = [sb.alloc(4 * 128, F32).rearrange("p (a b) -> p a b", a=4) for _ in range(2)]
    oo_t = [sb.alloc(TT, BF16) for _ in range(2)]
    p.op(SP, lambda e: e.dma_start(out=m01_t, in_=m01_d.rearrange("p (b t) -> p b t", b=16)), writes=((T, "m01"),), dma=("c", 0))
    p.op(SP, lambda e: e.dma_start(out=negm_t, in_=negm_d.rearrange("p (b t) -> p b t", b=16)), writes=((T, "negm"),), dma=("c", 1))
    p.op(SP, lambda e: e.dma_start(out=cm_t, in_=cmat_d.rearrange("p (a b) -> p a b", a=3)), writes=((T, "cm"),), dma=("c", 2))
    p.op(DVE, lambda e: e.tensor_copy(out=identf, in_=cm_t[:, 1, :]), reads=((T, "cm"),), writes=((T, "identf"),))
    negU = cm_t[:, 0, :]
    ident = cm_t[:, 1, :]
    negones = cm_t[:, 2, 0:1]
    zc = 0
    ec_ = 0
    spc = 0
    wc = 0
    Ec = 0
    obc = 0
    hj = 0
    for h in range(NH):
        kb_ = h % 2
        if cc:
            for Jl in range(NT):
                p.op(SP, (lambda e, kb_=kb_, h=h, Jl=Jl: e.dma_start(out=Kt[kb_][:, 4 * Jl:4 * Jl + 4, :],
                                                                    in_=Kg[Jl, h // 4, :, (h % 4) * 128:(h % 4 + 1) * 128, :].rearrange("r d t -> d r t"))),
                     reads=(), writes=((T, "K", kb_, Jl),), dma=("k", kb_))
        else:
            p.op(SP, (lambda e, kb_=kb_, h=h: e.dma_start(out=Kt[kb_], in_=Kg[:, h].rearrange("g d t -> d g t"))),
                 reads=((T, "Kg"),), writes=((T, "K", kb_),), dma=("k", kb_))
        p.op(SP, (lambda e, kb_=kb_, h=h: e.dma_start(out=Qt[kb_], in_=qT[h])),
             reads=((T, "qT"),), writes=((T, "Q", kb_),), dma=("q", kb_))
        if cc:
            for Jl in range(NT):
                p.op(SP, (lambda e, kb_=kb_, h=h, Jl=Jl: e.dma_start(out=Vt[kb_][:, 16 * Jl:16 * Jl + 16, :],
                                                                    in_=Vg[Jl, h // 4, :, :, (h % 4) * 128:(h % 4 + 1) * 128].rearrange("r (tb p) e -> p (r tb) e", p=128))),
                     reads=(), writes=((T, "V", kb_, Jl),), dma=("v", kb_))
        else:
            p.op(SP, (lambda e, kb_=kb_, h=h: e.dma_start(out=Vt[kb_], in_=Vg[:, :, h * 128:(h + 1) * 128].rearrange("g (tb p) e -> p (g tb) e", p=128))),
                 reads=((T, "Vg"),), writes=((T, "V", kb_),), dma=("v", kb_))
        for J in range(NT):
            qs = slice(J * TT, (J + 1) * TT)
            ob_ = hj % 2
            hj += 1
            blocks = []
            for Jp in range(J + 1):
                for rp in range(4):
                    for kbi in range(4):
                        blocks.append((4 * Jp + rp, kbi, (rp * 4 + kbi) if Jp == J else None))
            blocks = blocks[::-1]
            nb = len(blocks)
            st8 = {}
            st9 = {}
            st7 = {}

            def stage1(bi, g, kbi, mi):
                nonlocal zc, ec_, spc
                zb = zc % 4
                zc += 1
                p.op(PE, (lambda e, zb=zb, kb_=kb_, g=g, kbi=kbi, qs=qs: e.matmul(
                    z_ps[zb], lhsT=Kt[kb_][:, g, kbi * 128:(kbi + 1) * 128], rhs=Qt[kb_][:, qs], start=True, stop=True)),
                     reads=(((T, "K", kb_, g // 4) if cc else (T, "K", kb_)), (T, "Q", kb_)), writes=((T, "z", zb),))
                eb = ec_ % 3
                ec_ += 1
                p.op(ACT, (lambda e, zb=zb, eb=eb: e.activation(out=e_t[eb], in_=z_ps[zb], func=AF.Exp)),
                     reads=((T, "z", zb),), writes=((T, "e", eb),))
                sb_ = spc % 5
                spc += 1
                p.op(ACT, (lambda e, eb=eb, sb_=sb_: e.activation(out=sp_t[sb_], in_=e_t[eb], func=AF.Ln, bias=cx.ones_f[:, 0:1])),
                     reads=((T, "e", eb),), writes=((T, "sp", sb_),))
                if mi is not None:
                    p.op(DVE, (lambda e, sb_=sb_, mi=mi: e.tensor_tensor(out=sp_t[sb_], in0=sp_t[sb_], in1=m01_t[:, mi, :], op=ALU.mult)),
                         reads=((T, "sp", sb_), (T, "m01")), writes=((T, "sp", sb_),))
                st8[bi] = (zb, eb, sb_)

            def stage2(bi, g, kbi, mi):
                zb, eb, sb_ = st8.pop(bi)
                last_is_mask = mi is not None
                p.op(PE, (lambda e, zb=zb, sb_=sb_, lm=last_is_mask: e.matmul(z_ps[zb], lhsT=negU, rhs=sp_t[sb_], start=False, stop=(not lm))),
                     reads=((T, "sp", sb_), (T, "cm"), (T, "e", eb)), writes=((T, "z", zb),))
                if mi is not None:
                    p.op(PE, (lambda e, zb=zb, mi=mi: e.matmul(z_ps[zb], lhsT=ident, rhs=negm_t[:, mi, :], start=False, stop=True)),
                         reads=((T, "negm"), (T, "cm")), writes=((T, "z", zb),))
                st7[bi] = (zb, sb_)

            def stage2c(bi, g, kbi, mi):
                nonlocal wc, Ec
                zb, sb_ = st7.pop(bi)
                wb = wc % 3
                wc += 1
                p.op(ACT, (lambda e, zb=zb, wb=wb: e.activation(out=w_t[wb], in_=z_ps[zb], func=AF.Exp)),
                     reads=((T, "z", zb),), writes=((T, "w", wb),))
                Eb = Ec % 3
                if bi > 0:
                    Ec += 1
                    p.op(ACT, (lambda e, Eb=Eb: e.activation(out=E_t[Eb], in_=r_ps[:, 0:4], func=AF.Exp)),
                         reads=((T, "R"),), writes=((T, "E", Eb),))
                st9[bi] = (sb_, wb, Eb)

            def stage2b(bi, g, kbi, mi):
                nonlocal obc
                sb_, wb, Eb = st9.pop(bi)

                def f_r(e, sb_=sb_, bi=bi, nb=nb):
                    ins = None
                    for ts_ in range(4):
                        ins = e.matmul(r_ps[:, ts_:ts_ + 1], lhsT=sp_t[sb_][:, ts_ * 128:(ts_ + 1) * 128], rhs=negones,
                                       start=(bi == 0 and ts_ == 0), stop=(bi == nb - 1), skip_group_check=True)
                    return ins
                p.op(PE, f_r, reads=((T, "sp", sb_), (T, "cm")), writes=((T, "R"),))
                ob2 = obc % 2
                obc += 1
                vblk = g * 4 + kbi

                def f_pv(e, wb=wb, ob2=ob2, vblk=vblk, kb_=kb_):
                    ins = None
                    for ts_ in range(4):
                        ins = e.matmul(ob_ps[ob2][:, ts_ * 128:(ts_ + 1) * 128], lhsT=w_t[wb][:, ts_ * 128:(ts_ + 1) * 128],
                                       rhs=Vt[kb_][:, vblk, :], start=True, stop=True)
                    return ins
                p.op(PE, f_pv, reads=((T, "w", wb), ((T, "V", kb_, g // 4) if cc else (T, "V", kb_))), writes=((T, "ob", ob2),))
                if bi == 0:
                    p.op(DVE, (lambda e, ob2=ob2, ob_=ob_: e.tensor_copy(out=O_t[ob_], in_=ob_ps[ob2].rearrange("p (a b) -> p a b", a=4))),
                         reads=((T, "ob", ob2),), writes=((T, "O", ob_),))
                else:
                    def f_acc(e, ob2=ob2, ob_=ob_, Eb=Eb):
                        ins = None
                        for ts_ in range(4):
                            ins = e.scalar_tensor_tensor(out=O_t[ob_][:, ts_, :], in0=ob_ps[ob2][:, ts_ * 128:(ts_ + 1) * 128],
                                                         scalar=E_t[Eb][:, ts_:ts_ + 1], in1=O_t[ob_][:, ts_, :], op0=ALU.mult, op1=ALU.add)
                        return ins
                    p.op(DVE, f_acc, reads=((T, "ob", ob2), (T, "E", Eb)), writes=((T, "O", ob_),))

            for step in range(nb + 3):
                if step < nb:
                    stage1(step, *blocks[step])
                if 2 <= step < nb + 2:
                    stage2(step - 2, *blocks[step - 2])
                if step >= 3:
                    stage2b(step - 3, *blocks[step - 3])
                if 2 <= step < nb + 2:
                    stage2c(step - 2, *blocks[step - 2])
            oo = hj % 2
            for ts_ in range(4):
                p.op(PE, (lambda e, ts_=ts_, ob_=ob_: e.transpose(tr_ps[:, ts_ * 128:(ts_ + 1) * 128], O_t[ob_][:, ts_, :], identf)),
                     reads=((T, "O", ob_), (T, "identf")), writes=((T, "tr", ts_),))
            p.op(ACT, (lambda e, oo=oo: e.activation(out=oo_t[oo], in_=tr_ps, func=AF.Copy)),
                 reads=tuple((T, "tr", i) for i in range(4)), writes=((T, "oo", oo),))
            p.op(POOL, (lambda e, oo=oo, h=h, qs=qs: e.dma_start(out=oT[h * 128:(h + 1) * 128, qs], in_=oo_t[oo])),
                 reads=((T, "oo", oo),), writes=((T, "oT", h, J),), dma=("on", oo))


def alloc_common(cx, es):
    nc = cx.nc
    cap = 200 * 1024
    big = es.enter_context(nc.sbuf_tensor("big", [128, cap], U8))
    cx.sb = Sbuf(big, cap)
    cx.psum = [es.enter_context(nc.psum_tensor("ps%d" % i, [128, 512], F32)) for i in range(8)]
    cx.psum = [t[:] for t in cx.psum]
    cx.ones_f = cx.sb.alloc(128, F32)
    cx.ones_b = cx.sb.alloc(128, BF16)
    cx.ident_b = cx.sb.alloc(128, BF16)
    cx.eps_t = cx.sb.alloc(1, F32)
    cx.sb_base = cx.sb.off
    p = cx.p
    p.op(DVE, lambda e: e.memset(cx.ones_f, 1.0), writes=("ones_f",))
    p.op(DVE, lambda e: e.memset(cx.ones_b, 1.0), writes=("ones_b",))
    p.op(DVE, lambda e: e.memset(cx.eps_t, EPS), writes=("eps_t",))


def _new_cx():
    nc = bass.Bass("TRN2", target_bir_lowering=False)
    cx = Ctx()
    cx.nc = nc
    cx.p = Prog(nc)
    return nc, cx


def _ein(nc, name, shape, dt=F32):
    return nc.dram_tensor(name, list(shape), dt, kind="ExternalInput").ap()


def _eout(nc, name, shape, dt=F32):
    return nc.dram_tensor(name, list(shape), dt, kind="ExternalOutput").ap()


def ffn_blocks():
    blocks = []
    for fc in range(FC):
        blocks += [fc * 128, DFF + fc * 128]
    return blocks


def build_ffn_prog():
    nc, cx = _new_cx()
    x = _ein(nc, "x", [D, TL])
    g = _ein(nc, "g", [128, DC])
    wi = _ein(nc, "wi", [D, 2 * DFF])
    wo = _ein(nc, "wo", [DFF, D])
    y = _eout(nc, "y", [D, TL])
    wi_scr = mk_weight_scratch(cx, "wi_b", D, 128, 2 * FC)
    wo_scr = mk_weight_scratch(cx, "wo_b", DFF, 128, DC)
    with contextlib.ExitStack() as es:
        alloc_common(cx, es)
        emit_wcvt(cx, wi, wi_scr, ffn_blocks(), 128, "fwi")
        emit_wcvt(cx, wo, wo_scr, [i * 128 for i in range(DC)], 128, "fwo")
        cx.p.barrier()
        stage_ffn2(cx, x, y, g, wi_scr, wo_scr, "f", wi_tag="fwi", wo_tag="fwo")
        cx.p.emit()
    return nc


def qkv_specs(cx, tag, qT, kT, gains, nq, nk, q_block0, k_block0, scale_q):
    specs = []
    for oc in range(nq):
        specs.append((q_block0 + oc, (lambda ti, oc=oc: qT[oc][:, ti * TT:(ti + 1) * TT]),
                      gains[:, 0:1] if gains is not None else None, scale_q))
    for oc in range(nk):
        specs.append((k_block0 + oc, (lambda ti, oc=oc: kT[ti][oc]),
                      gains[:, 1:2] if gains is not None else None, 1.0))
    return specs


def build_qkv_prog(kind):
    nc, cx = _new_cx()
    x = _ein(nc, "x", [D, TL])
    g = _ein(nc, "g", [128, DC])
    ncol = {"A": 3 * D, "KV": 2 * D, "Q": D}[kind]
    w = _ein(nc, "w", [D, ncol])
    scale = HD ** -0.5
    qT = kT = v = None
    if kind in ("A", "Q"):
        qT = _eout(nc, "qT", [NH, 128, TL], BF16)
    if kind in ("A", "KV"):
        kT = _eout(nc, "kT", [NT, NH, 128, TT], BF16)
        v = _eout(nc, "v", [NT, TT, D], BF16)
    nfm = {"A": 32, "KV": 16, "Q": 16}[kind]
    fm_scr = mk_weight_scratch(cx, "wfm_b", D, 128, nfm)
    v_scr = mk_weight_scratch(cx, "wv_b", D, 512, 4) if kind != "Q" else None
    with contextlib.ExitStack() as es:
        alloc_common(cx, es)
        T = "p"
        emit_wcvt(cx, w, fm_scr, [i * 128 for i in range(nfm)], 128, T + "wfm")
        if v_scr is not None:
            emit_wcvt(cx, w, v_scr, [nfm * 128 + i * 512 for i in range(4)], 512, T + "wv")
        gains = None
        if kind == "A":
            gqk = _ein(nc, "gqk", [128, 2])
            gains = cx.sb.alloc(2, F32)
            cx.sb_base = cx.sb.off
            cx.p.op(SP, lambda e: e.dma_start(out=gains, in_=gqk), writes=((T, "gains0"),), dma=("c", 0))
            cx.p.op(DVE, lambda e: e.tensor_scalar(out=gains[:, 0:1], in0=gains[:, 0:1], scalar1=float(scale), scalar2=None, op0=ALU.mult),
                    reads=((T, "gains0"),), writes=((T, "gains"),))
        if kind == "A":
            specs = qkv_specs(cx, T, qT, kT, gains, 16, 16, 0, 16, 1.0)
        elif kind == "KV":
            specs = qkv_specs(cx, T, None, kT, None, 0, 16, 0, 0, 1.0)
        else:
            specs = qkv_specs(cx, T, qT, None, None, 16, 0, 0, 0, scale)
        cx.p.barrier()
        stage_qkv(cx, x, g, T, fm_scr, specs, v_scr, v, nfm_blocks=nfm)
        cx.p.emit()
    return nc


def build_wo_prog():
    nc, cx = _new_cx()
    x = _ein(nc, "x", [D, TL])
    oT = _ein(nc, "oT", [D, TL], BF16)
    w = _ein(nc, "w", [D, D])
    y = _eout(nc, "y", [D, TL])
    scr = mk_weight_scratch(cx, "wo_b", D, 128, DC)
    with contextlib.ExitStack() as es:
        alloc_common(cx, es)
        emit_wcvt(cx, w, scr, [i * 128 for i in range(DC)], 128, "ow")
        cx.p.barrier()
        stage_wo(cx, x, y, oT, scr, "o", w_tag="ow")
        cx.p.emit()
    return nc


def build_attn_a_prog(lambda_init, debug=False):
    nc, cx = _new_cx()
    dbg = _eout(nc, "dbg", [6, 128, TT]) if debug else None
    qT = _ein(nc, "qT", [NH, 128, TL], BF16)
    Kg = _ein(nc, "Kg", [16, NH, 128, TT], BF16)
    Vg = _ein(nc, "Vg", [16, TT, D], BF16)
    biasM = _ein(nc, "biasM", [NH, 128, 1024])
    sel = _ein(nc, "sel", [128, 17 * 4])
    b31 = _ein(nc, "b31", [128, NH])
    lam = _ein(nc, "lam", [128, 4 * 128])
    gsub = _ein(nc, "gsub", [128, 2])
    oT = _eout(nc, "oT", [D, TL], BF16)
    with contextlib.ExitStack() as es:
        alloc_common(cx, es)
        stage_attn_a(cx, qT, Kg, Vg, biasM, sel, b31, lam, gsub, oT, lambda_init, "a", dbg=dbg)
        cx.p.emit()
    return nc


def build_attn_b_prog():
    nc, cx = _new_cx()
    qT = _ein(nc, "qT", [NH, 128, TL], BF16)
    Kg = _ein(nc, "Kg", [16, NH, 128, TT], BF16)
    Vg = _ein(nc, "Vg", [16, TT, D], BF16)
    m01 = _ein(nc, "m01", [128, 16 * TT], BF16)
    negm = _ein(nc, "negm", [128, 16 * TT], BF16)
    cmat = _ein(nc, "cmat", [128, 3 * 128], BF16)
    oT = _eout(nc, "oT", [D, TL], BF16)
    with contextlib.ExitStack() as es:
        alloc_common(cx, es)
        stage_attn_b(cx, qT, Kg, Vg, m01, negm, cmat, oT, "b")
        cx.p.emit()
    return nc


def build_fused_prog():
    nc, cx = _new_cx()
    p = cx.p
    x = _ein(nc, "x", [D, TL])
    y = _eout(nc, "y", [D, TL])
    W = {}
    for nm, shp in (("ffn_pre_wi", [DEPTH, D, 2 * DFF]), ("ffn_pre_wo", [DEPTH, DFF, D]),
                    ("ffn_post_wi", [DEPTH, D, 2 * DFF]), ("ffn_post_wo", [DEPTH, DFF, D]),
                    ("a_wqkv", [NA, D, 3 * D]), ("a_wo", [NA, D, D]), ("b_wkv", [D, 2 * D]),
                    ("b_wq", [DEPTH - NA, D, D]), ("b_wo", [DEPTH - NA, D, D]),
                    ("ffn_pre_norm", [DEPTH, 128, DC]), ("mix_norm", [DEPTH, 128, DC]), ("ffn_post_norm", [DEPTH, 128, DC]),
                    ("kv_norm", [128, DC]), ("gqk", [NA, 128, 2]), ("lam", [NA, 128, 512]), ("gsub", [NA, 128, 2]),
                    ("biasM", [NH, 128, 1024]), ("sel", [128, 68]), ("b31", [128, NH])):
        W[nm] = _ein(nc, nm, shp)
    for nm, shp in (("m01", [128, 16 * TT]), ("negm", [128, 16 * TT]), ("cmat", [128, 3 * 128])):
        W[nm] = _ein(nc, nm, shp, BF16)
    xs = [nc.dram_tensor("xa", [D, TL], F32).ap(), nc.dram_tensor("xb", [D, TL], F32).ap()]
    qT = nc.dram_tensor("qT", [NH, 128, TL], BF16).ap()
    oT = nc.dram_tensor("oT", [D, TL], BF16).ap()
    kT_loc = [nc.dram_tensor("kT%d" % i, [NT, NH, 128, TT], BF16).ap() for i in range(3)]
    v_loc = [nc.dram_tensor("vl%d" % i, [NT, 4, TT, 512], BF16).ap() for i in range(3)]
    Kg = [nc.dram_tensor("Kg%d" % i, [NT, 4, 4, 512, TT], BF16).ap() for i in range(3)]
    Vg = [nc.dram_tensor("Vg%d" % i, [NT, 4, 4, TT, 512], BF16).ap() for i in range(3)]
    wi_scr = [mk_weight_scratch(cx, "wi_b%d" % i, D, 128, 2 * FC) for i in range(2)]
    wo_scr = [mk_weight_scratch(cx, "wo_b%d" % i, DFF, 128, DC) for i in range(2)]
    fm_scr = mk_weight_scratch(cx, "fm_b", D, 128, 32)
    vw_scr = mk_weight_scratch(cx, "vw_b", D, 512, 4)
    ow_scr = mk_weight_scratch(cx, "ow_b", D, 128, DC)
    groups = [[0, 1, 2, 3], [4, 5, 6, 7]]
    scale = HD ** -0.5

    stages = []
    cur = {"x": x, "i": 0, "nffn": 0, "kv": 0}

    def nxt():
        o = xs[cur["i"] % 2]
        cur["i"] += 1
        return o

    def add_ffn(wi, wo, g, name):
        k = cur["nffn"] % 2
        cur["nffn"] += 1

        def cv(par, bg, wi=wi, wo=wo, k=k):
            emit_wcvt(cx, wi, wi_scr[k], ffn_blocks(), 128, "fwi%d" % k, par, bg)
            emit_wcvt(cx, wo, wo_scr[k], [i * 128 for i in range(DC)], 128, "fwo%d" % k, par, bg)

        def st(k=k, g=g, name=name):
            xin = cur["x"]
            xo = y if name == "last" else nxt()
            stage_ffn2(cx, xin, xo, g, wi_scr[k], wo_scr[k], name, wi_tag="fwi%d" % k, wo_tag="fwo%d" % k)
            cur["x"] = xo
        stages.append((cv, st))

    def gather_hook(e_idx, T):
        def hook(ti):
            for hg in range(4):
                rk = tuple((T, "fmout", si, ti) for si in hook.kspecs[hg * 4:hg * 4 + 4])
                p.op(POOL, (lambda e, ti=ti, hg=hg: e.collective_compute(
                    "AllGather", ALU.bypass, groups, [kT_loc[e_idx][ti, hg * 4:hg * 4 + 4].rearrange("h d t -> (h d) t")],
                    [Kg[e_idx][ti, hg].rearrange("r d t -> (r d) t")])),
                     reads=rk, writes=((T, "Kgath", ti, hg),), dma=("cc",), inc=1)
            for eg in range(4):
                rv = tuple((T, "vout", ti, tb, eg) for tb in range(4))
                p.op(POOL, (lambda e, ti=ti, eg=eg: e.collective_compute(
                    "AllGather", ALU.bypass, groups, [v_loc[e_idx][ti, eg]],
                    [Vg[e_idx][ti, eg].rearrange("r t e -> (r t) e")])),
                     reads=rv, writes=((T, "Vgath", ti, eg),), dma=("cc",), inc=1)
        return hook

    def add_qkv(kind, w, g, name, l=0, e_idx=0):
        nfm = {"A": 32, "KV": 16, "Q": 16}[kind]

        def cv(par, bg, w=w, nfm=nfm, kind=kind):
            emit_wcvt(cx, w, fm_scr, [i * 128 for i in range(nfm)], 128, "fm", par, bg)
            if kind != "Q":
                emit_wcvt(cx, w, vw_scr, [nfm * 128 + i * 512 for i in range(4)], 512, "vw", par, bg)

        def st(kind=kind, g=g, name=name, l=l, e_idx=e_idx, nfm=nfm):
            T = name
            gains = None
            if kind == "A":
                cx.sb.reset(cx.sb_base0)
                gains = cx.sb.alloc(2, F32)
                cx.sb_base = cx.sb.off
                p.barrier()
                p.op(SP, lambda e: e.dma_start(out=gains, in_=W["gqk"][l]), writes=((T, "gains0"),), dma=("c", 0))
                p.op(DVE, lambda e: e.tensor_scalar(out=gains[:, 0:1], in0=gains[:, 0:1], scalar1=float(scale), scalar2=None, op0=ALU.mult),
                     reads=((T, "gains0"),), writes=((T, "gains"),))
                specs = qkv_specs(cx, T, qT, kT_loc[e_idx], gains, 16, 16, 0, 16, 1.0)
                kspecs = list(range(16, 32))
            elif kind == "KV":
                specs = qkv_specs(cx, T, None, kT_loc[e_idx], None, 0, 16, 0, 0, 1.0)
                kspecs = list(range(0, 16))
            else:
                specs = qkv_specs(cx, T, qT, None, None, 16, 0, 0, 0, scale)
                kspecs = None
            hook = None
            if kind != "Q":
                hook = gather_hook(e_idx, T)
                hook.kspecs = kspecs
            stage_qkv(cx, cur["x"], g, T, fm_scr, specs, vw_scr if kind != "Q" else None,
                      v_loc[e_idx] if kind != "Q" else None, fm_tag="fm", v_tag="vw", v_blocked=True,
                      post_tile=hook, nfm_blocks=nfm)
            cx.sb_base = cx.sb_base0
        stages.append((cv, st))

    def add_wo(w, name):
        def cv(par, bg, w=w):
            emit_wcvt(cx, w, ow_scr, [i * 128 for i in range(DC)], 128, "ow", par, bg)

        def st(name=name):
            xin = cur["x"]
            xo = nxt()
            stage_wo(cx, xin, xo, oT, ow_scr, name, w_tag="ow")
            cur["x"] = xo
        stages.append((cv, st))

    def add_attn_a(l, e_idx):
        lambda_init = 0.8 - 0.6 * math.exp(-0.3 * l)

        def st(l=l, e_idx=e_idx, lambda_init=lambda_init):
            stage_attn_a(cx, qT, Kg[e_idx], Vg[e_idx], W["biasM"], W["sel"], W["b31"], W["lam"][l], W["gsub"][l], oT,
                         lambda_init, "aa%d" % l, cc=True)
        stages.append((None, st))

    def add_attn_b(i):
        def st(i=i):
            stage_attn_b(cx, qT, Kg[2], Vg[2], W["m01"], W["negm"], W["cmat"], oT, "ab%d" % i, cc=True)
        stages.append((None, st))

    for l in range(DEPTH):
        if l == NA:
            add_qkv("KV", W["b_wkv"], W["kv_norm"], "kvb", e_idx=2)
        add_ffn(W["ffn_pre_wi"][l], W["ffn_pre_wo"][l], W["ffn_pre_norm"][l], "fpre%d" % l)
        if l < NA:
            add_qkv("A", W["a_wqkv"][l], W["mix_norm"][l], "qkva%d" % l, l=l, e_idx=l)
            add_attn_a(l, l)
            add_wo(W["a_wo"][l], "woa%d" % l)
        else:
            i = l - NA
            add_qkv("Q", W["b_wq"][i], W["mix_norm"][l], "qb%d" % i)
            add_attn_b(i)
            add_wo(W["b_wo"][i], "wob%d" % i)
        add_ffn(W["ffn_post_wi"][l], W["ffn_post_wo"][l], W["ffn_post_norm"][l], "last" if l == DEPTH - 1 else "fpost%d" % l)

    with contextlib.ExitStack() as es:
        alloc_common(cx, es)
        cx.sb_base0 = cx.sb_base
        done = set()

        def do_cv(si, bg):
            if si < len(stages) and stages[si][0] is not None and si not in done:
                stages[si][0](len(done) % 2, bg)
                done.add(si)
        do_cv(0, False)
        for si, (cv, st) in enumerate(stages):
            assert cv is None or si in done
            p.barrier()
            if si + 1 < len(stages):
                if stages[si + 1][0] is not None:
                    do_cv(si + 1, True)
                elif si + 2 < len(stages):
                    do_cv(si + 2, True)
            st()
        p.emit()
    return nc


def _tok_idx(r):
    return np.concatenate([np.arange((4 * J + r) * TT, (4 * J + r + 1) * TT) for J in range(NT)])


def _col(vec):
    return np.ascontiguousarray(np.asarray(vec, np.float32).reshape(-1, 128).T)


def _t5_bucket_np(n):
    n = np.maximum(n, 0)
    nf = np.maximum(n, 16).astype(np.float32)
    large = 16 + (np.log(nf / np.float32(16)) / np.float32(math.log(128 / 16)) * np.float32(16)).astype(np.int32)
    large = np.minimum(large, 31)
    return np.where(n < 16, n, large)


def _bias_master(rel_bias):
    s = np.arange(128)[:, None]
    u = np.arange(1024)[None, :]
    n = u - 384 - s
    bk = _t5_bucket_np(n)
    M = np.empty((NH, 128, 1024), np.float32)
    for h in range(NH):
        M[h] = np.where(n >= 0, rel_bias[bk, h], np.float32(NEG))
    return M


def _sel_table(r):
    sel = np.zeros((17, 4), np.float32)
    sel[0] = (0, 1, 0, 0) if r == 0 else (0, 0, 1, 0)
    for rp in range(4):
        for kbi in range(4):
            if rp == r:
                c = (1, 0, 0, 0)
            elif rp == r - 1 and kbi == 3:
                c = (0, 1, 0, 0)
            elif rp < r:
                c = (0, 0, 1, 0)
            else:
                c = (0, 0, 0, NEG)
            sel[1 + rp * 4 + kbi] = c
    return np.ascontiguousarray(np.broadcast_to(sel.reshape(1, -1), (128, 68)))


def _b_masks(r):
    s = np.arange(128)[:, None]
    t = np.arange(TT)[None, :]
    m01 = np.zeros((128, 16, TT), np.float32)
    for rp in range(4):
        for kbi in range(4):
            if rp < r:
                m = np.ones((128, TT), np.float32)
            elif rp > r:
                m = np.zeros((128, TT), np.float32)
            else:
                m = ((kbi * 128 + s) < t).astype(np.float32)
            m01[:, rp * 4 + kbi, :] = m
    negm = (1.0 - m01) * NEG
    bf = ml_dtypes.bfloat16
    return m01.reshape(128, -1).astype(bf), negm.reshape(128, -1).astype(bf)


def _cmat():
    j = np.arange(128)[:, None]
    s = np.arange(128)[None, :]
    negU = -(j >= s).astype(np.float32)
    ident = np.eye(128, dtype=np.float32)
    no = np.zeros((128, 128), np.float32)
    no[:, 0] = -1.0
    return np.concatenate([negU, ident, no], axis=1).astype(ml_dtypes.bfloat16)


_PROGS = {}


def _prog(key, fn, *a):
    if key not in _PROGS:
        _PROGS[key] = fn(*a)
    return _PROGS[key]


def _run(nc, in_maps):
    res = run_bass_kernel_spmd(nc, in_maps, core_ids=list(range(NCORES)))
    return res.results


def _gather_kv(kTs, vs):
    Kgs, Vgs = [], []
    for b in range(NB):
        Kg = np.empty((16, NH, 128, TT), kTs[0].dtype)
        Vg = np.empty((16, TT, D), vs[0].dtype)
        for rp in range(4):
            c = b * 4 + rp
            for J in range(NT):
                Kg[4 * J + rp] = kTs[c][J]
                Vg[4 * J + rp] = vs[c][J]
        Kgs.append(Kg)
        Vgs.append(Vg)
    return Kgs, Vgs


def _cols(mat):
    mat = np.asarray(mat, np.float32)
    return np.ascontiguousarray(mat.reshape(mat.shape[0], -1, 128).transpose(0, 2, 1))


def kernel(x, ffn_pre_norm, ffn_pre_wi, ffn_pre_wo, mix_norm, ffn_post_norm, ffn_post_wi,
           ffn_post_wo, rel_bias, a_wqkv, a_q_norm, a_k_norm, a_lambda, a_subln, a_wo,
           kv_norm, b_wkv, b_wq, b_wo):
    f32 = np.float32
    x = np.asarray(x, f32)
    rel_bias = np.asarray(rel_bias, f32)
    shared = {
        "ffn_pre_wi": np.asarray(ffn_pre_wi, f32), "ffn_pre_wo": np.asarray(ffn_pre_wo, f32),
        "ffn_post_wi": np.asarray(ffn_post_wi, f32), "ffn_post_wo": np.asarray(ffn_post_wo, f32),
        "a_wqkv": np.asarray(a_wqkv, f32), "a_wo": np.asarray(a_wo, f32), "b_wkv": np.asarray(b_wkv, f32),
        "b_wq": np.asarray(b_wq, f32), "b_wo": np.asarray(b_wo, f32),
        "ffn_pre_norm": _cols(ffn_pre_norm), "mix_norm": _cols(mix_norm), "ffn_post_norm": _cols(ffn_post_norm),
        "kv_norm": _col(kv_norm),
        "gqk": np.ascontiguousarray(np.stack([np.asarray(a_q_norm, f32), np.asarray(a_k_norm, f32)], axis=2)),
        "lam": np.ascontiguousarray(np.broadcast_to(np.asarray(a_lambda, f32).reshape(NA, 1, 512), (NA, 128, 512))),
        "gsub": _cols(a_subln),
        "biasM": _bias_master(rel_bias),
        "b31": np.ascontiguousarray(np.broadcast_to(rel_bias[31].reshape(1, NH), (128, NH))),
        "cmat": _cmat(),
    }
    in_maps = []
    for c in range(NCORES):
        b, r = divmod(c, 4)
        m = dict(shared)
        m["x"] = np.ascontiguousarray(x[b][_tok_idx(r)].T)
        m["sel"] = _sel_table(r)
        m["m01"], m["negm"] = _b_masks(r)
        in_maps.append(m)
    nc = _prog("fused", build_fused_prog)
    out = _run(nc, in_maps)
    y = np.empty((NB, SEQ, D), f32)
    for c in range(NCORES):
        b, r = divmod(c, 4)
        y[b][_tok_idx(r)] = out[c]["y"].T
    return y
```
